# Optimizing a Trainium2 kernel written in Bass

```python
import math
import jax
import jax.numpy as jnp
from jax import lax
import numpy as np

D_MODEL = 2048
BATCH = 4
SEQ = 4096
DEPTH = 4

GRID_W = 64
CTX_LEN = 256
EPS = 1e-6
NEG = -1e30
HEAD_DIM = 64
GROUP_W = D_MODEL // 4
MIX_W = 4 * GROUP_W

A_HEADS = GROUP_W // HEAD_DIM
A_KV_HEADS = max(1, A_HEADS // 4)
WINDOW = 128
BLOCK = 128
ROPE_BASE = 10000.0
HY_CH = GROUP_W
HY_ORDER = 2
HY_EMB = 33
HY_FFN = 64
HY_SHORT = 3
HY_DECAY_TARGET = 1e-2
HY_FAST_PCT = 0.3
HY_SLOW_PCT = 1.5
NA_HEADS = GROUP_W // HEAD_DIM
NA_KR = 8
NA_KC = 16
SSM_INNER = GROUP_W
SSM_HEAD_DIM = 64
SSM_HEADS = SSM_INNER // SSM_HEAD_DIM
SSM_STATE = 128
SSM_GROUPS = 2
SSM_CONV = 3
SSM_CHUNK = 128
SSM_XBC = SSM_INNER + 2 * SSM_GROUPS * SSM_STATE
D_FF = 4 * D_MODEL

A_COLS = (A_HEADS + 2 * A_KV_HEADS) * HEAD_DIM
HY_COLS = (HY_ORDER + 1) * HY_CH
NA_COLS = 3 * NA_HEADS * HEAD_DIM
SSM_COLS = SSM_INNER + SSM_XBC + 2 * SSM_HEADS
OFF_HY = A_COLS
OFF_NA = OFF_HY + HY_COLS
OFF_SSM = OFF_NA + NA_COLS
N_IN = OFF_SSM + SSM_COLS

kernel_name = "hybrid_parallel_heads_diffusion_trunk"

F32 = jnp.float32


def rms_norm(x, g):
    xf = x.astype(F32)
    y = xf * lax.rsqrt(jnp.mean(xf * xf, axis=-1, keepdims=True) + EPS)
    return (y * g.astype(F32)).astype(x.dtype)


def centred_conv(u, w, b):
    K = w.shape[0]
    L = u.shape[1]
    pad = K // 2
    up = jnp.pad(u, ((0, 0), (pad, pad), (0, 0)))
    out = up[:, 0:L] * w[0]
    for k in range(1, K):
        out = out + up[:, k:k + L] * w[k]
    return out + b


def axial_rope(x, row, col):
    half = x.shape[-1] // 2
    quarter = half // 2
    inv = ROPE_BASE ** (-jnp.arange(quarter, dtype=F32) / quarter)

    def rot(xp, pos):
        ang = pos.astype(F32)[:, None] * inv[None]
        cos = jnp.cos(ang)[None, :, None, :].astype(x.dtype)
        sin = jnp.sin(ang)[None, :, None, :].astype(x.dtype)
        x1, x2 = xp[..., :quarter], xp[..., quarter:]
        return jnp.concatenate([x1 * cos - x2 * sin, x2 * cos + x1 * sin], axis=-1)

    return jnp.concatenate([rot(x[..., :half], row), rot(x[..., half:], col)], axis=-1)


def split_heads(p, n_q, n_kv):
    Bn, L, _ = p.shape
    q = p[..., :n_q * HEAD_DIM].reshape(Bn, L, n_q, HEAD_DIM)
    k = p[..., n_q * HEAD_DIM:(n_q + n_kv) * HEAD_DIM].reshape(Bn, L, n_kv, HEAD_DIM)
    v = p[..., (n_q + n_kv) * HEAD_DIM:(n_q + 2 * n_kv) * HEAD_DIM].reshape(Bn, L, n_kv, HEAD_DIM)
    return q, k, v


def context_attention(qc, kc, vc, sink=None):
    Bn, Lc, Hq, Dh = qc.shape
    Hkv = kc.shape[2]
    G = Hq // Hkv
    qg = qc.reshape(Bn, Lc, Hkv, G, Dh)
    s = jnp.einsum('bqhgd,bkhd->bhgqk', qg, kc).astype(F32) * (Dh ** -0.5)
    if sink is not None:
        sk = jnp.broadcast_to(sink.astype(F32).reshape(1, Hkv, G, 1, 1), s.shape[:-1] + (1,))
        s = jnp.concatenate([s, sk], axis=-1)
    p = jax.nn.softmax(s, axis=-1)[..., :Lc].astype(vc.dtype)
    return jnp.einsum('bhgqk,bkhd->bqhgd', p, vc).reshape(Bn, Lc, Hq * Dh)


def window_attention(q, k, v, kc, vc, sink):
    Bn, S, Hq, Dh = q.shape
    Hkv = k.shape[2]
    G = Hq // Hkv
    nb = S // BLOCK
    Lc = kc.shape[1]
    scale = Dh ** -0.5
    qb = q.reshape(Bn, nb, BLOCK, Hkv, G, Dh)

    def band(t):
        tp = jnp.pad(t, ((0, 0), (BLOCK, BLOCK), (0, 0), (0, 0))).reshape(Bn, nb + 2, BLOCK, Hkv, Dh)
        return jnp.concatenate([tp[:, :-2], tp[:, 1:-1], tp[:, 2:]], axis=2)

    kb, vb = band(k), band(v)
    s_loc = jnp.einsum('bnqhgd,bnkhd->bnhgqk', qb, kb).astype(F32) * scale
    s_ctx = jnp.einsum('bnqhgd,bchd->bnhgqc', qb, kc).astype(F32) * scale
    qi = jnp.arange(BLOCK)[:, None]
    kj = jnp.arange(3 * BLOCK)[None, :]
    rel = kj - BLOCK - qi
    kpos = jnp.arange(nb)[:, None, None] * BLOCK + (kj - BLOCK)[None]
    valid = (jnp.abs(rel) <= WINDOW)[None] & (kpos >= 0) & (kpos < S)
    s_loc = jnp.where(valid[None, :, None, None], s_loc, NEG)
    sk = jnp.broadcast_to(sink.astype(F32).reshape(1, 1, Hkv, G, 1, 1), s_loc.shape[:-1] + (1,))
    p = jax.nn.softmax(jnp.concatenate([s_loc, s_ctx, sk], axis=-1), axis=-1)
    p_loc = p[..., :3 * BLOCK].astype(v.dtype)
    p_ctx = p[..., 3 * BLOCK:3 * BLOCK + Lc].astype(v.dtype)
    out = (jnp.einsum('bnhgqk,bnkhd->bnqhgd', p_loc, vb)
           + jnp.einsum('bnhgqc,bchd->bnqhgd', p_ctx, vc))
    return out.reshape(Bn, S, Hq * Dh)


def neighbourhood_attention(q, k, v, kc, vc, rpb):
    Bn, S, H, Dh = q.shape
    rows = S // GRID_W
    kr = min(NA_KR, rows)
    Lc = kc.shape[1]
    scale = Dh ** -0.5
    qg = q.reshape(Bn, rows, GRID_W, H, Dh)
    r = jnp.arange(rows)
    rstart = jnp.clip(r - kr // 2, 0, rows - kr)
    ridx = rstart[:, None] + jnp.arange(kr)[None]
    kg = jnp.take(k.reshape(Bn, rows, GRID_W, H, Dh), ridx, axis=1)
    vg = jnp.take(v.reshape(Bn, rows, GRID_W, H, Dh), ridx, axis=1)
    s_loc = jnp.einsum('brqhd,brawhd->brhqaw', qg, kg).astype(F32) * scale
    cq = jnp.arange(GRID_W)
    ck = jnp.arange(GRID_W)
    cstart = jnp.clip(cq - NA_KC // 2, 0, GRID_W - NA_KC)
    col_valid = (ck[None] >= cstart[:, None]) & (ck[None] < cstart[:, None] + NA_KC)
    roff = ridx - r[:, None] + NA_KR - 1
    coff = jnp.clip(ck[None] - cq[:, None], -(NA_KC - 1), NA_KC - 1) + NA_KC - 1
    bias = rpb[:, roff][:, :, :, coff]
    bias = bias.transpose(1, 0, 3, 2, 4).astype(F32)
    s_loc = jnp.where(col_valid[:, None, :], s_loc + bias[None], NEG)
    s_loc = s_loc.reshape(Bn, rows, H, GRID_W, kr * GRID_W)
    s_ctx = jnp.einsum('brqhd,bchd->brhqc', qg, kc).astype(F32) * scale
    p = jax.nn.softmax(jnp.concatenate([s_loc, s_ctx], axis=-1), axis=-1)
    p_loc = p[..., :kr * GRID_W].reshape(Bn, rows, H, GRID_W, kr, GRID_W).astype(v.dtype)
    p_ctx = p[..., kr * GRID_W:].astype(v.dtype)
    out = (jnp.einsum('brhqaw,brawhd->brqhd', p_loc, vg)
           + jnp.einsum('brhqc,bchd->brqhd', p_ctx, vc))
    return out.reshape(Bn, S, H * Dh)


def hyena_filters(L, w1, b1, w2, b2, w3, b3, w4, freq):
    t = jnp.linspace(0.0, 1.0, L, dtype=F32)[:, None]
    bands = (HY_EMB - 1) // 2
    f = jnp.linspace(1e-4, bands - 1, bands, dtype=F32)[None]
    wpos = 2.0 * math.pi * jnp.arange(L, dtype=F32)[:, None] / L
    z = jnp.concatenate([t, jnp.cos(f * wpos), -jnp.sin(f * wpos)], axis=-1)
    fr = freq.astype(F32)
    h = jnp.sin(fr * (z @ w1.astype(F32) + b1.astype(F32)))
    h = jnp.sin(fr * (h @ w2.astype(F32) + b2.astype(F32)))
    h = jnp.sin(fr * (h @ w3.astype(F32) + b3.astype(F32)))
    h = (h @ w4.astype(F32)).reshape(L, HY_ORDER, 2, HY_CH)
    max_decay = math.log(HY_DECAY_TARGET) / HY_FAST_PCT
    min_decay = math.log(HY_DECAY_TARGET) / HY_SLOW_PCT
    deltas = jnp.linspace(min_decay, max_decay, HY_CH, dtype=F32)
    decay = jnp.exp(-t * jnp.abs(deltas)[None])
    return h * decay[:, None, None, :]


def bidir_long_conv(u, h_fwd, h_bwd, skip):
    Bn, L, C = u.shape
    kern = jnp.concatenate([h_fwd, jnp.zeros((1, C), F32), h_bwd[1:][::-1]], axis=0)
    U = jnp.fft.rfft(u, n=2 * L, axis=1)
    Kf = jnp.fft.rfft(kern, n=2 * L, axis=0)
    y = jnp.fft.irfft(U * Kf[None], n=2 * L, axis=1)[:, :L]
    return y + u * skip.astype(F32)


def hyena_mixer(p, short_w, short_b, filt, skip):
    u = centred_conv(p, short_w, short_b).astype(F32)
    x1, x2, v = u[..., :HY_CH], u[..., HY_CH:2 * HY_CH], u[..., 2 * HY_CH:]
    z = x1 * bidir_long_conv(v, filt[:, 0, 0], filt[:, 0, 1], skip[0])
    z = x2 * bidir_long_conv(z, filt[:, 1, 0], filt[:, 1, 1], skip[1])
    return z.astype(p.dtype)


def segsum(a):
    T = a.shape[-1]
    cs = jnp.cumsum(a, axis=-1)
    diff = cs[..., :, None] - cs[..., None, :]
    mask = jnp.tril(jnp.ones((T, T), dtype=bool))
    return jnp.where(mask, diff, -jnp.inf)


def ssd_chunked(xs, dt, A, Bm, Cm, h0, return_y):
    b, l, nh, p = xs.shape
    g, n = Bm.shape[2], Bm.shape[3]
    nc = l // SSM_CHUNK
    Bh = jnp.repeat(Bm, nh // g, axis=2).reshape(b, nc, SSM_CHUNK, nh, n)
    Ch = jnp.repeat(Cm, nh // g, axis=2).reshape(b, nc, SSM_CHUNK, nh, n)
    X = (xs * dt[..., None]).reshape(b, nc, SSM_CHUNK, nh, p)
    Adt = (dt * A).reshape(b, nc, SSM_CHUNK, nh).transpose(0, 3, 1, 2)
    A_cs = jnp.cumsum(Adt, axis=-1)
    decay_states = jnp.exp(A_cs[..., -1:] - A_cs).transpose(0, 2, 3, 1)
    states = jnp.einsum('bclhn,bclhp->bchpn', Bh, X * decay_states[..., None])
    states = jnp.concatenate([h0[:, None], states], axis=1)
    chunk_decay = jnp.exp(segsum(jnp.pad(A_cs[..., -1], ((0, 0), (0, 0), (1, 0)))))
    new_states = jnp.einsum('bhzc,bchpn->bzhpn', chunk_decay, states)
    final = new_states[:, -1]
    if not return_y:
        return None, final
    prev = new_states[:, :-1]
    Lmat = jnp.exp(segsum(Adt))
    cb = jnp.einsum('bclhn,bcshn->bhcls', Ch, Bh) * Lmat
    y_diag = jnp.einsum('bhcls,bcshp->bclhp', cb, X)
    out_decay = jnp.exp(A_cs).transpose(0, 2, 3, 1)
    y_off = jnp.einsum('bclhn,bchpn->bclhp', Ch, prev) * out_decay[..., None]
    return (y_diag + y_off).reshape(b, l, nh, p), final


def ssm_inputs(p, conv_w, conv_b):
    Bn, L, _ = p.shape
    z = p[..., :SSM_INNER]
    xbc = jax.nn.silu(centred_conv(p[..., SSM_INNER:SSM_INNER + SSM_XBC], conv_w, conv_b)).astype(F32)
    dt_raw = p[..., SSM_INNER + SSM_XBC:].astype(F32).reshape(Bn, L, 2, SSM_HEADS)
    gn = SSM_GROUPS * SSM_STATE
    xs = xbc[..., :SSM_INNER].reshape(Bn, L, SSM_HEADS, SSM_HEAD_DIM)
    Bm = xbc[..., SSM_INNER:SSM_INNER + gn].reshape(Bn, L, SSM_GROUPS, SSM_STATE)
    Cm = xbc[..., SSM_INNER + gn:].reshape(Bn, L, SSM_GROUPS, SSM_STATE)
    return z, xs, Bm, Cm, dt_raw


def ssm_direction(xs, Bm, Cm, dt_raw_d, dt_bias_d, a_log_d, h0, reverse, return_y):
    dt = jax.nn.softplus(dt_raw_d + dt_bias_d.astype(F32))
    A = -jnp.exp(a_log_d.astype(F32))
    if reverse:
        xs, Bm, Cm, dt = (jnp.flip(t, axis=1) for t in (xs, Bm, Cm, dt))
    y, hT = ssd_chunked(xs, dt, A, Bm, Cm, h0, return_y)
    if reverse and return_y:
        y = jnp.flip(y, axis=1)
    return y, hT


def ssm_output(y, xs, z, d_skip, norm_w):
    Bn, L = z.shape[0], z.shape[1]
    y = y + xs * d_skip.astype(F32)[:, None]
    y = y.reshape(Bn, L, SSM_INNER) * jax.nn.silu(z.astype(F32))
    yg = y.reshape(Bn, L, SSM_GROUPS, SSM_INNER // SSM_GROUPS)
    yg = yg * lax.rsqrt(jnp.mean(yg * yg, axis=-1, keepdims=True) + EPS)
    return (yg.reshape(Bn, L, SSM_INNER) * norm_w.astype(F32)).astype(z.dtype)


def sq_relu_mlp(h, w1, w2):
    return jnp.square(jax.nn.relu(h @ w1)) @ w2


def setup_inputs(seed: int = 0) -> dict:
    key = jax.random.key(seed)
    ks = iter(jax.random.split(key, 40))

    def nrm(shape, scale):
        return jax.random.normal(next(ks), shape, F32) * scale

    x = nrm((BATCH, SEQ, D_MODEL), 1.0)
    c = nrm((BATCH, D_MODEL), 1.0)
    ctx = nrm((BATCH, CTX_LEN, D_MODEL), 1.0)
    c_ctx = nrm((D_MODEL,), 1.0)
    ada_w = nrm((DEPTH, D_MODEL, 6 * D_MODEL), D_MODEL ** -0.5)
    ada_b = nrm((DEPTH, 6 * D_MODEL), 0.02)
    norm_mix = 1.0 + nrm((DEPTH, D_MODEL), 0.02)
    norm_mlp = 1.0 + nrm((DEPTH, D_MODEL), 0.02)
    w_in = nrm((DEPTH, D_MODEL, N_IN), D_MODEL ** -0.5)
    w_out = nrm((DEPTH, MIX_W, D_MODEL), MIX_W ** -0.5)
    attn_sink = nrm((DEPTH, A_HEADS), 0.5)
    hy_short_w = nrm((DEPTH, HY_SHORT, HY_COLS), HY_SHORT ** -0.5)
    hy_short_b = nrm((DEPTH, HY_COLS), 0.02)
    hy_w1 = nrm((DEPTH, HY_EMB, HY_FFN), HY_EMB ** -0.5)
    hy_b1 = nrm((DEPTH, HY_FFN), 0.02)
    hy_w2 = nrm((DEPTH, HY_FFN, HY_FFN), HY_FFN ** -0.5)
    hy_b2 = nrm((DEPTH, HY_FFN), 0.02)
    hy_w3 = nrm((DEPTH, HY_FFN, HY_FFN), HY_FFN ** -0.5)
    hy_b3 = nrm((DEPTH, HY_FFN), 0.02)
    hy_w4 = nrm((DEPTH, HY_FFN, HY_ORDER * 2 * HY_CH), 0.02 * HY_FFN ** -0.5)
    hy_freq = 1.0 + nrm((DEPTH, HY_FFN), 0.02)
    hy_skip = 1.0 + nrm((DEPTH, HY_ORDER, HY_CH), 0.1)
    na_rpb = nrm((DEPTH, NA_HEADS, 2 * NA_KR - 1, 2 * NA_KC - 1), 0.02)
    ssm_conv_w = nrm((DEPTH, SSM_CONV, SSM_XBC), SSM_CONV ** -0.5)
    ssm_conv_b = nrm((DEPTH, SSM_XBC), 0.02)
    u = jax.random.uniform(next(ks), (DEPTH, 2, SSM_HEADS), F32)
    dt0 = jnp.exp(u * (math.log(0.1) - math.log(0.001)) + math.log(0.001))
    ssm_dt_bias = dt0 + jnp.log(-jnp.expm1(-dt0))
    ssm_a_log = jnp.log(jax.random.uniform(next(ks), (DEPTH, 2, SSM_HEADS), F32, 1.0, 16.0))
    ssm_d = 1.0 + nrm((DEPTH, SSM_HEADS), 0.02)
    ssm_norm = 1.0 + nrm((DEPTH, SSM_INNER), 0.02)
    mlp_w1 = nrm((DEPTH, D_MODEL, D_FF), D_MODEL ** -0.5)
    mlp_w2 = nrm((DEPTH, D_FF, D_MODEL), D_FF ** -0.5)
    final_norm = 1.0 + nrm((D_MODEL,), 0.02)
    return {"x": x, "c": c, "ctx": ctx, "c_ctx": c_ctx, "ada_w": ada_w, "ada_b": ada_b,
            "norm_mix": norm_mix, "norm_mlp": norm_mlp, "w_in": w_in, "w_out": w_out,
            "attn_sink": attn_sink, "hy_short_w": hy_short_w, "hy_short_b": hy_short_b,
            "hy_w1": hy_w1, "hy_b1": hy_b1, "hy_w2": hy_w2, "hy_b2": hy_b2, "hy_w3": hy_w3,
            "hy_b3": hy_b3, "hy_w4": hy_w4, "hy_freq": hy_freq, "hy_skip": hy_skip,
            "na_rpb": na_rpb, "ssm_conv_w": ssm_conv_w, "ssm_conv_b": ssm_conv_b,
            "ssm_dt_bias": ssm_dt_bias, "ssm_a_log": ssm_a_log, "ssm_d": ssm_d, "ssm_norm": ssm_norm,
            "mlp_w1": mlp_w1, "mlp_w2": mlp_w2, "final_norm": final_norm}


def reference(x, c, ctx, c_ctx, ada_w, ada_b, norm_mix, norm_mlp, w_in, w_out, attn_sink,
              hy_short_w, hy_short_b, hy_w1, hy_b1, hy_w2, hy_b2, hy_w3, hy_b3, hy_w4, hy_freq, hy_skip,
              na_rpb, ssm_conv_w, ssm_conv_b, ssm_dt_bias, ssm_a_log, ssm_d, ssm_norm,
              mlp_w1, mlp_w2, final_norm):
    Bn, S, _ = x.shape
    Lc = ctx.shape[1]
    t = jnp.arange(S)
    row, col = t // GRID_W, t % GRID_W
    xc = ctx
    for i in range(DEPTH):
        last = i == DEPTH - 1
        need_ctx = not last
        mod = (jax.nn.silu(c) @ ada_w[i] + ada_b[i])[:, None, :]
        sh_a, sc_a, g_a, sh_m, sc_m, g_m = jnp.split(mod, 6, axis=-1)
        modc = (jax.nn.silu(c_ctx) @ ada_w[i] + ada_b[i])[None, None, :]
        shc_a, scc_a, gc_a, shc_m, scc_m, gc_m = jnp.split(modc, 6, axis=-1)

        h = rms_norm(x, norm_mix[i]) * (1 + sc_a) + sh_a
        hc = rms_norm(xc, norm_mix[i]) * (1 + scc_a) + shc_a
        p = h @ w_in[i]
        pc = hc @ w_in[i]

        qa, ka, va = split_heads(p[..., :OFF_HY], A_HEADS, A_KV_HEADS)
        qac, kac, vac = split_heads(pc[..., :OFF_HY], A_HEADS, A_KV_HEADS)
        qa = axial_rope(qa, row, col)
        ka = axial_rope(ka, row, col)
        ya = window_attention(qa, ka, va, kac, vac, attn_sink[i])

        filt = hyena_filters(S, hy_w1[i], hy_b1[i], hy_w2[i], hy_b2[i], hy_w3[i], hy_b3[i], hy_w4[i], hy_freq[i])
        yb = hyena_mixer(p[..., OFF_HY:OFF_NA], hy_short_w[i], hy_short_b[i], filt, hy_skip[i])

        qn, kn, vn = split_heads(p[..., OFF_NA:OFF_SSM], NA_HEADS, NA_HEADS)
        qnc, knc, vnc = split_heads(pc[..., OFF_NA:OFF_SSM], NA_HEADS, NA_HEADS)
        yn = neighbourhood_attention(qn, kn, vn, knc, vnc, na_rpb[i])

        zc, xsc, Bc, Cc, dtc = ssm_inputs(pc[..., OFF_SSM:], ssm_conv_w[i], ssm_conv_b[i])
        h0 = jnp.zeros((Bn, SSM_HEADS, SSM_HEAD_DIM, SSM_STATE), F32)
        yc_f, hf = ssm_direction(xsc, Bc, Cc, dtc[:, :, 0], ssm_dt_bias[i, 0], ssm_a_log[i, 0], h0, False, need_ctx)
        yc_b, hb = ssm_direction(xsc, Bc, Cc, dtc[:, :, 1], ssm_dt_bias[i, 1], ssm_a_log[i, 1], h0, True, need_ctx)
        zl, xsl, Bl, Cl, dtl = ssm_inputs(p[..., OFF_SSM:], ssm_conv_w[i], ssm_conv_b[i])
        yl_f, _ = ssm_direction(xsl, Bl, Cl, dtl[:, :, 0], ssm_dt_bias[i, 0], ssm_a_log[i, 0], hf, False, True)
        yl_b, _ = ssm_direction(xsl, Bl, Cl, dtl[:, :, 1], ssm_dt_bias[i, 1], ssm_a_log[i, 1], hb, True, True)
        yd = ssm_output(yl_f + yl_b, xsl, zl, ssm_d[i], ssm_norm[i])

        y = jnp.concatenate([ya, yb, yn, yd], axis=-1)
        x = x + g_a * (y @ w_out[i])

        if need_ctx:
            filt_c = hyena_filters(Lc, hy_w1[i], hy_b1[i], hy_w2[i], hy_b2[i], hy_w3[i], hy_b3[i], hy_w4[i], hy_freq[i])
            yc = jnp.concatenate([
                context_attention(qac, kac, vac, attn_sink[i]),
                hyena_mixer(pc[..., OFF_HY:OFF_NA], hy_short_w[i], hy_short_b[i], filt_c, hy_skip[i]),
                context_attention(qnc, knc, vnc),
                ssm_output(yc_f + yc_b, xsc, zc, ssm_d[i], ssm_norm[i]),
            ], axis=-1)
            xc = xc + gc_a * (yc @ w_out[i])

        hm = rms_norm(x, norm_mlp[i]) * (1 + sc_m) + sh_m
        x = x + g_m * sq_relu_mlp(hm, mlp_w1[i], mlp_w2[i])
        if need_ctx:
            hmc = rms_norm(xc, norm_mlp[i]) * (1 + scc_m) + shc_m
            xc = xc + gc_m * sq_relu_mlp(hmc, mlp_w1[i], mlp_w2[i])
    return rms_norm(x, final_norm)
```

```python
import math
import numpy as np
import concourse.bass as bass
import concourse.mybir as mybir
from concourse.bass_utils import run_bass_kernel_spmd

F32 = mybir.dt.float32
BF16 = mybir.dt.bfloat16
AF = mybir.ActivationFunctionType
ALU = mybir.AluOpType
AX = mybir.AxisListType


class Buf:
    __slots__ = ("t", "name", "w", "r", "dsem", "dcnt")

    def __init__(self, t, name):
        self.t = t
        self.name = name
        self.w = {}
        self.r = {}
        self.dsem = None
        self.dcnt = 0

    def __getitem__(self, idx):
        return self.t[idx]


class _Rec:
    def __init__(self):
        self.call = None

    def __getattr__(self, name):
        def f(*a, **k):
            self.call = (name, a, k)
            return self
        return f


class Prog:
    ENG = ("pe", "act", "dve", "pool", "sp")

    def __init__(self, nc):
        self.nc = nc
        self.q = {e: [] for e in self.ENG}
        self.cnt = {e: 0 for e in self.ENG}
        self.sem = {}
        self.known = {e: {} for e in self.ENG}
        self.ctx = []
        self.perm = []
        self.nsem = 0
        self.out_tokens = {}
        self.scopes = []
        self.free_dsems = []
        self.all_dsems = []
        self.semcount = {}
        self.scope_bufs = []
        for e in self.ENG:
            self.sem[e] = self._newsem("s_" + e)

    def _newsem(self, name):
        g = self.nc.semaphore(name)
        s = g.__enter__()
        self.perm.append(g)
        self.nsem += 1
        return s

    def sb(self, name, shape, dt):
        self.uid = getattr(self, "uid", 0) + 1
        g = self.nc.sbuf_tensor(name + "_%d" % self.uid, list(shape), dt)
        t = g.__enter__()
        self.ctx.append(g)
        b = Buf(t, name)
        self.scope_bufs.append(b)
        return b

    def _get_dsem(self, owner):
        if owner.dsem is None:
            if self.free_dsems:
                owner.dsem = self.free_dsems.pop()
            else:
                owner.dsem = self._newsem_perm("d%d" % len(self.all_dsems))
                self.all_dsems.append(owner.dsem)
                self.semcount[owner.dsem] = 0
        return owner.dsem

    def _newsem_perm(self, name):
        g = self.nc.semaphore(name)
        s = g.__enter__()
        self.perm.append(g)
        return s

    def scope_begin(self):
        self.scopes.append((len(self.ctx), len(self.scope_bufs)))

    def barrier(self):
        for e in self.ENG:
            waits = []
            kn = self.known[e]
            for e2 in self.ENG:
                if e2 != e and self.cnt[e2] > kn.get(self.sem[e2], 0):
                    kn[self.sem[e2]] = self.cnt[e2]
                    waits.append((self.sem[e2], self.cnt[e2]))
            for s_ in self.all_dsems:
                v = self.semcount[s_]
                if v > kn.get(s_, 0):
                    kn[s_] = v
                    waits.append((s_, v))
            if waits:
                self.q[e].append((waits, None, None, 0))

    def scope_end(self):
        self.barrier()
        n, nb = self.scopes.pop()
        for b in self.scope_bufs[nb:]:
            if b.dsem is not None:
                self.free_dsems.append(b.dsem)
                b.dsem = None
        del self.scope_bufs[nb:]
        while len(self.ctx) > n:
            g = self.ctx.pop()
            g.__exit__(None, None, None)

    def ps(self, name, shape, dt=F32):
        g = self.nc.psum_tensor(name, list(shape), dt)
        t = g.__enter__()
        self.ctx.append(g)
        return Buf(t, name)

    def dram(self, name, shape, dt, kind="Internal"):
        t = self.nc.dram_tensor(name, list(shape), dt, kind=kind)
        return Buf(t.ap(), name)

    def mm(self, out, lhsT, rhs, start, stop, reads, writes):
        return self.op("pe", lambda e: e.matmul(out, lhsT, rhs, start=start, stop=stop, skip_group_check=True),
                       reads=reads, writes=writes)

    def _waits(self, eng, reads, writes, wdisj=()):
        need = {}
        for b in wdisj:
            for s, v in b.r.items():
                if need.get(s, 0) < v:
                    need[s] = v
        for b in reads:
            for s, v in b.w.items():
                if need.get(s, 0) < v:
                    need[s] = v
        for b in writes:
            for s, v in b.w.items():
                if need.get(s, 0) < v:
                    need[s] = v
            for s, v in b.r.items():
                if need.get(s, 0) < v:
                    need[s] = v
        out = []
        kn = self.known[eng]
        for s, v in need.items():
            if eng == "pe" and s is self.sem["pe"]:
                continue
            if kn.get(s, 0) >= v:
                continue
            kn[s] = v
            out.append((s, v))
        return out

    def _record(self, tok, reads, writes, wdisj=()):
        s, v = tok
        for b in wdisj:
            if b.r:
                b.w = {s: v}
                b.r = {}
            elif b.w.get(s, 0) < v:
                b.w[s] = v
        for b in reads:
            if b.r.get(s, 0) < v:
                b.r[s] = v
        for b in writes:
            b.w = {s: v}
            b.r = {}

    def op(self, eng, fn, reads=(), writes=(), wdisj=()):
        rec = _Rec()
        fn(rec)
        name_, a_, k_ = rec.call

        def fn(e, name_=name_, a_=a_, k_=k_):
            return getattr(e, name_)(*a_, **k_)
        waits = self._waits(eng, reads, writes, wdisj)
        self.cnt[eng] += 1
        tok = (self.sem[eng], self.cnt[eng])
        self.q[eng].append((waits, fn, tok[0], 1))
        self._record(tok, reads, writes, wdisj)
        return tok

    def dma(self, eng, out_ap, in_ap, reads=(), writes=(), wdisj=(), owner=None, final=False):
        if owner is None:
            owner = writes[0] if writes else reads[0]
        ds = self._get_dsem(owner)
        waits = self._waits(eng, reads, writes, wdisj)
        kn = self.known[eng]
        cur = self.semcount[ds]
        if cur and kn.get(ds, 0) < cur:
            kn[ds] = cur
            waits.append((ds, cur))
        self.semcount[ds] = cur + 16
        tok = (ds, cur + 16)

        def fn(e, o=out_ap, i=in_ap):
            return e.dma_start(out=o, in_=i)
        self.q[eng].append((waits, fn, tok[0], 16))
        self._record(tok, reads, writes, wdisj)
        if final:
            self.out_tokens[tok[0]] = tok[1]
        return tok

    def simulate_sync(self):
        pos = {e: 0 for e in self.ENG}
        val = {}
        progress = True
        while progress:
            progress = False
            for e in self.ENG:
                q = self.q[e]
                while pos[e] < len(q):
                    waits, fn, s_, inc = q[pos[e]]
                    if any(val.get(id(ws), 0) < wv for ws, wv in waits):
                        break
                    if fn is not None:
                        val[id(s_)] = val.get(id(s_), 0) + inc
                    pos[e] += 1
                    progress = True
        stuck = {e: (pos[e], len(self.q[e])) for e in self.ENG if pos[e] < len(self.q[e])}
        if not stuck:
            return None
        rep = {}
        for e, (p, n) in stuck.items():
            waits = self.q[e][p][0]
            rep[e] = (p, n, [(getattr(ws, "name", str(ws)), wv, val.get(id(ws), 0)) for ws, wv in waits])
        return rep

    def emit(self):
        nc = self.nc
        fin = list(self.out_tokens.items())
        q = self.q
        with nc.Block() as block:
            def run(e, lst, extra=()):
                for waits, fn, s, inc in lst:
                    for ws, wv in waits:
                        e.wait_ge(ws, wv)
                    if fn is not None:
                        fn(e).then_inc(s, inc)
                for ws, wv in extra:
                    e.wait_ge(ws, wv)

            @block.sync
            def _(e):
                run(e, q["sp"], fin)

            @block.tensor
            def _(e):
                run(e, q["pe"])

            @block.scalar
            def _(e):
                run(e, q["act"])

            @block.vector
            def _(e):
                run(e, q["dve"])

            @block.gpsimd
            def _(e):
                run(e, q["pool"])
        for g in reversed(self.ctx):
            g.__exit__(None, None, None)
        for g in reversed(self.perm):
            g.__exit__(None, None, None)


D = 2048
KC = 16
T = 4096
TC = 256
TT = T + TC
NL = 4
NFM = 44
NTM = 656
CHUNKS = [(i * 512, 512, 0) for i in range(8)] + [(T, TC, 1)]
NEGM = -30000.0


def _fm_cols():
    cols = []
    rp = np.concatenate([np.arange(16, 32), np.arange(0, 16), np.arange(48, 64), np.arange(32, 48)])
    qa = np.arange(512)
    qap = (np.arange(8)[:, None] * 64 + rp[None]).reshape(-1)
    cols += [qa, qap]
    for perm in (False, True):
        for g in range(2):
            base = 512 + g * 64 + (rp if perm else np.arange(64))
            cols.append(np.concatenate([base, base]))
    cols.append(768 + np.arange(1536))
    cols.append(2304 + np.arange(1024))
    cols.append(3840 + np.arange(512))
    cols.append(4352 + np.arange(1024))
    c = np.concatenate(cols)
    assert c.shape[0] == NFM * 128
    return c


def _tm_cols():
    return np.concatenate([640 + np.arange(128), 3328 + np.arange(512), 5376 + np.arange(16)])


def _rope_tables():
    t = np.arange(T)
    row, col = t // 64, t % 64
    inv = 10000.0 ** (-np.arange(16, dtype=np.float64) / 16)
    d = np.arange(64)
    pos = np.where(d[:, None] < 32, row[None], col[None]).astype(np.float64)
    ang = pos * inv[d % 16][:, None]
    cos = np.cos(ang)
    sin = np.sin(ang) * np.where((d % 32) < 16, -1.0, 1.0)[:, None]
    cos2 = np.concatenate([cos, cos], 0)
    sin2 = np.concatenate([sin, sin], 0)
    return np.stack([cos2 * 0.125, sin2 * 0.125, cos2, sin2]).astype(np.float32)


def _na_cases():
    cases = [(10, 10 + dk) for dk in range(-2, 3)]
    for R2 in (0, 1):
        cases += [(R2, K2) for K2 in range(4)]
    for R2 in (30, 31):
        cases += [(R2, K2) for K2 in range(28, 32)]
    return cases


def _na_case_id(R2, K2):
    if 2 <= R2 <= 29:
        return K2 - R2 + 2
    if R2 < 2:
        return 5 + R2 * 4 + K2
    return 13 + (R2 - 30) * 4 + (K2 - 28)


def _na_tables(rpb):
    cases = _na_cases()
    kk = np.arange(128)
    qq = np.arange(128)
    out = np.empty((len(cases), 2, 128, 512), np.float32)
    for ci, (R2, K2) in enumerate(cases):
        kr = 2 * K2 + kk // 64
        ck = kk % 64
        r = 2 * R2 + qq // 64
        cq = qq % 64
        rstart = np.clip(r - 4, 0, 56)
        cstart = np.clip(cq - 8, 0, 48)
        vr = (kr[:, None] >= rstart[None]) & (kr[:, None] < rstart[None] + 8)
        vc = (ck[:, None] >= cstart[None]) & (ck[:, None] < cstart[None] + 16)
        roff = np.clip(kr[:, None] - r[None] + 7, 0, 14)
        coff = np.clip(ck[:, None] - cq[None], -15, 15) + 15
        valid = vr & vc
        for h in range(8):
            bias = rpb[h][roff, coff]
            out[ci, h // 4, :, (h % 4) * 128:(h % 4 + 1) * 128] = np.where(valid, bias, np.float32(NEGM))
    return out


def _hy_tables(L):
    t = np.linspace(0.0, 1.0, L, dtype=np.float32)[:, None]
    f = np.linspace(1e-4, 15, 16, dtype=np.float32)[None]
    wpos = (2.0 * math.pi * np.arange(L, dtype=np.float32)[:, None] / L).astype(np.float32)
    z = np.concatenate([t, np.cos(f * wpos), -np.sin(f * wpos)], axis=-1).astype(np.float32)
    max_decay = math.log(1e-2) / 0.3
    min_decay = math.log(1e-2) / 1.5
    deltas = np.linspace(min_decay, max_decay, 512, dtype=np.float32)
    decay = np.exp(-t * np.abs(deltas)[None]).astype(np.float32)
    return np.ascontiguousarray(z.T), np.ascontiguousarray(decay.T)


def _prep_shared(inp):
    sh = {}
    f32 = np.float32
    sh["adaw"] = np.ascontiguousarray(inp["ada_w"].reshape(NL, 128, 16, 6 * D).transpose(0, 2, 1, 3))
    sh["adab"] = np.ascontiguousarray(inp["ada_b"])
    sh["nrm"] = np.ascontiguousarray(np.stack([inp["norm_mix"], inp["norm_mlp"]], 1).reshape(NL, 2, 128, 16))
    sh["fnorm"] = np.ascontiguousarray(inp["final_norm"].reshape(128, 16))
    w_in = inp["w_in"].reshape(NL, 128, 16, -1)
    fm = w_in[..., _fm_cols()].reshape(NL, 128, 16, NFM, 128)
    sh["win_fm"] = np.ascontiguousarray(fm.transpose(0, 3, 1, 2, 4))
    sh["win_tm"] = np.ascontiguousarray(w_in[..., _tm_cols()])
    wo = inp["w_out"].reshape(NL, 16, 128, 128, 16)
    sh["wout"] = np.ascontiguousarray(wo.transpose(0, 4, 2, 1, 3))
    w1 = inp["mlp_w1"].reshape(NL, 128, 16, 64, 128)
    sh["w1"] = np.ascontiguousarray(w1.transpose(0, 3, 1, 2, 4))
    w2 = inp["mlp_w2"].reshape(NL, 64, 128, 128, 16)
    sh["w2"] = np.ascontiguousarray(w2.transpose(0, 4, 2, 1, 3))
    sh["rope"] = _rope_tables()
    kk = np.arange(128)[:, None]
    qq = np.arange(128)[None]
    mp = np.where(qq <= kk, 0.0, NEGM).astype(f32)
    mn = np.where(kk <= qq, 0.0, NEGM).astype(f32)
    sh["cmask"] = np.stack([np.tile(mp, (1, 4)), np.tile(mn, (1, 4))]).astype(f32)
    sh["sink"] = np.ascontiguousarray(inp["attn_sink"])
    sh["nbt"] = np.stack([_na_tables(inp["na_rpb"][l]) for l in range(NL)])
    sh["scw"] = np.ascontiguousarray(inp["ssm_conv_w"].reshape(NL, 3, 8, 128).transpose(0, 3, 2, 1))
    sh["scb"] = np.ascontiguousarray(inp["ssm_conv_b"].reshape(NL, 8, 128).transpose(0, 2, 1))
    sh["sdtb"] = np.ascontiguousarray(inp["ssm_dt_bias"].reshape(NL, 16))
    sh["salog"] = np.ascontiguousarray(inp["ssm_a_log"].reshape(NL, 16))
    sh["sd"] = np.ascontiguousarray(np.repeat(inp["ssm_d"], 64, axis=1).reshape(NL, 4, 128).transpose(0, 2, 1))
    sh["snw"] = np.ascontiguousarray(inp["ssm_norm"].reshape(NL, 4, 128).transpose(0, 2, 1))
    tri = np.triu(np.ones((128, 128), f32))
    sh["tri"] = np.stack([tri, tri.T]).astype(f32)
    mk = np.where(tri > 0, 0.0, NEGM).astype(f32)
    sh["trimask"] = np.stack([mk, mk.T]).astype(f32)
    sh["ident"] = np.eye(128, dtype=f32)
    sh["hsw"] = np.ascontiguousarray(inp["hy_short_w"].reshape(NL, 3, 12, 128).transpose(0, 3, 2, 1))
    sh["hsb"] = np.ascontiguousarray(inp["hy_short_b"].reshape(NL, 12, 128).transpose(0, 2, 1))
    w123 = np.zeros((NL, 3, 128, 128), f32)
    w123[:, 0, :33, :64] = inp["hy_w1"]
    w123[:, 1, :64, :64] = inp["hy_w2"]
    w123[:, 2, :64, :64] = inp["hy_w3"]
    sh["hw123"] = w123
    w4p = np.zeros((NL, 128, 2048), f32)
    w4p[:, :64, :] = inp["hy_w4"]
    sh["hw4p"] = w4p
    hb = np.zeros((NL, 128, 4), f32)
    hb[:, :64, :] = np.stack([inp["hy_b1"], inp["hy_b2"], inp["hy_b3"], inp["hy_freq"]], 2)
    sh["hb"] = hb
    sh["hskip"] = np.ascontiguousarray(inp["hy_skip"].reshape(NL, 2, 4, 128).transpose(0, 3, 1, 2))
    zl, dl = _hy_tables(T)
    zc, dc = _hy_tables(TC)
    hz = np.zeros((128, TT), f32)
    hz[:33] = np.concatenate([zl, zc], 1)
    sh["hz"] = hz
    sh["hdec"] = np.ascontiguousarray(np.concatenate([dl, dc], 1))
    return sh


def _prep_core(inp, b):
    xcat = np.concatenate([inp["x"][b], inp["ctx"][b]], 0)
    xT = np.ascontiguousarray(xcat.reshape(TT, 128, 16).transpose(1, 2, 0))
    cv = np.ascontiguousarray(np.stack([inp["c"][b].reshape(128, 16), inp["c_ctx"].reshape(128, 16)], 2))
    return {"xT": xT, "cv": cv}


def bcast(ap, shape):
    return ap.broadcast_to(list(shape))


class Ctx:
    pass


def build_program(nlayers=NL, dbg=False, stop=99, attn=True):
    nc = bass.Bass("TRN2", target_bir_lowering=False)
    P = Prog(nc)
    G = Ctx()
    G.P = P
    G.dbg = dbg
    di = lambda n, s, dt=F32: P.dram(n, s, dt, kind="ExternalInput")
    G.xT = di("xT", [128, KC, TT])
    G.cv = di("cv", [128, KC, 2])
    G.adaw = di("adaw", [NL, KC, 128, 6 * D])
    G.adab = di("adab", [NL, 6 * D])
    G.nrm = di("nrm", [NL, 2, 128, KC])
    G.fnorm = di("fnorm", [128, KC])
    G.win_fm = di("win_fm", [NL, NFM, 128, KC, 128])
    G.win_tm = di("win_tm", [NL, 128, KC, NTM])
    G.wout = di("wout", [NL, 16, 128, 16, 128])
    G.w1 = di("w1", [NL, 64, 128, KC, 128])
    G.w2 = di("w2", [NL, 16, 128, 64, 128])
    G.rope = di("rope", [4, 128, T])
    G.cmask = di("cmask", [2, 128, 512])
    G.sink = di("sink", [NL, 8])
    G.nbt = di("nbt", [NL, 21, 2, 128, 512])
    G.scw = di("scw", [NL, 128, 8, 3])
    G.scb = di("scb", [NL, 128, 8])
    G.sdtb = di("sdtb", [NL, 16])
    G.salog = di("salog", [NL, 16])
    G.sd = di("sd", [NL, 128, 4])
    G.snw = di("snw", [NL, 128, 4])
    G.tri = di("tri", [2, 128, 128])
    G.trimask = di("trimask", [2, 128, 128])
    G.ident = di("ident", [128, 128])
    G.hsw = di("hsw", [NL, 128, 12, 3])
    G.hsb = di("hsb", [NL, 128, 12])
    G.hw123 = di("hw123", [NL, 3, 128, 128])
    G.hw4p = di("hw4p", [NL, 128, 2048])
    G.hb = di("hb", [NL, 128, 4])
    G.hskip = di("hskip", [NL, 128, 2, 4])
    G.hz = di("hz", [128, TT])
    G.hdec = di("hdec", [512, TT])
    G.out = P.dram("out", [128, KC, T], F32, kind="ExternalOutput")
    G.xs = P.dram("xs", [128, KC, TT], F32)
    G.modv = P.dram("modv", [NL, 2, 6 * D], F32)
    G.pF = P.dram("pF", [NFM * 128, TT], F32)
    G.pT = P.dram("pT", [TT, NTM], F32)
    G.yT = P.dram("yT", [2048, TT], BF16)
    G.sxs = P.dram("sxs", [512, TT], F32)
    G.ysd = P.dram("ysd", [2, 512, TT], F32)
    if dbg:
        G.d_pF = P.dram("d_pF", [NFM * 128, TT], F32, kind="ExternalOutput")
        G.d_pT = P.dram("d_pT", [TT, NTM], F32, kind="ExternalOutput")
        G.d_yT = P.dram("d_yT", [2048, TT], BF16, kind="ExternalOutput")
        G.d_xs = P.dram("d_xs", [128, KC, TT], F32, kind="ExternalOutput")
        G.d_mod = P.dram("d_mod", [NL, 2, 6 * D], F32, kind="ExternalOutput")
    G.PB = [P.ps("pb%d" % i, [128, 512]) for i in range(8)]
    G.pbi = 0

    def pb():
        G.pbi = (G.pbi + 1) % 8
        return G.PB[G.pbi]
    G.pb = pb
    G.pb6i = 0

    def pb6():
        G.pb6i = (G.pb6i + 1) % 6
        return G.PB[G.pb6i]
    G.pb6 = pb6
    G.ones = P.sb("ones", [128, 128], BF16)
    P.op("dve", lambda e: e.memset(G.ones[:], 1.0), writes=[G.ones])
    G.identb = P.sb("identb", [128, 128], BF16)
    P.dma("pool", G.identb[:], G.ident[:], reads=[G.ident], writes=[G.identb])
    G.modsb = P.sb("modsb", [128, 2, 6, KC], F32)
    G.amod = P.sb("amod", [128, 2, 2, KC], F32)
    G.nrmsb = P.sb("nrmsb", [128, 2, KC], F32)

    phase_mod(G)
    P.dma("sp", G.xs[:], G.xT[:], reads=[G.xT], writes=[G.xs], owner=G.ones)
    for l in range(nlayers):
        last = (l == NL - 1)
        if stop < 1:
            break
        load_mod(G, l)
        for half in (CHUNKS[0:4], CHUNKS[4:9]):
            phase_inproj(G, l, half)
        if stop < 2:
            break
        if dbg and l == 0:
            P.dma("sp", G.d_pF[:], G.pF[:], reads=[G.pF], writes=[G.d_pF], owner=G.ones, final=True)
            P.dma("sp", G.d_pT[:], G.pT[:], reads=[G.pT], writes=[G.d_pT], owner=G.identb, final=True)
        if attn:
            phase_attn_a(G, l, last)
            if stop < 3:
                break
            phase_attn_c(G, l, last)
            if stop < 4:
                break
        else:
            _zero_rows(G, 0, 512)
            _zero_rows(G, 1024, 1536)
        phase_ssd(G, l, last)
        phase_hyena(G, l, last)
        phase_out_mlp(G, l, last)
    if dbg:
        P.dma("sp", G.d_yT[:], G.yT[:], reads=[G.yT], writes=[G.d_yT], owner=G.ones, final=True)
        P.dma("sp", G.d_xs[:], G.xs[:], reads=[G.xs], writes=[G.d_xs], owner=G.ones, final=True)
        P.dma("sp", G.d_mod[:], G.modv[:], reads=[G.modv], writes=[G.d_mod], owner=G.identb, final=True)
    phase_final(G)
    P.emit()
    return nc


def evac(G, i, out_ap, in_ap, reads, writes, wdisj=()):
    P = G.P
    if i % 2 == 0:
        return P.op("act", lambda e: e.activation(out_ap, in_ap, AF.Copy), reads=reads, writes=writes, wdisj=wdisj)
    return P.op("dve", lambda e: e.tensor_copy(out_ap, in_ap), reads=reads, writes=writes, wdisj=wdisj)


def phase_mod(G):
    P = G.P
    P.scope_begin()
    cvs = P.sb("cvs", [128, KC, 2], F32)
    scb = P.sb("scb", [128, KC, 2], BF16)
    P.dma("sp", cvs[:], G.cv[:], reads=[G.cv], writes=[cvs])
    P.op("act", lambda e: e.activation(scb[:], cvs[:], AF.Silu), reads=[cvs], writes=[scb])
    wb = [P.sb("adw%d" % i, [128, KC, 512], BF16) for i in range(2)]
    bt = [P.sb("adb%d" % i, [2, 512], F32) for i in range(2)]
    mr = [P.sb("mr%d" % i, [2, 512], F32) for i in range(2)]
    it = 0
    for l in range(NL):
        for nch in range(24):
            w = wb[it % 2]
            b_ = bt[it % 2]
            m_ = mr[it % 2]
            P.dma("pool", w[:], G.adaw[l, :, :, nch * 512:(nch + 1) * 512].rearrange("k p n -> p k n"),
                  reads=[G.adaw], writes=[w])
            P.dma("sp", b_[:], G.adab[l:l + 1, nch * 512:(nch + 1) * 512].broadcast_to([2, 512]),
                  reads=[G.adab], writes=[b_])
            ps = G.pb()
            for kc in range(KC):
                P.mm(ps[0:2, :], scb[:, kc, :], w[:, kc, :], kc == 0, kc == KC - 1, [scb, w], [ps])
            P.op("dve", lambda e, m_=m_, ps=ps, b_=b_: e.tensor_tensor(m_[:], ps[0:2, :], b_[:], ALU.add),
                 reads=[ps, b_], writes=[m_])
            P.dma("sp", G.modv[l, :, nch * 512:(nch + 1) * 512], m_[:], reads=[m_], wdisj=[G.modv], owner=m_)
            it += 1
    P.scope_end()


def load_mod(G, l):
    P = G.P
    P.dma("sp", G.modsb[:], G.modv[l].rearrange("s (x p k) -> p s x k", x=6, p=128, k=KC),
          reads=[G.modv], writes=[G.modsb])
    P.dma("sp", G.nrmsb[:], G.nrm[l].rearrange("a p k -> p a k"), reads=[G.nrm], writes=[G.nrmsb])
    for s in range(2):
        for j, x in ((0, 1), (1, 4)):
            P.op("dve", lambda e, s=s, j=j, x=x: e.scalar_tensor_tensor(
                G.amod[:, s, j, :], G.modsb[:, s, x, :], 1.0, G.nrmsb[:, j, :], ALU.add, ALU.mult),
                reads=[G.modsb, G.nrmsb], writes=[G.amod])


def norm_mod(G, xc, n, s, which, hT, hoff, sqb, rstd):
    P = G.P
    P.op("act", lambda e: e.activation(sqb[:, :, :n], xc[:, :, :n], AF.Square), reads=[xc], writes=[sqb])
    ps = G.pb()
    for kc in range(KC):
        P.mm(ps[:, :n], G.ones[:], sqb[:, kc, :n], kc == 0, kc == KC - 1, [G.ones, sqb], [ps])
    P.op("dve", lambda e: e.tensor_scalar(rstd[:, :n], ps[:, :n], 1.0 / D, 1e-6, ALU.mult, ALU.add),
         reads=[ps], writes=[rstd])
    P.op("act", lambda e: e.activation(rstd[:, :n], rstd[:, :n], AF.Sqrt), reads=[rstd], writes=[rstd])
    P.op("dve", lambda e: e.reciprocal(rstd[:, :n], rstd[:, :n]), reads=[rstd], writes=[rstd])
    P.op("dve", lambda e: e.tensor_tensor(xc[:, :, :n], xc[:, :, :n], bcast(rstd[:, None, :n], [128, KC, n]), ALU.mult),
         reads=[xc, rstd], writes=[xc])
    shi = 0 if which == 0 else 3
    for kc in range(KC):
        P.op("act", lambda e, kc=kc: e.activation(hT[:, kc, hoff:hoff + n], xc[:, kc, :n], AF.Identity,
                                                  bias=G.modsb[:, s, shi, kc:kc + 1],
                                                  scale=G.amod[:, s, which, kc:kc + 1]),
             reads=[xc, G.modsb, G.amod], writes=[hT])


def phase_inproj(G, l, chunks):
    P = G.P
    P.scope_begin()
    ntok = sum(c[1] for c in chunks)
    tbase = chunks[0][0]
    hT = P.sb("hT", [128, KC, ntok], BF16)
    xc = P.sb("xc", [128, KC, 512], F32)
    sqb = P.sb("sqb", [128, KC, 512], BF16)
    rstd = P.sb("rstd", [128, 512], F32)
    for (t0, n, s) in chunks:
        P.dma("sp", xc[:, :, :n], G.xs[:, :, t0:t0 + n], reads=[G.xs], writes=[xc])
        norm_mod(G, xc, n, s, 0, hT, t0 - tbase, sqb, rstd)
    wb = [P.sb("wfm%d" % i, [128, KC, 128], BF16) for i in range(2)]
    stg = [P.sb("stg%d" % i, [128, 512], F32) for i in range(4)]
    it = 0
    for mt in range(NFM):
        w = wb[mt % 2]
        P.dma("pool", w[:], G.win_fm[l, mt], reads=[G.win_fm], writes=[w])
        for (t0, n, s) in chunks:
            ps = G.pb()
            for kc in range(KC):
                P.mm(ps[:, :n], w[:, kc, :], hT[:, kc, t0 - tbase:t0 - tbase + n], kc == 0, kc == KC - 1, [w, hT], [ps])
            sg = stg[it % 4]
            evac(G, it, sg[:, :n], ps[:, :n], [ps], [sg])
            P.dma("sp" if it % 2 == 0 else "act", G.pF[mt * 128:(mt + 1) * 128, t0:t0 + n], sg[:, :n],
                  reads=[sg], wdisj=[G.pF], owner=sg)
            it += 1
    wtm = P.sb("wtm", [128, KC, NTM], BF16)
    P.dma("pool", wtm[:], G.win_tm[l], reads=[G.win_tm], writes=[wtm])
    stT = [P.sb("stT%d" % i, [128, NTM], F32) for i in range(2)]
    for ti in range(ntok // 128):
        psa = G.pb()
        psb = G.pb()
        for kc in range(KC):
            P.mm(psa[:, :], hT[:, kc, ti * 128:(ti + 1) * 128], wtm[:, kc, 0:512], kc == 0, kc == KC - 1, [wtm, hT], [psa])
        for kc in range(KC):
            P.mm(psb[:, :NTM - 512], hT[:, kc, ti * 128:(ti + 1) * 128], wtm[:, kc, 512:NTM], kc == 0, kc == KC - 1, [wtm, hT], [psb])
        sg = stT[ti % 2]
        evac(G, 0, sg[:, 0:512], psa[:, :], [psa], [], wdisj=[sg])
        evac(G, 1, sg[:, 512:NTM], psb[:, :NTM - 512], [psb], [], wdisj=[sg])
        P.dma("sp", G.pT[tbase + ti * 128: tbase + (ti + 1) * 128, :], sg[:], reads=[sg], wdisj=[G.pT], owner=sg)
    P.scope_end()


def phase_out_mlp(G, l, last):
    P = G.P
    P.scope_begin()
    xc = P.sb("xc", [128, KC, 512], F32)
    yc = P.sb("yc", [128, KC, 512], BF16)
    hm = yc
    sqb = P.sb("sqb", [128, KC, 512], BF16)
    rstd = P.sb("rstd", [128, 512], F32)
    hid = P.sb("hid", [128, 64, 512], BF16)
    wo = [P.sb("wo%d" % i, [128, KC, 128], BF16) for i in range(2)]
    w1b = [P.sb("w1b%d" % i, [128, KC, 128], BF16) for i in range(2)]
    w2b = [P.sb("w2b%d" % i, [128, 64, 128], BF16) for i in range(2)]
    it = 0
    for (t0, n, s) in CHUNKS:
        if last and s == 1:
            continue
        P.dma("sp", xc[:, :, :n], G.xs[:, :, t0:t0 + n], reads=[G.xs], writes=[xc])
        P.dma("act", yc[:, :, :n], G.yT[:, t0:t0 + n].rearrange("(k p) t -> p k t", p=128), reads=[G.yT], writes=[yc])
        for mt in range(16):
            w = wo[mt % 2]
            P.dma("pool", w[:], G.wout[l, mt], reads=[G.wout], writes=[w])
            ps = G.pb()
            for kc in range(KC):
                P.mm(ps[:, :n], w[:, kc, :], yc[:, kc, :n], kc == 0, kc == KC - 1, [w, yc], [ps])
            P.op("dve", lambda e, ps=ps, mt=mt: e.scalar_tensor_tensor(
                xc[:, mt, :n], ps[:, :n], G.modsb[:, s, 2, mt:mt + 1], xc[:, mt, :n], ALU.mult, ALU.add),
                reads=[ps, G.modsb, xc], writes=[xc])
        P.dma("sp", G.xs[:, :, t0:t0 + n], xc[:, :, :n], reads=[xc], wdisj=[G.xs], owner=sqb)
        norm_mod(G, xc, n, s, 1, hm, 0, sqb, rstd)
        for ht in range(64):
            w = w1b[ht % 2]
            P.dma("pool", w[:], G.w1[l, ht], reads=[G.w1], writes=[w])
            ps = G.pb()
            for kc in range(KC):
                P.mm(ps[:, :n], w[:, kc, :], hm[:, kc, :n], kc == 0, kc == KC - 1, [w, hm], [ps])
            P.op("act", lambda e, ps=ps, ht=ht: e.activation(sqb[:, ht % KC, :n], ps[:, :n], AF.Relu),
                 reads=[ps], wdisj=[sqb])
            P.op("dve", lambda e, ht=ht: e.tensor_tensor(
                hid[:, ht, :n], sqb[:, ht % KC, :n], sqb[:, ht % KC, :n], ALU.mult), reads=[sqb], wdisj=[hid])
        P.dma("sp", xc[:, :, :n], G.xs[:, :, t0:t0 + n], reads=[G.xs], writes=[xc])
        for mt in range(16):
            w = w2b[mt % 2]
            P.dma("pool", w[:], G.w2[l, mt], reads=[G.w2], writes=[w])
            ps = G.pb()
            for hc in range(64):
                P.mm(ps[:, :n], w[:, hc, :], hid[:, hc, :n], hc == 0, hc == 63, [w, hid], [ps])
            P.op("dve", lambda e, ps=ps, mt=mt: e.scalar_tensor_tensor(
                xc[:, mt, :n], ps[:, :n], G.modsb[:, s, 5, mt:mt + 1], xc[:, mt, :n], ALU.mult, ALU.add),
                reads=[ps, G.modsb, xc], writes=[xc])
        P.dma("sp", G.xs[:, :, t0:t0 + n], xc[:, :, :n], reads=[xc], wdisj=[G.xs], owner=rstd)
        it += 1
    P.scope_end()


def phase_final(G):
    P = G.P
    P.scope_begin()
    xc = P.sb("xc", [128, KC, 512], F32)
    sqb = P.sb("sqb", [128, KC, 512], BF16)
    rstd = P.sb("rstd", [128, 512], F32)
    fw = P.sb("fw", [128, KC], F32)
    P.dma("sp", fw[:], G.fnorm[:], reads=[G.fnorm], writes=[fw])
    for (t0, n, s) in CHUNKS[:8]:
        P.dma("sp", xc[:, :, :n], G.xs[:, :, t0:t0 + n], reads=[G.xs], writes=[xc])
        P.op("act", lambda e: e.activation(sqb[:, :, :n], xc[:, :, :n], AF.Square), reads=[xc], writes=[sqb])
        ps = G.pb()
        for kc in range(KC):
            P.mm(ps[:, :n], G.ones[:], sqb[:, kc, :n], kc == 0, kc == KC - 1, [G.ones, sqb], [ps])
        P.op("dve", lambda e, ps=ps: e.tensor_scalar(rstd[:, :n], ps[:, :n], 1.0 / D, 1e-6, ALU.mult, ALU.add),
             reads=[ps], writes=[rstd])
        P.op("act", lambda e: e.activation(rstd[:, :n], rstd[:, :n], AF.Sqrt), reads=[rstd], writes=[rstd])
        P.op("dve", lambda e: e.reciprocal(rstd[:, :n], rstd[:, :n]), reads=[rstd], writes=[rstd])
        P.op("dve", lambda e: e.tensor_tensor(xc[:, :, :n], xc[:, :, :n], bcast(rstd[:, None, :n], [128, KC, n]), ALU.mult),
             reads=[xc, rstd], writes=[xc])
        P.op("dve", lambda e: e.tensor_tensor(xc[:, :, :n], xc[:, :, :n], bcast(fw[:, :, None], [128, KC, n]), ALU.mult),
             reads=[xc, fw], writes=[xc])
        P.dma("sp", G.out[:, :, t0:t0 + n], xc[:, :, :n], reads=[xc], wdisj=[G.out], owner=xc, final=True)
    P.scope_end()


def bc_mid(ap2, k):
    p, n = ap2.shape
    return ap2.unsqueeze(1).broadcast_to([p, k, n])


def bc_last(ap2, n):
    p, k = ap2.shape
    return ap2.unsqueeze(2).broadcast_to([p, k, n])


def attn_core(G, S_mm, key_tiles, pv_mm, n_pt, PT, ptc):
    P = G.P
    nk = len(key_tiles)
    for i, kt in enumerate(key_tiles):
        ps = G.pb6()
        S_mm(ps, kt)
        pt = PT[ptc[0] % n_pt]
        ptc[0] += 1
        P.op("act", lambda e, pt=pt, ps=ps: e.activation(pt[:], ps[:], AF.Exp), reads=[ps], writes=[pt])
        pv_mm(pt, kt, i == 0, i == nk - 1)


def phase_attn_a(G, l, last):
    P = G.P
    P.scope_begin()
    QA = P.sb("QA", [128, 4, TT], BF16)
    KAz = P.sb("KAz", [128, 2, 2, TT], BF16)
    VA = P.sb("VA", [128, 34, 128], BF16)
    cm = P.sb("cm", [128, 2, 512], BF16)
    es = P.sb("es", [128, 8], F32)
    P.op("pool", lambda e: e.memset(KAz[:], 0.0), writes=[KAz])
    P.dma("pool", cm[:], G.cmask[:].rearrange("a p n -> p a n"), reads=[G.cmask], writes=[cm])
    P.dma("sp", es[:], G.sink[l:l + 1, :].broadcast_to([128, 8]), reads=[G.sink], writes=[es])
    P.op("act", lambda e: e.activation(es[:], es[:], AF.Exp), reads=[es], writes=[es])
    for a0 in range(0, 34, 2):
        P.dma("pool", VA[:, a0:a0 + 2, :], G.pT[a0 * 128:(a0 + 2) * 128, 0:128].rearrange("(a p) c -> p a c", p=128),
              reads=[G.pT], wdisj=[VA], owner=VA)
    raw = P.sb("raw", [128, 12, 512], F32)
    rp = P.sb("rp", [128, 4, 512], F32)
    tmp = P.sb("tmp", [128, 4, 512], F32)
    for (t0, n, s) in CHUNKS:
        P.dma("sp", raw[:, :, :n], G.pF[0:12 * 128, t0:t0 + n].rearrange("(a p) t -> p a t", p=128),
              reads=[G.pF], writes=[raw])
        if s == 0:
            P.dma("act", rp[:, :, :n], G.rope[:, :, t0:t0 + n].rearrange("a p t -> p a t"), reads=[G.rope], writes=[rp])
            P.op("dve", lambda e: e.tensor_tensor(tmp[:, :, :n], raw[:, 0:4, :n], bc_mid(rp[:, 0, :n], 4), ALU.mult),
                 reads=[raw, rp], writes=[tmp])
            P.op("pool", lambda e: e.tensor_tensor(raw[:, 4:8, :n], raw[:, 4:8, :n], bc_mid(rp[:, 1, :n], 4), ALU.mult),
                 reads=[raw, rp], writes=[raw])
            P.op("dve", lambda e: e.tensor_tensor(QA[:, :, t0:t0 + n], tmp[:, :, :n], raw[:, 4:8, :n], ALU.add),
                 reads=[tmp, raw], wdisj=[QA])
            P.op("dve", lambda e: e.tensor_tensor(tmp[:, 0:2, :n], raw[:, 8:10, :n], bc_mid(rp[:, 2, :n], 2), ALU.mult),
                 reads=[raw, rp], writes=[tmp])
            P.op("pool", lambda e: e.tensor_tensor(raw[:, 10:12, :n], raw[:, 10:12, :n], bc_mid(rp[:, 3, :n], 2), ALU.mult),
                 reads=[raw, rp], writes=[raw])
            for hf in range(2):
                P.op("dve", lambda e: e.tensor_tensor(KAz[hf * 64:(hf + 1) * 64, :, hf, t0:t0 + n],
                                                      tmp[hf * 64:(hf + 1) * 64, 0:2, :n],
                                                      raw[hf * 64:(hf + 1) * 64, 10:12, :n], ALU.add),
                     reads=[tmp, raw], wdisj=[KAz])
        else:
            P.op("act", lambda e: e.activation(QA[:, :, t0:t0 + n], raw[:, 0:4, :n], AF.Copy, scale=0.125),
                 reads=[raw], wdisj=[QA])
            for hf in range(2):
                P.op("dve", lambda e: e.tensor_copy(KAz[hf * 64:(hf + 1) * 64, :, hf, t0:t0 + n],
                                                    raw[hf * 64:(hf + 1) * 64, 8:10, :n]), reads=[raw], wdisj=[KAz])
    PT = [P.sb("PT%d" % i, [128, 512], BF16) for i in range(3)]
    ptc = [0]
    yst = [P.sb("yst%d" % i, [128, 4, 512], BF16) for i in range(2)]
    dn = P.sb("dn", [128, 4, 128], F32)
    yTa = G.yT[0:512, :].rearrange("(h d) t -> d h t", d=64)
    nblocks = 32 if last else 34
    for g in range(2):
        for n in range(nblocks):
            kts = []
            if n < 32:
                if n > 0:
                    kts.append((n - 1, 0))
                kts.append((n, None))
                if n < 31:
                    kts.append((n + 1, 1))
            kts += [(32, None), (33, None)]
            num = G.PB[6]
            den = G.PB[7]

            def S_mm(ps, kt, g=g, n=n):
                kti, mk = kt
                first = True
                if mk is not None:
                    P.mm(ps[:], G.identb[:], cm[:, mk, :], True, False, [G.identb, cm], [ps])
                    first = False
                for hh in range(4):
                    h = 4 * g + hh
                    j, hf = h // 2, h % 2
                    P.mm(ps[:, hh * 128:(hh + 1) * 128], KAz[:, g, hf, kti * 128:(kti + 1) * 128],
                         QA[:, j, n * 128:(n + 1) * 128], first, hh == 3, [KAz, QA], [ps])
                    first = False

            def pv_mm(pt, kt, first, lastk, g=g, num=num, den=den):
                kti, mk = kt
                P.mm(num[:], VA[:, kti, :], pt[:], first, lastk, [VA, pt], [num])
                P.mm(den[:], G.ones[:], pt[:], first, lastk, [G.ones, pt], [den])
            attn_core(G, S_mm, kts, pv_mm, 3, PT, ptc)
            ys = yst[(n // 4) % 2]
            co = (n % 4) * 128
            P.op("dve", lambda e: e.tensor_tensor(
                dn[:], den[:].rearrange("p (h q) -> p h q", h=4), bc_last(es[:, 4 * g:4 * g + 4], 128), ALU.add),
                reads=[den, es], writes=[dn])
            P.op("dve", lambda e: e.reciprocal(dn[:], dn[:]), reads=[dn], writes=[dn])
            P.op("dve", lambda e: e.tensor_tensor(
                ys[:, :, co:co + 128], num[:].rearrange("p (h q) -> p h q", h=4), dn[:], ALU.mult),
                reads=[num, dn], wdisj=[ys])
            if n % 4 == 3 or n == nblocks - 1:
                t0 = (n // 4) * 512
                nn = (n % 4 + 1) * 128
                P.dma("sp", yTa[:, 4 * g:4 * g + 4, t0:t0 + nn], ys[g * 64:(g + 1) * 64, :, :nn],
                      reads=[ys], wdisj=[G.yT], owner=ys)
    P.scope_end()


def phase_attn_c(G, l, last):
    P = G.P
    P.scope_begin()
    QC = P.sb("QC", [128, 4, TT], BF16)
    KCz = P.sb("KCz", [128, 4, 2, TT], BF16)
    VC = P.sb("VC", [128, 34, 512], BF16)
    P.op("pool", lambda e: e.memset(KCz[:], 0.0), writes=[KCz])
    for a0 in range(0, 34, 2):
        P.dma("pool", VC[:, a0:a0 + 2, :], G.pT[a0 * 128:(a0 + 2) * 128, 128:640].rearrange("(a p) c -> p a c", p=128),
              reads=[G.pT], wdisj=[VC], owner=VC)
    raw = P.sb("raw", [128, 8, 512], F32)
    for (t0, n, s) in CHUNKS:
        P.dma("sp", raw[:, :, :n], G.pF[24 * 128:32 * 128, t0:t0 + n].rearrange("(a p) t -> p a t", p=128),
              reads=[G.pF], writes=[raw])
        P.op("act", lambda e: e.activation(QC[:, :, t0:t0 + n], raw[:, 0:4, :n], AF.Copy, scale=0.125),
             reads=[raw], wdisj=[QC])
        for hf in range(2):
            P.op("dve" if hf == 0 else "pool", lambda e: e.tensor_copy(
                KCz[hf * 64:(hf + 1) * 64, :, hf, t0:t0 + n], raw[hf * 64:(hf + 1) * 64, 4:8, :n]),
                reads=[raw], wdisj=[KCz])
    PT = [P.sb("PT%d" % i, [128, 512], BF16) for i in range(3)]
    BT = [P.sb("BT%d" % i, [128, 512], BF16) for i in range(3)]
    ptc = [0]
    btc = [0]
    yst = [P.sb("yst%d" % i, [128, 4, 512], BF16) for i in range(2)]
    dn = P.sb("dn", [128, 512], F32)
    yTc = G.yT[1024:1536, :].rearrange("(h d) t -> d h t", d=64)
    nblocks = 32 if last else 34
    for pg in range(2):
        for n in range(nblocks):
            kts = []
            if n < 32:
                R2 = n
                if R2 < 2:
                    ks = range(0, 4)
                elif R2 > 29:
                    ks = range(28, 32)
                else:
                    ks = range(R2 - 2, R2 + 3)
                kts += [(K2, _na_case_id(R2, K2)) for K2 in ks]
            kts += [(32, None), (33, None)]
            num = G.PB[6]
            den = G.PB[7]

            def S_mm(ps, kt, pg=pg, n=n):
                kti, case = kt
                first = True
                if case is not None:
                    bt = BT[btc[0] % 3]
                    btc[0] += 1
                    P.dma("pool", bt[:], G.nbt[l, case, pg], reads=[G.nbt], writes=[bt])
                    P.mm(ps[:], G.identb[:], bt[:], True, False, [G.identb, bt], [ps])
                    first = False
                for hh in range(4):
                    h = 4 * pg + hh
                    j, hf = h // 2, h % 2
                    P.mm(ps[:, hh * 128:(hh + 1) * 128], KCz[:, j, hf, kti * 128:(kti + 1) * 128],
                         QC[:, j, n * 128:(n + 1) * 128], first, hh == 3, [KCz, QC], [ps])
                    first = False

            def pv_mm(pt, kt, first, lastk, pg=pg, num=num, den=den):
                kti, case = kt
                for hh in range(4):
                    h = 4 * pg + hh
                    j = h // 2
                    P.mm(num[:, hh * 128:(hh + 1) * 128], VC[:, kti, j * 128:(j + 1) * 128], pt[:, hh * 128:(hh + 1) * 128],
                         first and hh == 0, lastk and hh == 3, [VC, pt], [num])
                P.mm(den[:], G.ones[:], pt[:], first, lastk, [G.ones, pt], [den])
            attn_core(G, S_mm, kts, pv_mm, 3, PT, ptc)
            ys = yst[(n // 4) % 2]
            co = (n % 4) * 128
            P.op("dve", lambda e: e.reciprocal(dn[:], den[:]), reads=[den], writes=[dn])
            P.op("dve", lambda e: e.tensor_tensor(
                ys[:, :, co:co + 128], num[:].rearrange("p (h q) -> p h q", h=4),
                dn[:].rearrange("p (h q) -> p h q", h=4), ALU.mult),
                reads=[num, dn], wdisj=[ys])
            if n % 4 == 3 or n == nblocks - 1:
                t0 = (n // 4) * 512
                nn = (n % 4 + 1) * 128
                for hh in range(4):
                    hf = hh % 2
                    P.dma("sp" if hh < 2 else "act", yTc[:, 4 * pg + hh, t0:t0 + nn], ys[hf * 64:(hf + 1) * 64, hh, :nn],
                          reads=[ys], wdisj=[G.yT], owner=ys)
    P.scope_end()


def _zero_rows(G, r0, r1):
    P = G.P
    P.scope_begin()
    z = P.sb("zrow", [128, 2176], BF16)
    P.op("pool", lambda e: e.memset(z[:], 0.0), writes=[z])
    for r in range(r0, r1, 128):
        for c in range(0, TT, 2176):
            P.dma("sp", G.yT[r:r + 128, c:c + 2176], z[:], reads=[z], wdisj=[G.yT], owner=z)
    P.scope_end()


def phase_ssd(G, l, last):
    P = G.P
    PB = G.PB
    P.scope_begin()
    cw = P.sb("cw", [128, 8, 3], F32)
    cb = P.sb("cb", [128, 8], F32)
    P.dma("sp", cw[:], G.scw[l], reads=[G.scw], writes=[cw])
    P.dma("sp", cb[:], G.scb[l], reads=[G.scb], writes=[cb])
    XSb = P.sb("XSb", [128, 4, TT], BF16)
    BC = P.sb("BC", [128, 4, TT], BF16)
    raw = P.sb("raw", [128, 8, 514], F32)
    acc = P.sb("acc", [128, 8, 512], F32)
    for (t0, n, s) in CHUNKS:
        seq0, seq1 = (0, T) if s == 0 else (T, TT)
        lo = max(t0 - 1, seq0)
        hi = min(t0 + n + 1, seq1)
        if lo == t0 or hi == t0 + n:
            P.op("pool", lambda e: e.memset(raw[:], 0.0), writes=[raw])
        P.dma("sp", raw[:, :, lo - (t0 - 1):hi - (t0 - 1)],
              G.pF[36 * 128:44 * 128, lo:hi].rearrange("(a p) t -> p a t", p=128), reads=[G.pF], writes=[raw])
        for ti in range(8):
            eng = "dve"
            P.op(eng, lambda e: e.tensor_scalar(acc[:, ti, :n], raw[:, ti, 1:n + 1], cw[:, ti, 1:2], cb[:, ti:ti + 1],
                                                ALU.mult, ALU.add), reads=[raw, cw, cb], wdisj=[acc])
            P.op(eng, lambda e: e.scalar_tensor_tensor(acc[:, ti, :n], raw[:, ti, 0:n], cw[:, ti, 0:1], acc[:, ti, :n],
                                                       ALU.mult, ALU.add), reads=[raw, cw, acc], wdisj=[acc])
            P.op(eng, lambda e: e.scalar_tensor_tensor(acc[:, ti, :n], raw[:, ti, 2:n + 2], cw[:, ti, 2:3], acc[:, ti, :n],
                                                       ALU.mult, ALU.add), reads=[raw, cw, acc], wdisj=[acc])
        P.op("act", lambda e: e.activation(acc[:, :, :n], acc[:, :, :n], AF.Silu), reads=[acc], writes=[acc])
        P.dma("sp", G.sxs[:, t0:t0 + n].rearrange("(a p) t -> p a t", p=128), acc[:, 0:4, :n], reads=[acc], wdisj=[G.sxs], owner=acc)
        P.op("pool", lambda e: e.tensor_copy(XSb[:, :, t0:t0 + n], acc[:, 0:4, :n]), reads=[acc], wdisj=[XSb])
        P.op("dve", lambda e: e.tensor_copy(BC[:, :, t0:t0 + n], acc[:, 4:8, :n]), reads=[acc], wdisj=[BC])
    import os
    sstop = int(os.environ.get("SSTOP", "99"))
    ssub = int(os.environ.get("SSUB", "99"))
    sson = os.environ.get("SSONLY", "abcd")
    if sstop < 2:
        P.scope_end()
        return
    dtr = P.sb("dtr", [128, 34, 16], F32)
    dtv = P.sb("dtv", [128, 34, 16], F32)
    adt = P.sb("adt", [128, 34, 16], F32)
    dtb = P.sb("dtb", [128, 16], F32)
    aa = P.sb("aa", [128, 16], F32)
    P.dma("sp", dtr[:], G.pT[:, 640:656].rearrange("(a p) c -> p a c", p=128), reads=[G.pT], writes=[dtr])
    P.dma("sp", dtb[:], G.sdtb[l:l + 1, :].broadcast_to([128, 16]), reads=[G.sdtb], writes=[dtb])
    P.dma("sp", aa[:], G.salog[l:l + 1, :].broadcast_to([128, 16]), reads=[G.salog], writes=[aa])
    P.op("act", lambda e: e.activation(aa[:], aa[:], AF.Exp), reads=[aa], writes=[aa])
    P.op("dve", lambda e: e.tensor_tensor(dtr[:], dtr[:], bc_mid(dtb[:], 34), ALU.add), reads=[dtr, dtb], writes=[dtr])
    P.op("act", lambda e: e.activation(dtv[:], dtr[:], AF.Exp), reads=[dtr], writes=[dtv])
    P.op("dve", lambda e: e.tensor_scalar(dtv[:], dtv[:], 1.0, None, ALU.add), reads=[dtv], writes=[dtv])
    P.op("act", lambda e: e.activation(dtv[:], dtv[:], AF.Ln), reads=[dtv], writes=[dtv])
    P.op("dve", lambda e: e.tensor_tensor(adt[:], dtv[:], bc_mid(aa[:], 34), ALU.mult), reads=[dtv, aa], writes=[adt])
    P.op("dve", lambda e: e.tensor_scalar(adt[:], adt[:], -1.0, None, ALU.mult), reads=[adt], writes=[adt])
    adth = P.sb("adth", [128, 34, 16], BF16)
    adtl = P.sb("adtl", [128, 34, 16], BF16)
    adthf = P.sb("adthf", [128, 34, 16], F32)
    P.op("dve", lambda e: e.tensor_copy(adth[:], adt[:]), reads=[adt], writes=[adth])
    P.op("dve", lambda e: e.tensor_copy(adthf[:], adth[:]), reads=[adth], writes=[adthf])
    P.op("dve", lambda e: e.tensor_tensor(adthf[:], adt[:], adthf[:], ALU.subtract), reads=[adt, adthf], writes=[adthf])
    P.op("dve", lambda e: e.tensor_copy(adtl[:], adthf[:]), reads=[adthf], writes=[adtl])
    if sstop < 3:
        P.scope_end()
        return
    Xtm = P.sb("Xtm", [128, 34, 512], BF16)
    Btm = P.sb("Btm", [128, 34, 256], BF16)
    for ti in range(34):
        c0 = ti * 128
        ps = PB[7] if ti % 2 == 0 else PB[6]
        for j in range(4):
            P.mm(ps[:, j * 128:(j + 1) * 128], XSb[:, j, c0:c0 + 128], G.identb[:], j == 0, j == 3, [XSb, G.identb], [ps])
        P.op("act", lambda e: e.activation(Xtm[:, ti, :], ps[:], AF.Copy), reads=[ps], wdisj=[Xtm])
        ps2 = PB[5] if ti % 2 == 0 else PB[4]
        for j in range(2):
            P.mm(ps2[:, j * 128:(j + 1) * 128], BC[:, j, c0:c0 + 128], G.identb[:], j == 0, j == 1, [BC, G.identb], [ps2])
        P.op("dve", lambda e: e.tensor_copy(Btm[:, ti, :], ps2[:, 0:256]), reads=[ps2], wdisj=[Btm])
    if sstop < 5:
        P.scope_end()
        return
    tri = P.sb("tri", [128, 2, 128], BF16)
    tmk = P.sb("tmk", [128, 2, 128], F32)
    P.dma("pool", tri[:], G.tri[:].rearrange("a p n -> p a n"), reads=[G.tri], writes=[tri])
    P.dma("sp", tmk[:], G.trimask[:].rearrange("a p n -> p a n"), reads=[G.trimask], writes=[tmk])
    ST = P.sb("ST", [128, 8, 64], F32)
    STb = P.sb("STb", [128, 8, 64], BF16)
    AdtBh = P.sb("AdtBh", [128, 8, 128], BF16)
    AdtBl = P.sb("AdtBl", [128, 8, 128], BF16)
    css = P.sb("css", [128, 8], F32)
    bl = P.sb("bl", [128, 8], F32)
    dsv = P.sb("dsv", [128, 8], F32)
    ed = P.sb("ed", [128, 8], F32)
    EB = P.sb("EB", [128, 8, 128], F32)
    bcs = P.sb("bcs", [128, 8, 128], F32)
    Gs = P.sb("Gs", [128, 2, 128], F32)
    arg = P.sb("arg", [128, 8, 128], F32)
    Ce = P.sb("Ce", [128, 8, 128], BF16)
    Mt = P.sb("Mt", [128, 8, 128], BF16)
    Xd = P.sb("Xd", [128, 8, 64], BF16)
    Xs = P.sb("Xs", [128, 8, 64], BF16)
    yst = [P.sb("ysd%d" % i, [128, 8, 128], F32) for i in range(2)]
    it = 0
    for d in range(2):
        P.op("dve", lambda e: e.memset(ST[:], 0.0), writes=[ST])
        P.op("pool", lambda e: e.memset(STb[:], 0.0), writes=[STb])
        order = ([32, 33] + list(range(32))) if d == 0 else ([33, 32] + list(range(31, -1, -1)))
        lastc = 127 if d == 0 else 0
        if sstop < 6:
            order = order[:3]
        for ti in order:
            c0 = ti * 128
            need_y = (ti < 32) or (not last)
            if "a" in sson:
                P.op("dve", lambda e: e.tensor_copy(AdtBh[:], bc_last(adth[:, ti, d * 8:(d + 1) * 8], 128)), reads=[adth], writes=[AdtBh])
                P.op("dve", lambda e: e.tensor_copy(AdtBl[:], bc_last(adtl[:, ti, d * 8:(d + 1) * 8], 128)), reads=[adtl], writes=[AdtBl])
            if "b" in sson:
                P.mm(PB[0][:, 0:8], tri[:, d, :], adth[:, ti, d * 8:(d + 1) * 8], True, False, [tri, adth], [PB[0]])
                P.mm(PB[0][:, 0:8], tri[:, d, :], adtl[:, ti, d * 8:(d + 1) * 8], False, True, [tri, adtl], [PB[0]])
            for h in range(8):
                if "c" not in sson:
                    break
                bk = PB[1 + h // 4]
                P.mm(bk[:, (h % 4) * 128:(h % 4 + 1) * 128], AdtBh[:, h, :], tri[:, d, :], h % 4 == 0, False, [AdtBh, tri], [bk])
                P.mm(bk[:, (h % 4) * 128:(h % 4 + 1) * 128], AdtBl[:, h, :], tri[:, d, :], False, h % 4 == 3, [AdtBl, tri], [bk])
            for g in range(2):
                if "d" not in sson:
                    break
                P.mm(PB[3][:, g * 128:(g + 1) * 128], BC[:, g, c0:c0 + 128], BC[:, 2 + g, c0:c0 + 128], g == 0, g == 1, [BC], [PB[3]])
            if ssub < 1:
                continue
            P.op("act", lambda e: e.activation(css[:], PB[0][:, 0:8], AF.Copy), reads=[PB[0]], writes=[css])
            for b in range(2):
                bk = PB[1 + b]
                if b == 0:
                    P.op("act", lambda e: e.activation(bcs[:, 0:4, :].rearrange("p h l -> p (h l)"), bk[:], AF.Copy),
                         reads=[bk], wdisj=[bcs])
                else:
                    P.op("dve", lambda e: e.tensor_copy(bcs[:, 4:8, :].rearrange("p h l -> p (h l)"), bk[:]),
                         reads=[bk], wdisj=[bcs])
            P.op("dve", lambda e: e.tensor_copy(bl[:], bcs[:, :, lastc]), reads=[bcs], writes=[bl])
            P.op("act", lambda e: e.activation(EB[:], bcs[:], AF.Exp), reads=[bcs], writes=[EB])
            P.op("dve", lambda e: e.tensor_tensor(arg[:], bcs[:], bc_last(css[:], 128), ALU.subtract),
                 reads=[bcs, css], writes=[arg])
            if ssub < 2:
                continue
            P.op("dve", lambda e: e.tensor_tensor(arg[:], arg[:], bc_mid(tmk[:, d, :], 8), ALU.add), reads=[arg, tmk], writes=[arg])
            P.op("act", lambda e: e.activation(arg[:], arg[:], AF.Exp), reads=[arg], writes=[arg])
            for g in range(2):
                P.op("dve", lambda e: e.tensor_tensor(Ce[:, 4 * g:4 * g + 4, :], EB[:, 4 * g:4 * g + 4, :],
                                                       bc_mid(BC[:, 2 + g, c0:c0 + 128], 4), ALU.mult),
                     reads=[EB, BC], wdisj=[Ce])
                if g == 0:
                    P.op("act", lambda e: e.activation(Gs[:].rearrange("p g l -> p (g l)"), PB[3][:, 0:256], AF.Copy),
                         reads=[PB[3]], writes=[Gs])
                P.op("dve", lambda e: e.tensor_tensor(Mt[:, 4 * g:4 * g + 4, :], arg[:, 4 * g:4 * g + 4, :],
                                                      bc_mid(Gs[:, g, :], 4), ALU.mult),
                     reads=[arg, Gs], wdisj=[Mt])
            if ssub < 3:
                continue
            P.op("dve", lambda e: e.tensor_tensor(Xd[:], Xtm[:, ti, :].rearrange("p (h q) -> p h q", h=8),
                                                   bc_last(dtv[:, ti, d * 8:(d + 1) * 8], 64), ALU.mult),
                 reads=[Xtm, dtv], writes=[Xd])
            P.op("dve", lambda e: e.tensor_tensor(dsv[:], bl[:], css[:], ALU.subtract), reads=[bl, css], writes=[dsv])
            P.op("act", lambda e: e.activation(dsv[:], dsv[:], AF.Exp), reads=[dsv], writes=[dsv])
            P.op("act", lambda e: e.activation(ed[:], bl[:], AF.Exp), reads=[bl], writes=[ed])
            P.op("dve", lambda e: e.tensor_tensor(Xs[:], Xd[:], bc_last(dsv[:], 64), ALU.mult), reads=[Xd, dsv], writes=[Xs])
            if ssub < 4:
                continue
            if need_y:
                for h in range(8):
                    bk = PB[4 + h // 4]
                    cs_ = slice((h % 4) * 128, (h % 4 + 1) * 128)
                    pr = h // 2
                    P.mm(bk[:, cs_], Xd[:, 2 * pr:2 * pr + 2, :].rearrange("p a q -> p (a q)"), Mt[:, h, :],
                         h % 4 == 0, False, [Xd, Mt], [bk])
                    P.mm(bk[:, cs_], STb[:, 2 * pr:2 * pr + 2, :].rearrange("p a q -> p (a q)"), Ce[:, h, :],
                         False, True, [STb, Ce], [bk])
                ys = yst[it % 2]
                it += 1
                for b in range(2):
                    bk = PB[4 + b]
                    P.op("act", lambda e: e.activation(ys[:, 4 * b:4 * b + 4, :], bk[:].rearrange("p (h l) -> p h l", h=4), AF.Copy),
                         reads=[bk], wdisj=[ys])
                ysv = ys[:].rearrange("p (q two) l -> p q two l", two=2)
                dst = G.ysd[d].rearrange("(q two c) t -> two c q t", two=2, c=64)
                for par in range(2):
                    P.dma("sp" if par == 0 else "act", dst[par, :, :, c0:c0 + 128], ysv[par * 64:(par + 1) * 64, :, par, :],
                          reads=[ys], wdisj=[G.ysd], owner=ys)
            if ssub < 5:
                continue
            for g in range(2):
                P.mm(PB[6][:, g * 256:(g + 1) * 256], Btm[:, ti, g * 128:(g + 1) * 128],
                     Xs[:, 4 * g:4 * g + 4, :].rearrange("p a q -> p (a q)"), g == 0, g == 1, [Btm, Xs], [PB[6]])
            P.op("dve", lambda e: e.tensor_tensor(ST[:], ST[:], bc_last(ed[:], 64), ALU.mult), reads=[ST, ed], writes=[ST])
            P.op("dve", lambda e: e.tensor_tensor(ST[:].rearrange("p h q -> p (h q)"), PB[6][:], ST[:].rearrange("p h q -> p (h q)"), ALU.add),
                 reads=[ST, PB[6]], writes=[ST])
            P.op("act", lambda e: e.activation(STb[:], ST[:], AF.Copy), reads=[ST], writes=[STb])
    P.scope_end()
    if sstop < 7:
        return
    P.scope_begin()
    sdv = P.sb("sdv", [128, 4], F32)
    snw = P.sb("snw", [128, 4], F32)
    P.dma("sp", sdv[:], G.sd[l], reads=[G.sd], writes=[sdv])
    P.dma("sp", snw[:], G.snw[l], reads=[G.snw], writes=[snw])
    yf = P.sb("yf", [128, 4, 512], F32)
    yb = P.sb("yb", [128, 4, 512], F32)
    xsl = P.sb("xsl", [128, 4, 512], F32)
    zz = P.sb("zz", [128, 4, 512], F32)
    sqv = P.sb("sqv", [128, 4, 512], BF16)
    rs = P.sb("rs", [128, 2, 512], F32)
    yo = P.sb("yo", [128, 4, 512], BF16)
    for (t0, n, s) in CHUNKS:
        if last and s == 1:
            continue
        P.dma("sp", yf[:, :, :n], G.ysd[0, :, t0:t0 + n].rearrange("(a p) t -> p a t", p=128), reads=[G.ysd], writes=[yf])
        P.dma("act", yb[:, :, :n], G.ysd[1, :, t0:t0 + n].rearrange("(a p) t -> p a t", p=128), reads=[G.ysd], writes=[yb])
        P.dma("sp", xsl[:, :, :n], G.sxs[:, t0:t0 + n].rearrange("(a p) t -> p a t", p=128), reads=[G.sxs], writes=[xsl])
        P.dma("act", zz[:, :, :n], G.pF[32 * 128:36 * 128, t0:t0 + n].rearrange("(a p) t -> p a t", p=128), reads=[G.pF], writes=[zz])
        P.op("dve", lambda e: e.tensor_tensor(yf[:, :, :n], yf[:, :, :n], yb[:, :, :n], ALU.add), reads=[yf, yb], writes=[yf])
        P.op("pool", lambda e: e.tensor_tensor(xsl[:, :, :n], xsl[:, :, :n], bc_last(sdv[:], n), ALU.mult), reads=[xsl, sdv], writes=[xsl])
        P.op("dve", lambda e: e.tensor_tensor(yf[:, :, :n], yf[:, :, :n], xsl[:, :, :n], ALU.add), reads=[yf, xsl], writes=[yf])
        P.op("act", lambda e: e.activation(zz[:, :, :n], zz[:, :, :n], AF.Silu), reads=[zz], writes=[zz])
        P.op("dve", lambda e: e.tensor_tensor(yf[:, :, :n], yf[:, :, :n], zz[:, :, :n], ALU.mult), reads=[yf, zz], writes=[yf])
        P.op("act", lambda e: e.activation(sqv[:, :, :n], yf[:, :, :n], AF.Square), reads=[yf], writes=[sqv])
        for g in range(2):
            ps = G.pb()
            for j in range(2):
                P.mm(ps[:, :n], G.ones[:], sqv[:, 2 * g + j, :n], j == 0, j == 1, [G.ones, sqv], [ps])
            P.op("dve", lambda e: e.tensor_scalar(rs[:, g, :n], ps[:, :n], 1.0 / 256, 1e-6, ALU.mult, ALU.add), reads=[ps], wdisj=[rs])
        P.op("act", lambda e: e.activation(rs[:, :, :n], rs[:, :, :n], AF.Sqrt), reads=[rs], writes=[rs])
        P.op("dve", lambda e: e.reciprocal(rs[:, :, :n], rs[:, :, :n]), reads=[rs], writes=[rs])
        for j in range(4):
            P.op("dve", lambda e: e.scalar_tensor_tensor(
                yo[:, j, :n], yf[:, j, :n], snw[:, j:j + 1], rs[:, j // 2, :n], ALU.mult, ALU.mult),
                reads=[yf, snw, rs], wdisj=[yo])
        P.dma("sp", G.yT[1536:2048, t0:t0 + n].rearrange("(a p) t -> p a t", p=128), yo[:, :, :n], reads=[yo], wdisj=[G.yT], owner=yo)
    P.scope_end()


def phase_hyena(G, l, last):
    P = G.P
    P.scope_begin()
    TWO_PI = 2.0 * math.pi
    zt = P.sb("zt", [128, TT], F32)
    P.dma("sp", zt[:], G.hz[:], reads=[G.hz], writes=[zt])
    hbv = P.sb("hbv", [128, 4], F32)
    P.dma("sp", hbv[:], G.hb[l], reads=[G.hb], writes=[hbv])
    wm = P.sb("wm", [128, 3, 128], F32)
    P.dma("sp", wm[:], G.hw123[l].rearrange("a p n -> p a n"), reads=[G.hw123], writes=[wm])
    hA = P.sb("hA", [128, TT], F32)
    hB = P.sb("hB", [128, TT], F32)
    ttm = P.sb("ttm", [128, 512], F32)
    src = zt
    for li in range(3):
        dst = hA if li % 2 == 0 else hB
        for (t0, n, s) in CHUNKS:
            ps = G.pb()
            P.mm(ps[:, :n], wm[:, li, :], src[:, t0:t0 + n], True, True, [wm, src], [ps])
            P.op("dve", lambda e: e.tensor_scalar(dst[:, t0:t0 + n], ps[:, :n], hbv[:, li:li + 1], hbv[:, 3:4], ALU.add, ALU.mult),
                 reads=[ps, hbv], wdisj=[dst])
            P.op("act", lambda e: e.activation(dst[:, t0:t0 + n], dst[:, t0:t0 + n], AF.Sin, scale=1.0 / 9.0), reads=[dst], wdisj=[dst])
            for _ in range(2):
                P.op("dve", lambda e: e.tensor_tensor(ttm[:, :n], dst[:, t0:t0 + n], dst[:, t0:t0 + n], ALU.mult), reads=[dst], writes=[ttm])
                P.op("dve", lambda e: e.tensor_scalar(ttm[:, :n], ttm[:, :n], -4.0, 3.0, ALU.mult, ALU.add), reads=[ttm], writes=[ttm])
                P.op("dve", lambda e: e.tensor_tensor(dst[:, t0:t0 + n], dst[:, t0:t0 + n], ttm[:, :n], ALU.mult), reads=[dst, ttm], wdisj=[dst])
        src = dst
    h3 = src
    w4 = P.sb("w4", [128, 2048], F32)
    P.dma("sp", w4[:], G.hw4p[l], reads=[G.hw4p], writes=[w4])
    sw = P.sb("sw", [128, 12, 3], F32)
    sbv = P.sb("sbv", [128, 12], F32)
    skp = P.sb("skp", [128, 2, 4], F32)
    P.dma("sp", sw[:], G.hsw[l], reads=[G.hsw], writes=[sw])
    P.dma("sp", sbv[:], G.hsb[l], reads=[G.hsb], writes=[sbv])
    P.dma("sp", skp[:], G.hskip[l], reads=[G.hskip], writes=[skp])
    pin = P.sb("pin", [128, T + 2], F32)
    ux = [P.sb("ux%d" % i, [128, T], F32) for i in range(3)]
    kern = P.sb("kern", [128, 2, T], F32)
    dec = P.sb("dec", [128, T], F32)
    acc = P.sb("acc", [128, T], F32)
    k0 = P.sb("k0", [128, 1], F32)
    yo = P.sb("yo", [128, T], BF16)
    seqs = [(0, T)] if last else [(0, T), (T, TC)]
    for ct in range(4):
        for (q0, L) in seqs:
            P.dma("sp", dec[:, :L], G.hdec[ct * 128:(ct + 1) * 128, q0:q0 + L], reads=[G.hdec], writes=[dec])
            for i in range(3):
                tile = 12 + 4 * i + ct
                wi = 4 * i + ct
                P.op("pool", lambda e: e.memset(pin[:, 0:L + 2], 0.0), writes=[pin])
                P.dma("sp", pin[:, 1:L + 1], G.pF[tile * 128:(tile + 1) * 128, q0:q0 + L], reads=[G.pF], writes=[pin])
                u = ux[i]
                P.op("dve", lambda e: e.tensor_scalar(u[:, :L], pin[:, 1:L + 1], sw[:, wi, 1:2], sbv[:, wi:wi + 1], ALU.mult, ALU.add),
                     reads=[pin, sw, sbv], writes=[u])
                P.op("dve", lambda e: e.scalar_tensor_tensor(u[:, :L], pin[:, 0:L], sw[:, wi, 0:1], u[:, :L], ALU.mult, ALU.add),
                     reads=[pin, sw, u], writes=[u])
                P.op("dve", lambda e: e.scalar_tensor_tensor(u[:, :L], pin[:, 2:L + 2], sw[:, wi, 2:3], u[:, :L], ALU.mult, ALU.add),
                     reads=[pin, sw, u], writes=[u])
            cur = ux[2]
            for o in range(2):
                for dr in range(2):
                    c0 = o * 1024 + dr * 512 + ct * 128
                    for a in range(0, L, 512):
                        n = min(512, L - a)
                        ps = G.pb()
                        P.mm(ps[:, :n], w4[:, c0:c0 + 128], h3[:, q0 + a:q0 + a + n], True, True, [w4, h3], [ps])
                        P.op("dve", lambda e: e.tensor_tensor(kern[:, dr, a:a + n], ps[:, :n], dec[:, a:a + n], ALU.mult),
                             reads=[ps, dec], wdisj=[kern])
                P.op("dve", lambda e: e.tensor_tensor(k0[:], kern[:, 0, 0:1], skp[:, o, ct:ct + 1], ALU.add), reads=[kern, skp], writes=[k0])
                P.op("dve", lambda e: e.tensor_scalar(acc[:, :L], cur[:, :L], k0[:, 0:1], None, ALU.mult), reads=[cur, k0], writes=[acc])
                for tau in range(1, L):
                    P.op("dve", lambda e: e.scalar_tensor_tensor(acc[:, tau:L], cur[:, 0:L - tau], kern[:, 0, tau:tau + 1], acc[:, tau:L],
                                                                ALU.mult, ALU.add), reads=[cur, kern, acc], writes=[acc])
                    P.op("dve", lambda e: e.scalar_tensor_tensor(acc[:, 0:L - tau], cur[:, tau:L], kern[:, 1, tau:tau + 1], acc[:, 0:L - tau],
                                                                ALU.mult, ALU.add), reads=[cur, kern, acc], writes=[acc])
                if o == 0:
                    P.op("dve", lambda e: e.tensor_tensor(ux[2][:, :L], ux[0][:, :L], acc[:, :L], ALU.mult), reads=[ux[0], acc], writes=[ux[2]])
                else:
                    P.op("dve", lambda e: e.tensor_tensor(yo[:, :L], ux[1][:, :L], acc[:, :L], ALU.mult), reads=[ux[1], acc], writes=[yo])
            P.dma("sp", G.yT[512 + ct * 128:512 + (ct + 1) * 128, q0:q0 + L], yo[:, :L], reads=[yo], wdisj=[G.yT], owner=yo)
    P.scope_end()


_CACHE = {}


def kernel(**inputs):
    inp = {k: np.asarray(v) for k, v in inputs.items()}
    if "nc" not in _CACHE:
        _CACHE["nc"] = build_program()
    nc = _CACHE["nc"]
    sh = _prep_shared(inp)
    in_maps = []
    for c in range(8):
        m = dict(sh)
        m.update(_prep_core(inp, c % 4))
        in_maps.append(m)
    res = run_bass_kernel_spmd(nc, in_maps, core_ids=list(range(8)))
    outs = []
    for b in range(4):
        o = np.asarray(res.results[b]["out"])
        outs.append(o.transpose(2, 0, 1).reshape(T, D))
    return np.stack(outs).astype(np.float32)
```

```python
import math
import numpy as np
import concourse.bass as bass
import concourse.mybir as mybir
from concourse.bass_utils import run_bass_kernel_spmd

F32 = mybir.dt.float32
BF16 = mybir.dt.bfloat16
AF = mybir.ActivationFunctionType
ALU = mybir.AluOpType
AX = mybir.AxisListType


class Buf:
    __slots__ = ("t", "name", "w", "r", "dsem", "dcnt")

    def __init__(self, t, name):
        self.t = t
        self.name = name
        self.w = {}
        self.r = {}
        self.dsem = None
        self.dcnt = 0

    def __getitem__(self, idx):
        return self.t[idx]


class _Rec:
    def __init__(self):
        self.call = None

    def __getattr__(self, name):
        def f(*a, **k):
            self.call = (name, a, k)
            return self
        return f


class Prog:
    ENG = ("pe", "act", "dve", "pool", "sp")

    def __init__(self, nc):
        self.nc = nc
        self.q = {e: [] for e in self.ENG}
        self.cnt = {e: 0 for e in self.ENG}
        self.sem = {}
        self.known = {e: {} for e in self.ENG}
        self.ctx = []
        self.perm = []
        self.nsem = 0
        self.out_tokens = {}
        self.scopes = []
        self.free_dsems = []
        self.all_dsems = []
        self.semcount = {}
        self.scope_bufs = []
        for e in self.ENG:
            self.sem[e] = self._newsem("s_" + e)

    def _newsem(self, name):
        g = self.nc.semaphore(name)
        s = g.__enter__()
        self.perm.append(g)
        self.nsem += 1
        return s

    def sb(self, name, shape, dt):
        self.uid = getattr(self, "uid", 0) + 1
        g = self.nc.sbuf_tensor(name + "_%d" % self.uid, list(shape), dt)
        t = g.__enter__()
        self.ctx.append(g)
        b = Buf(t, name)
        self.scope_bufs.append(b)
        return b

    def _get_dsem(self, owner):
        if owner.dsem is None:
            if self.free_dsems:
                owner.dsem = self.free_dsems.pop()
            else:
                owner.dsem = self._newsem_perm("d%d" % len(self.all_dsems))
                self.all_dsems.append(owner.dsem)
                self.semcount[owner.dsem] = 0
        return owner.dsem

    def _newsem_perm(self, name):
        g = self.nc.semaphore(name)
        s = g.__enter__()
        self.perm.append(g)
        return s

    def scope_begin(self):
        self.scopes.append((len(self.ctx), len(self.scope_bufs)))

    def barrier(self):
        for e in self.ENG:
            waits = []
            kn = self.known[e]
            for e2 in self.ENG:
                if e2 != e and self.cnt[e2] > kn.get(self.sem[e2], 0):
                    kn[self.sem[e2]] = self.cnt[e2]
                    waits.append((self.sem[e2], self.cnt[e2]))
            for s_ in self.all_dsems:
                v = self.semcount[s_]
                if v > kn.get(s_, 0):
                    kn[s_] = v
                    waits.append((s_, v))
            if waits:
                self.q[e].append((waits, None, None, 0))

    def scope_end(self):
        self.barrier()
        n, nb = self.scopes.pop()
        for b in self.scope_bufs[nb:]:
            if b.dsem is not None:
                self.free_dsems.append(b.dsem)
                b.dsem = None
        del self.scope_bufs[nb:]
        while len(self.ctx) > n:
            g = self.ctx.pop()
            g.__exit__(None, None, None)

    def ps(self, name, shape, dt=F32):
        g = self.nc.psum_tensor(name, list(shape), dt)
        t = g.__enter__()
        self.ctx.append(g)
        return Buf(t, name)

    def dram(self, name, shape, dt, kind="Internal"):
        t = self.nc.dram_tensor(name, list(shape), dt, kind=kind)
        return Buf(t.ap(), name)

    def mm(self, out, lhsT, rhs, start, stop, reads, writes):
        return self.op("pe", lambda e: e.matmul(out, lhsT, rhs, start=start, stop=stop, skip_group_check=True),
                       reads=reads, writes=writes)

    def _waits(self, eng, reads, writes, wdisj=()):
        need = {}
        for b in wdisj:
            for s, v in b.r.items():
                if need.get(s, 0) < v:
                    need[s] = v
        for b in reads:
            for s, v in b.w.items():
                if need.get(s, 0) < v:
                    need[s] = v
        for b in writes:
            for s, v in b.w.items():
                if need.get(s, 0) < v:
                    need[s] = v
            for s, v in b.r.items():
                if need.get(s, 0) < v:
                    need[s] = v
        out = []
        kn = self.known[eng]
        for s, v in need.items():
            if eng == "pe" and s is self.sem["pe"]:
                continue
            if kn.get(s, 0) >= v:
                continue
            kn[s] = v
            out.append((s, v))
        return out

    def _record(self, tok, reads, writes, wdisj=()):
        s, v = tok
        for b in wdisj:
            if b.r:
                b.w = {s: v}
                b.r = {}
            elif b.w.get(s, 0) < v:
                b.w[s] = v
        for b in reads:
            if b.r.get(s, 0) < v:
                b.r[s] = v
        for b in writes:
            b.w = {s: v}
            b.r = {}

    def op(self, eng, fn, reads=(), writes=(), wdisj=()):
        rec = _Rec()
        fn(rec)
        name_, a_, k_ = rec.call

        def fn(e, name_=name_, a_=a_, k_=k_):
            return getattr(e, name_)(*a_, **k_)
        waits = self._waits(eng, reads, writes, wdisj)
        self.cnt[eng] += 1
        tok = (self.sem[eng], self.cnt[eng])
        self.q[eng].append((waits, fn, tok[0], 1))
        self._record(tok, reads, writes, wdisj)
        return tok

    def dma(self, eng, out_ap, in_ap, reads=(), writes=(), wdisj=(), owner=None, final=False):
        if owner is None:
            owner = writes[0] if writes else reads[0]
        ds = self._get_dsem(owner)
        waits = self._waits(eng, reads, writes, wdisj)
        kn = self.known[eng]
        cur = self.semcount[ds]
        if cur and kn.get(ds, 0) < cur:
            kn[ds] = cur
            waits.append((ds, cur))
        self.semcount[ds] = cur + 16
        tok = (ds, cur + 16)

        def fn(e, o=out_ap, i=in_ap):
            return e.dma_start(out=o, in_=i)
        self.q[eng].append((waits, fn, tok[0], 16))
        self._record(tok, reads, writes, wdisj)
        if final:
            self.out_tokens[tok[0]] = tok[1]
        return tok

    def simulate_sync(self):
        pos = {e: 0 for e in self.ENG}
        val = {}
        progress = True
        while progress:
            progress = False
            for e in self.ENG:
                q = self.q[e]
                while pos[e] < len(q):
                    waits, fn, s_, inc = q[pos[e]]
                    if any(val.get(id(ws), 0) < wv for ws, wv in waits):
                        break
                    if fn is not None:
                        val[id(s_)] = val.get(id(s_), 0) + inc
                    pos[e] += 1
                    progress = True
        stuck = {e: (pos[e], len(self.q[e])) for e in self.ENG if pos[e] < len(self.q[e])}
        if not stuck:
            return None
        rep = {}
        for e, (p, n) in stuck.items():
            waits = self.q[e][p][0]
            rep[e] = (p, n, [(getattr(ws, "name", str(ws)), wv, val.get(id(ws), 0)) for ws, wv in waits])
        return rep

    def emit(self):
        nc = self.nc
        fin = list(self.out_tokens.items())
        q = self.q
        with nc.Block() as block:
            def run(e, lst, extra=()):
                for waits, fn, s, inc in lst:
                    for ws, wv in waits:
                        e.wait_ge(ws, wv)
                    if fn is not None:
                        fn(e).then_inc(s, inc)
                for ws, wv in extra:
                    e.wait_ge(ws, wv)

            @block.sync
            def _(e):
                run(e, q["sp"], fin)

            @block.tensor
            def _(e):
                run(e, q["pe"])

            @block.scalar
            def _(e):
                run(e, q["act"])

            @block.vector
            def _(e):
                run(e, q["dve"])

            @block.gpsimd
            def _(e):
                run(e, q["pool"])
        for g in reversed(self.ctx):
            g.__exit__(None, None, None)
        for g in reversed(self.perm):
            g.__exit__(None, None, None)


D = 2048
KC = 16
T = 4096
TC = 256
TT = T + TC
NL = 4
NFM = 44
NTM = 656
CHUNKS = [(i * 512, 512, 0) for i in range(8)] + [(T, TC, 1)]
NEGM = -30000.0


def _fm_cols():
    cols = []
    rp = np.concatenate([np.arange(16, 32), np.arange(0, 16), np.arange(48, 64), np.arange(32, 48)])
    qa = np.arange(512)
    qap = (np.arange(8)[:, None] * 64 + rp[None]).reshape(-1)
    cols += [qa, qap]
    for perm in (False, True):
        for g in range(2):
            base = 512 + g * 64 + (rp if perm else np.arange(64))
            cols.append(np.concatenate([base, base]))
    cols.append(768 + np.arange(1536))
    cols.append(2304 + np.arange(1024))
    cols.append(3840 + np.arange(512))
    cols.append(4352 + np.arange(1024))
    c = np.concatenate(cols)
    assert c.shape[0] == NFM * 128
    return c


def _tm_cols():
    return np.concatenate([640 + np.arange(128), 3328 + np.arange(512), 5376 + np.arange(16)])


def _rope_tables():
    t = np.arange(T)
    row, col = t // 64, t % 64
    inv = 10000.0 ** (-np.arange(16, dtype=np.float64) / 16)
    d = np.arange(64)
    pos = np.where(d[:, None] < 32, row[None], col[None]).astype(np.float64)
    ang = pos * inv[d % 16][:, None]
    cos = np.cos(ang)
    sin = np.sin(ang) * np.where((d % 32) < 16, -1.0, 1.0)[:, None]
    cos2 = np.concatenate([cos, cos], 0)
    sin2 = np.concatenate([sin, sin], 0)
    return np.stack([cos2 * 0.125, sin2 * 0.125, cos2, sin2]).astype(np.float32)


def _na_cases():
    cases = [(10, 10 + dk) for dk in range(-2, 3)]
    for R2 in (0, 1):
        cases += [(R2, K2) for K2 in range(4)]
    for R2 in (30, 31):
        cases += [(R2, K2) for K2 in range(28, 32)]
    return cases


def _na_case_id(R2, K2):
    if 2 <= R2 <= 29:
        return K2 - R2 + 2
    if R2 < 2:
        return 5 + R2 * 4 + K2
    return 13 + (R2 - 30) * 4 + (K2 - 28)


def _na_tables(rpb):
    cases = _na_cases()
    kk = np.arange(128)
    qq = np.arange(128)
    out = np.empty((len(cases), 2, 128, 512), np.float32)
    for ci, (R2, K2) in enumerate(cases):
        kr = 2 * K2 + kk // 64
        ck = kk % 64
        r = 2 * R2 + qq // 64
        cq = qq % 64
        rstart = np.clip(r - 4, 0, 56)
        cstart = np.clip(cq - 8, 0, 48)
        vr = (kr[:, None] >= rstart[None]) & (kr[:, None] < rstart[None] + 8)
        vc = (ck[:, None] >= cstart[None]) & (ck[:, None] < cstart[None] + 16)
        roff = np.clip(kr[:, None] - r[None] + 7, 0, 14)
        coff = np.clip(ck[:, None] - cq[None], -15, 15) + 15
        valid = vr & vc
        for h in range(8):
            bias = rpb[h][roff, coff]
            out[ci, h // 4, :, (h % 4) * 128:(h % 4 + 1) * 128] = np.where(valid, bias, np.float32(NEGM))
    return out


def _hy_tables(L):
    t = np.linspace(0.0, 1.0, L, dtype=np.float32)[:, None]
    f = np.linspace(1e-4, 15, 16, dtype=np.float32)[None]
    wpos = (2.0 * math.pi * np.arange(L, dtype=np.float32)[:, None] / L).astype(np.float32)
    z = np.concatenate([t, np.cos(f * wpos), -np.sin(f * wpos)], axis=-1).astype(np.float32)
    max_decay = math.log(1e-2) / 0.3
    min_decay = math.log(1e-2) / 1.5
    deltas = np.linspace(min_decay, max_decay, 512, dtype=np.float32)
    decay = np.exp(-t * np.abs(deltas)[None]).astype(np.float32)
    return np.ascontiguousarray(z.T), np.ascontiguousarray(decay.T)


def _prep_shared(inp):
    sh = {}
    f32 = np.float32
    sh["adaw"] = np.ascontiguousarray(inp["ada_w"].reshape(NL, 128, 16, 6 * D).transpose(0, 2, 1, 3))
    sh["adab"] = np.ascontiguousarray(inp["ada_b"])
    sh["nrm"] = np.ascontiguousarray(np.stack([inp["norm_mix"], inp["norm_mlp"]], 1).reshape(NL, 2, 128, 16))
    sh["fnorm"] = np.ascontiguousarray(inp["final_norm"].reshape(128, 16))
    w_in = inp["w_in"].reshape(NL, 128, 16, -1)
    fm = w_in[..., _fm_cols()].reshape(NL, 128, 16, NFM, 128)
    sh["win_fm"] = np.ascontiguousarray(fm.transpose(0, 3, 1, 2, 4))
    sh["win_tm"] = np.ascontiguousarray(w_in[..., _tm_cols()])
    wo = inp["w_out"].reshape(NL, 16, 128, 128, 16)
    sh["wout"] = np.ascontiguousarray(wo.transpose(0, 4, 2, 1, 3))
    w1 = inp["mlp_w1"].reshape(NL, 128, 16, 64, 128)
    sh["w1"] = np.ascontiguousarray(w1.transpose(0, 3, 1, 2, 4))
    w2 = inp["mlp_w2"].reshape(NL, 64, 128, 128, 16)
    sh["w2"] = np.ascontiguousarray(w2.transpose(0, 4, 2, 1, 3))
    sh["rope"] = _rope_tables()
    kk = np.arange(128)[:, None]
    qq = np.arange(128)[None]
    mp = np.where(qq <= kk, 0.0, NEGM).astype(f32)
    mn = np.where(kk <= qq, 0.0, NEGM).astype(f32)
    sh["cmask"] = np.stack([np.tile(mp, (1, 4)), np.tile(mn, (1, 4))]).astype(f32)
    sh["sink"] = np.ascontiguousarray(inp["attn_sink"])
    sh["nbt"] = np.stack([_na_tables(inp["na_rpb"][l]) for l in range(NL)])
    sh["scw"] = np.ascontiguousarray(inp["ssm_conv_w"].reshape(NL, 3, 8, 128).transpose(0, 3, 2, 1))
    sh["scb"] = np.ascontiguousarray(inp["ssm_conv_b"].reshape(NL, 8, 128).transpose(0, 2, 1))
    sh["sdtb"] = np.ascontiguousarray(inp["ssm_dt_bias"].reshape(NL, 16))
    sh["salog"] = np.ascontiguousarray(inp["ssm_a_log"].reshape(NL, 16))
    sh["sd"] = np.ascontiguousarray(np.repeat(inp["ssm_d"], 64, axis=1).reshape(NL, 4, 128).transpose(0, 2, 1))
    sh["snw"] = np.ascontiguousarray(inp["ssm_norm"].reshape(NL, 4, 128).transpose(0, 2, 1))
    tri = np.triu(np.ones((128, 128), f32))
    sh["tri"] = np.stack([tri, tri.T]).astype(f32)
    mk = np.where(tri > 0, 0.0, NEGM).astype(f32)
    sh["trimask"] = np.stack([mk, mk.T]).astype(f32)
    sh["ident"] = np.eye(128, dtype=f32)
    sh["hsw"] = np.ascontiguousarray(inp["hy_short_w"].reshape(NL, 3, 12, 128).transpose(0, 3, 2, 1))
    sh["hsb"] = np.ascontiguousarray(inp["hy_short_b"].reshape(NL, 12, 128).transpose(0, 2, 1))
    w123 = np.zeros((NL, 3, 128, 128), f32)
    w123[:, 0, :33, :64] = inp["hy_w1"]
    w123[:, 1, :64, :64] = inp["hy_w2"]
    w123[:, 2, :64, :64] = inp["hy_w3"]
    sh["hw123"] = w123
    w4p = np.zeros((NL, 128, 2048), f32)
    w4p[:, :64, :] = inp["hy_w4"]
    sh["hw4p"] = w4p
    hb = np.zeros((NL, 128, 4), f32)
    hb[:, :64, :] = np.stack([inp["hy_b1"], inp["hy_b2"], inp["hy_b3"], inp["hy_freq"]], 2)
    sh["hb"] = hb
    sh["hskip"] = np.ascontiguousarray(inp["hy_skip"].reshape(NL, 2, 4, 128).transpose(0, 3, 1, 2))
    zl, dl = _hy_tables(T)
    zc, dc = _hy_tables(TC)
    hz = np.zeros((128, TT + T), f32)
    hz[:33] = np.concatenate([zl, zc, zl[:, ::-1]], 1)
    sh["hz"] = hz
    sh["hdecr"] = np.ascontiguousarray(dl[:, ::-1])
    sh["rev"] = np.ascontiguousarray(np.eye(128, dtype=f32)[::-1])
    sh["hdec"] = np.ascontiguousarray(np.concatenate([dl, dc], 1))
    return sh


def _prep_core(inp, b):
    xcat = np.concatenate([inp["x"][b], inp["ctx"][b]], 0)
    xT = np.ascontiguousarray(xcat.reshape(TT, 128, 16).transpose(1, 2, 0))
    cv = np.ascontiguousarray(np.stack([inp["c"][b].reshape(128, 16), inp["c_ctx"].reshape(128, 16)], 2))
    return {"xT": xT, "cv": cv}


def bcast(ap, shape):
    return ap.broadcast_to(list(shape))


class Ctx:
    pass


def build_program(nlayers=NL, dbg=False, stop=99, attn=True):
    nc = bass.Bass("TRN2", target_bir_lowering=False)
    P = Prog(nc)
    G = Ctx()
    G.P = P
    G.dbg = dbg
    di = lambda n, s, dt=F32: P.dram(n, s, dt, kind="ExternalInput")
    G.xT = di("xT", [128, KC, TT])
    G.cv = di("cv", [128, KC, 2])
    G.adaw = di("adaw", [NL, KC, 128, 6 * D])
    G.adab = di("adab", [NL, 6 * D])
    G.nrm = di("nrm", [NL, 2, 128, KC])
    G.fnorm = di("fnorm", [128, KC])
    G.win_fm = di("win_fm", [NL, NFM, 128, KC, 128])
    G.win_tm = di("win_tm", [NL, 128, KC, NTM])
    G.wout = di("wout", [NL, 16, 128, 16, 128])
    G.w1 = di("w1", [NL, 64, 128, KC, 128])
    G.w2 = di("w2", [NL, 16, 128, 64, 128])
    G.rope = di("rope", [4, 128, T])
    G.cmask = di("cmask", [2, 128, 512])
    G.sink = di("sink", [NL, 8])
    G.nbt = di("nbt", [NL, 21, 2, 128, 512])
    G.scw = di("scw", [NL, 128, 8, 3])
    G.scb = di("scb", [NL, 128, 8])
    G.sdtb = di("sdtb", [NL, 16])
    G.salog = di("salog", [NL, 16])
    G.sd = di("sd", [NL, 128, 4])
    G.snw = di("snw", [NL, 128, 4])
    G.tri = di("tri", [2, 128, 128])
    G.trimask = di("trimask", [2, 128, 128])
    G.ident = di("ident", [128, 128])
    G.hsw = di("hsw", [NL, 128, 12, 3])
    G.hsb = di("hsb", [NL, 128, 12])
    G.hw123 = di("hw123", [NL, 3, 128, 128])
    G.hw4p = di("hw4p", [NL, 128, 2048])
    G.hb = di("hb", [NL, 128, 4])
    G.hskip = di("hskip", [NL, 128, 2, 4])
    G.hz = di("hz", [128, TT + T])
    G.hdecr = di("hdecr", [512, T])
    G.rev = di("rev", [128, 128])
    G.kvd = [P.dram("kvd%d" % i, [128, 8320], BF16) for i in range(2)]
    G.hdec = di("hdec", [512, TT])
    G.out = P.dram("out", [128, KC, T], F32, kind="ExternalOutput")
    G.xs = P.dram("xs", [128, KC, TT], F32)
    G.modv = P.dram("modv", [NL, 2, 6 * D], F32)
    G.pF = P.dram("pF", [NFM * 128, TT], F32)
    G.pT = P.dram("pT", [TT, NTM], F32)
    G.yT = P.dram("yT", [2048, TT], BF16)
    G.sxs = P.dram("sxs", [512, TT], F32)
    G.ysd = P.dram("ysd", [2, 512, TT], F32)
    if dbg:
        G.d_pF = P.dram("d_pF", [NFM * 128, TT], F32, kind="ExternalOutput")
        G.d_pT = P.dram("d_pT", [TT, NTM], F32, kind="ExternalOutput")
        G.d_yT = P.dram("d_yT", [2048, TT], BF16, kind="ExternalOutput")
        G.d_xs = P.dram("d_xs", [128, KC, TT], F32, kind="ExternalOutput")
        G.d_mod = P.dram("d_mod", [NL, 2, 6 * D], F32, kind="ExternalOutput")
    G.PB = [P.ps("pb%d" % i, [128, 512]) for i in range(8)]
    G.pbi = 0

    def pb():
        G.pbi = (G.pbi + 1) % 8
        return G.PB[G.pbi]
    G.pb = pb
    G.pb6i = 0

    def pb6():
        G.pb6i = (G.pb6i + 1) % 6
        return G.PB[G.pb6i]
    G.pb6 = pb6
    G.ones = P.sb("ones", [128, 128], BF16)
    P.op("dve", lambda e: e.memset(G.ones[:], 1.0), writes=[G.ones])
    G.identb = P.sb("identb", [128, 128], BF16)
    P.dma("pool", G.identb[:], G.ident[:], reads=[G.ident], writes=[G.identb])
    G.modsb = P.sb("modsb", [128, 2, 6, KC], F32)
    G.amod = P.sb("amod", [128, 2, 2, KC], F32)
    G.nrmsb = P.sb("nrmsb", [128, 2, KC], F32)

    phase_mod(G)
    P.dma("sp", G.xs[:], G.xT[:], reads=[G.xT], writes=[G.xs], owner=G.ones)
    for l in range(nlayers):
        last = (l == NL - 1)
        if stop < 1:
            break
        load_mod(G, l)
        for half in (CHUNKS[0:4], CHUNKS[4:9]):
            phase_inproj(G, l, half)
        if stop < 2:
            break
        if dbg and l == 0:
            P.dma("sp", G.d_pF[:], G.pF[:], reads=[G.pF], writes=[G.d_pF], owner=G.ones, final=True)
            P.dma("sp", G.d_pT[:], G.pT[:], reads=[G.pT], writes=[G.d_pT], owner=G.identb, final=True)
        if attn:
            phase_attn_a(G, l, last)
            if stop < 3:
                break
            phase_attn_c(G, l, last)
            if stop < 4:
                break
        else:
            _zero_rows(G, 0, 512)
            _zero_rows(G, 1024, 1536)
        phase_ssd(G, l, last)
        phase_hyena(G, l, last)
        phase_out_mlp(G, l, last)
    if dbg:
        P.dma("sp", G.d_yT[:], G.yT[:], reads=[G.yT], writes=[G.d_yT], owner=G.ones, final=True)
        P.dma("sp", G.d_xs[:], G.xs[:], reads=[G.xs], writes=[G.d_xs], owner=G.ones, final=True)
        P.dma("sp", G.d_mod[:], G.modv[:], reads=[G.modv], writes=[G.d_mod], owner=G.identb, final=True)
    phase_final(G)
    P.emit()
    return nc


def evac(G, i, out_ap, in_ap, reads, writes, wdisj=()):
    P = G.P
    if i % 2 == 0:
        return P.op("act", lambda e: e.activation(out_ap, in_ap, AF.Copy), reads=reads, writes=writes, wdisj=wdisj)
    return P.op("dve", lambda e: e.tensor_copy(out_ap, in_ap), reads=reads, writes=writes, wdisj=wdisj)


def phase_mod(G):
    P = G.P
    P.scope_begin()
    cvs = P.sb("cvs", [128, KC, 2], F32)
    scb = P.sb("scb", [128, KC, 2], BF16)
    P.dma("sp", cvs[:], G.cv[:], reads=[G.cv], writes=[cvs])
    P.op("act", lambda e: e.activation(scb[:], cvs[:], AF.Silu), reads=[cvs], writes=[scb])
    wb = [P.sb("adw%d" % i, [128, KC, 512], BF16) for i in range(2)]
    bt = [P.sb("adb%d" % i, [2, 512], F32) for i in range(2)]
    mr = [P.sb("mr%d" % i, [2, 512], F32) for i in range(2)]
    it = 0
    for l in range(NL):
        for nch in range(24):
            w = wb[it % 2]
            b_ = bt[it % 2]
            m_ = mr[it % 2]
            P.dma("pool", w[:], G.adaw[l, :, :, nch * 512:(nch + 1) * 512].rearrange("k p n -> p k n"),
                  reads=[G.adaw], writes=[w])
            P.dma("sp", b_[:], G.adab[l:l + 1, nch * 512:(nch + 1) * 512].broadcast_to([2, 512]),
                  reads=[G.adab], writes=[b_])
            ps = G.pb()
            for kc in range(KC):
                P.mm(ps[0:2, :], scb[:, kc, :], w[:, kc, :], kc == 0, kc == KC - 1, [scb, w], [ps])
            P.op("dve", lambda e, m_=m_, ps=ps, b_=b_: e.tensor_tensor(m_[:], ps[0:2, :], b_[:], ALU.add),
                 reads=[ps, b_], writes=[m_])
            P.dma("sp", G.modv[l, :, nch * 512:(nch + 1) * 512], m_[:], reads=[m_], wdisj=[G.modv], owner=m_)
            it += 1
    P.scope_end()


def load_mod(G, l):
    P = G.P
    P.dma("sp", G.modsb[:], G.modv[l].rearrange("s (x p k) -> p s x k", x=6, p=128, k=KC),
          reads=[G.modv], writes=[G.modsb])
    P.dma("sp", G.nrmsb[:], G.nrm[l].rearrange("a p k -> p a k"), reads=[G.nrm], writes=[G.nrmsb])
    for s in range(2):
        for j, x in ((0, 1), (1, 4)):
            P.op("dve", lambda e, s=s, j=j, x=x: e.scalar_tensor_tensor(
                G.amod[:, s, j, :], G.modsb[:, s, x, :], 1.0, G.nrmsb[:, j, :], ALU.add, ALU.mult),
                reads=[G.modsb, G.nrmsb], writes=[G.amod])


def norm_mod(G, xc, n, s, which, hT, hoff, sqb, rstd):
    P = G.P
    P.op("act", lambda e: e.activation(sqb[:, :, :n], xc[:, :, :n], AF.Square), reads=[xc], writes=[sqb])
    ps = G.pb()
    for kc in range(KC):
        P.mm(ps[:, :n], G.ones[:], sqb[:, kc, :n], kc == 0, kc == KC - 1, [G.ones, sqb], [ps])
    P.op("dve", lambda e: e.tensor_scalar(rstd[:, :n], ps[:, :n], 1.0 / D, 1e-6, ALU.mult, ALU.add),
         reads=[ps], writes=[rstd])
    P.op("act", lambda e: e.activation(rstd[:, :n], rstd[:, :n], AF.Sqrt), reads=[rstd], writes=[rstd])
    P.op("dve", lambda e: e.reciprocal(rstd[:, :n], rstd[:, :n]), reads=[rstd], writes=[rstd])
    P.op("dve", lambda e: e.tensor_tensor(xc[:, :, :n], xc[:, :, :n], bcast(rstd[:, None, :n], [128, KC, n]), ALU.mult),
         reads=[xc, rstd], writes=[xc])
    shi = 0 if which == 0 else 3
    for kc in range(KC):
        P.op("act", lambda e, kc=kc: e.activation(hT[:, kc, hoff:hoff + n], xc[:, kc, :n], AF.Identity,
                                                  bias=G.modsb[:, s, shi, kc:kc + 1],
                                                  scale=G.amod[:, s, which, kc:kc + 1]),
             reads=[xc, G.modsb, G.amod], writes=[hT])


def phase_inproj(G, l, chunks):
    P = G.P
    P.scope_begin()
    ntok = sum(c[1] for c in chunks)
    tbase = chunks[0][0]
    hT = P.sb("hT", [128, KC, ntok], BF16)
    xc = P.sb("xc", [128, KC, 512], F32)
    sqb = P.sb("sqb", [128, KC, 512], BF16)
    rstd = P.sb("rstd", [128, 512], F32)
    for (t0, n, s) in chunks:
        P.dma("sp", xc[:, :, :n], G.xs[:, :, t0:t0 + n], reads=[G.xs], writes=[xc])
        norm_mod(G, xc, n, s, 0, hT, t0 - tbase, sqb, rstd)
    wb = [P.sb("wfm%d" % i, [128, KC, 128], BF16) for i in range(2)]
    stg = [P.sb("stg%d" % i, [128, 512], F32) for i in range(4)]
    it = 0
    for mt in range(NFM):
        w = wb[mt % 2]
        P.dma("pool", w[:], G.win_fm[l, mt], reads=[G.win_fm], writes=[w])
        for (t0, n, s) in chunks:
            ps = G.pb()
            for kc in range(KC):
                P.mm(ps[:, :n], w[:, kc, :], hT[:, kc, t0 - tbase:t0 - tbase + n], kc == 0, kc == KC - 1, [w, hT], [ps])
            sg = stg[it % 4]
            evac(G, it, sg[:, :n], ps[:, :n], [ps], [sg])
            P.dma("sp" if it % 2 == 0 else "act", G.pF[mt * 128:(mt + 1) * 128, t0:t0 + n], sg[:, :n],
                  reads=[sg], wdisj=[G.pF], owner=sg)
            it += 1
    wtm = P.sb("wtm", [128, KC, NTM], BF16)
    P.dma("pool", wtm[:], G.win_tm[l], reads=[G.win_tm], writes=[wtm])
    stT = [P.sb("stT%d" % i, [128, NTM], F32) for i in range(2)]
    for ti in range(ntok // 128):
        psa = G.pb()
        psb = G.pb()
        for kc in range(KC):
            P.mm(psa[:, :], hT[:, kc, ti * 128:(ti + 1) * 128], wtm[:, kc, 0:512], kc == 0, kc == KC - 1, [wtm, hT], [psa])
        for kc in range(KC):
            P.mm(psb[:, :NTM - 512], hT[:, kc, ti * 128:(ti + 1) * 128], wtm[:, kc, 512:NTM], kc == 0, kc == KC - 1, [wtm, hT], [psb])
        sg = stT[ti % 2]
        evac(G, 0, sg[:, 0:512], psa[:, :], [psa], [], wdisj=[sg])
        evac(G, 1, sg[:, 512:NTM], psb[:, :NTM - 512], [psb], [], wdisj=[sg])
        P.dma("sp", G.pT[tbase + ti * 128: tbase + (ti + 1) * 128, :], sg[:], reads=[sg], wdisj=[G.pT], owner=sg)
    P.scope_end()


def phase_out_mlp(G, l, last):
    P = G.P
    P.scope_begin()
    xc = P.sb("xc", [128, KC, 512], F32)
    yc = P.sb("yc", [128, KC, 512], BF16)
    hm = yc
    sqb = P.sb("sqb", [128, KC, 512], BF16)
    rstd = P.sb("rstd", [128, 512], F32)
    hid = P.sb("hid", [128, 64, 512], BF16)
    wo = [P.sb("wo%d" % i, [128, KC, 128], BF16) for i in range(2)]
    w1b = [P.sb("w1b%d" % i, [128, KC, 128], BF16) for i in range(2)]
    w2b = [P.sb("w2b%d" % i, [128, 64, 128], BF16) for i in range(2)]
    it = 0
    for (t0, n, s) in CHUNKS:
        if last and s == 1:
            continue
        P.dma("sp", xc[:, :, :n], G.xs[:, :, t0:t0 + n], reads=[G.xs], writes=[xc])
        P.dma("act", yc[:, :, :n], G.yT[:, t0:t0 + n].rearrange("(k p) t -> p k t", p=128), reads=[G.yT], writes=[yc])
        for mt in range(16):
            w = wo[mt % 2]
            P.dma("pool", w[:], G.wout[l, mt], reads=[G.wout], writes=[w])
            ps = G.pb()
            for kc in range(KC):
                P.mm(ps[:, :n], w[:, kc, :], yc[:, kc, :n], kc == 0, kc == KC - 1, [w, yc], [ps])
            P.op("dve", lambda e, ps=ps, mt=mt: e.scalar_tensor_tensor(
                xc[:, mt, :n], ps[:, :n], G.modsb[:, s, 2, mt:mt + 1], xc[:, mt, :n], ALU.mult, ALU.add),
                reads=[ps, G.modsb, xc], writes=[xc])
        P.dma("sp", G.xs[:, :, t0:t0 + n], xc[:, :, :n], reads=[xc], wdisj=[G.xs], owner=sqb)
        norm_mod(G, xc, n, s, 1, hm, 0, sqb, rstd)
        for ht in range(64):
            w = w1b[ht % 2]
            P.dma("pool", w[:], G.w1[l, ht], reads=[G.w1], writes=[w])
            ps = G.pb()
            for kc in range(KC):
                P.mm(ps[:, :n], w[:, kc, :], hm[:, kc, :n], kc == 0, kc == KC - 1, [w, hm], [ps])
            P.op("act", lambda e, ps=ps, ht=ht: e.activation(sqb[:, ht % KC, :n], ps[:, :n], AF.Relu),
                 reads=[ps], wdisj=[sqb])
            P.op("dve", lambda e, ht=ht: e.tensor_tensor(
                hid[:, ht, :n], sqb[:, ht % KC, :n], sqb[:, ht % KC, :n], ALU.mult), reads=[sqb], wdisj=[hid])
        P.dma("sp", xc[:, :, :n], G.xs[:, :, t0:t0 + n], reads=[G.xs], writes=[xc])
        for mt in range(16):
            w = w2b[mt % 2]
            P.dma("pool", w[:], G.w2[l, mt], reads=[G.w2], writes=[w])
            ps = G.pb()
            for hc in range(64):
                P.mm(ps[:, :n], w[:, hc, :], hid[:, hc, :n], hc == 0, hc == 63, [w, hid], [ps])
            P.op("dve", lambda e, ps=ps, mt=mt: e.scalar_tensor_tensor(
                xc[:, mt, :n], ps[:, :n], G.modsb[:, s, 5, mt:mt + 1], xc[:, mt, :n], ALU.mult, ALU.add),
                reads=[ps, G.modsb, xc], writes=[xc])
        P.dma("sp", G.xs[:, :, t0:t0 + n], xc[:, :, :n], reads=[xc], wdisj=[G.xs], owner=rstd)
        it += 1
    P.scope_end()


def phase_final(G):
    P = G.P
    P.scope_begin()
    xc = P.sb("xc", [128, KC, 512], F32)
    sqb = P.sb("sqb", [128, KC, 512], BF16)
    rstd = P.sb("rstd", [128, 512], F32)
    fw = P.sb("fw", [128, KC], F32)
    P.dma("sp", fw[:], G.fnorm[:], reads=[G.fnorm], writes=[fw])
    for (t0, n, s) in CHUNKS[:8]:
        P.dma("sp", xc[:, :, :n], G.xs[:, :, t0:t0 + n], reads=[G.xs], writes=[xc])
        P.op("act", lambda e: e.activation(sqb[:, :, :n], xc[:, :, :n], AF.Square), reads=[xc], writes=[sqb])
        ps = G.pb()
        for kc in range(KC):
            P.mm(ps[:, :n], G.ones[:], sqb[:, kc, :n], kc == 0, kc == KC - 1, [G.ones, sqb], [ps])
        P.op("dve", lambda e, ps=ps: e.tensor_scalar(rstd[:, :n], ps[:, :n], 1.0 / D, 1e-6, ALU.mult, ALU.add),
             reads=[ps], writes=[rstd])
        P.op("act", lambda e: e.activation(rstd[:, :n], rstd[:, :n], AF.Sqrt), reads=[rstd], writes=[rstd])
        P.op("dve", lambda e: e.reciprocal(rstd[:, :n], rstd[:, :n]), reads=[rstd], writes=[rstd])
        P.op("dve", lambda e: e.tensor_tensor(xc[:, :, :n], xc[:, :, :n], bcast(rstd[:, None, :n], [128, KC, n]), ALU.mult),
             reads=[xc, rstd], writes=[xc])
        P.op("dve", lambda e: e.tensor_tensor(xc[:, :, :n], xc[:, :, :n], bcast(fw[:, :, None], [128, KC, n]), ALU.mult),
             reads=[xc, fw], writes=[xc])
        P.dma("sp", G.out[:, :, t0:t0 + n], xc[:, :, :n], reads=[xc], wdisj=[G.out], owner=xc, final=True)
    P.scope_end()


def bc_mid(ap2, k):
    p, n = ap2.shape
    return ap2.unsqueeze(1).broadcast_to([p, k, n])


def bc_last(ap2, n):
    p, k = ap2.shape
    return ap2.unsqueeze(2).broadcast_to([p, k, n])


def attn_core(G, S_mm, key_tiles, pv_mm, n_pt, PT, ptc):
    P = G.P
    nk = len(key_tiles)
    for i, kt in enumerate(key_tiles):
        ps = G.pb6()
        S_mm(ps, kt)
        pt = PT[ptc[0] % n_pt]
        ptc[0] += 1
        P.op("act", lambda e, pt=pt, ps=ps: e.activation(pt[:], ps[:], AF.Exp), reads=[ps], writes=[pt])
        pv_mm(pt, kt, i == 0, i == nk - 1)


def phase_attn_a(G, l, last):
    P = G.P
    P.scope_begin()
    QA = P.sb("QA", [128, 4, TT], BF16)
    KAz = P.sb("KAz", [128, 2, 2, TT], BF16)
    VA = P.sb("VA", [128, 34, 128], BF16)
    cm = P.sb("cm", [128, 2, 512], BF16)
    es = P.sb("es", [128, 8], F32)
    P.op("pool", lambda e: e.memset(KAz[:], 0.0), writes=[KAz])
    P.dma("pool", cm[:], G.cmask[:].rearrange("a p n -> p a n"), reads=[G.cmask], writes=[cm])
    P.dma("sp", es[:], G.sink[l:l + 1, :].broadcast_to([128, 8]), reads=[G.sink], writes=[es])
    P.op("act", lambda e: e.activation(es[:], es[:], AF.Exp), reads=[es], writes=[es])
    for a0 in range(0, 34, 2):
        P.dma("pool", VA[:, a0:a0 + 2, :], G.pT[a0 * 128:(a0 + 2) * 128, 0:128].rearrange("(a p) c -> p a c", p=128),
              reads=[G.pT], wdisj=[VA], owner=VA)
    raw = P.sb("raw", [128, 12, 512], F32)
    rp = P.sb("rp", [128, 4, 512], F32)
    tmp = P.sb("tmp", [128, 4, 512], F32)
    for (t0, n, s) in CHUNKS:
        P.dma("sp", raw[:, :, :n], G.pF[0:12 * 128, t0:t0 + n].rearrange("(a p) t -> p a t", p=128),
              reads=[G.pF], writes=[raw])
        if s == 0:
            P.dma("act", rp[:, :, :n], G.rope[:, :, t0:t0 + n].rearrange("a p t -> p a t"), reads=[G.rope], writes=[rp])
            P.op("dve", lambda e: e.tensor_tensor(tmp[:, :, :n], raw[:, 0:4, :n], bc_mid(rp[:, 0, :n], 4), ALU.mult),
                 reads=[raw, rp], writes=[tmp])
            P.op("pool", lambda e: e.tensor_tensor(raw[:, 4:8, :n], raw[:, 4:8, :n], bc_mid(rp[:, 1, :n], 4), ALU.mult),
                 reads=[raw, rp], writes=[raw])
            P.op("dve", lambda e: e.tensor_tensor(QA[:, :, t0:t0 + n], tmp[:, :, :n], raw[:, 4:8, :n], ALU.add),
                 reads=[tmp, raw], wdisj=[QA])
            P.op("dve", lambda e: e.tensor_tensor(tmp[:, 0:2, :n], raw[:, 8:10, :n], bc_mid(rp[:, 2, :n], 2), ALU.mult),
                 reads=[raw, rp], writes=[tmp])
            P.op("pool", lambda e: e.tensor_tensor(raw[:, 10:12, :n], raw[:, 10:12, :n], bc_mid(rp[:, 3, :n], 2), ALU.mult),
                 reads=[raw, rp], writes=[raw])
            for hf in range(2):
                P.op("dve", lambda e: e.tensor_tensor(KAz[hf * 64:(hf + 1) * 64, :, hf, t0:t0 + n],
                                                      tmp[hf * 64:(hf + 1) * 64, 0:2, :n],
                                                      raw[hf * 64:(hf + 1) * 64, 10:12, :n], ALU.add),
                     reads=[tmp, raw], wdisj=[KAz])
        else:
            P.op("act", lambda e: e.activation(QA[:, :, t0:t0 + n], raw[:, 0:4, :n], AF.Copy, scale=0.125),
                 reads=[raw], wdisj=[QA])
            for hf in range(2):
                P.op("dve", lambda e: e.tensor_copy(KAz[hf * 64:(hf + 1) * 64, :, hf, t0:t0 + n],
                                                    raw[hf * 64:(hf + 1) * 64, 8:10, :n]), reads=[raw], wdisj=[KAz])
    PT = [P.sb("PT%d" % i, [128, 512], BF16) for i in range(3)]
    ptc = [0]
    yst = [P.sb("yst%d" % i, [128, 4, 512], BF16) for i in range(2)]
    dn = P.sb("dn", [128, 4, 128], F32)
    yTa = G.yT[0:512, :].rearrange("(h d) t -> d h t", d=64)
    nblocks = 32 if last else 34
    for g in range(2):
        for n in range(nblocks):
            kts = []
            if n < 32:
                if n > 0:
                    kts.append((n - 1, 0))
                kts.append((n, None))
                if n < 31:
                    kts.append((n + 1, 1))
            kts += [(32, None), (33, None)]
            num = G.PB[6]
            den = G.PB[7]

            def S_mm(ps, kt, g=g, n=n):
                kti, mk = kt
                first = True
                if mk is not None:
                    P.mm(ps[:], G.identb[:], cm[:, mk, :], True, False, [G.identb, cm], [ps])
                    first = False
                for hh in range(4):
                    h = 4 * g + hh
                    j, hf = h // 2, h % 2
                    P.mm(ps[:, hh * 128:(hh + 1) * 128], KAz[:, g, hf, kti * 128:(kti + 1) * 128],
                         QA[:, j, n * 128:(n + 1) * 128], first, hh == 3, [KAz, QA], [ps])
                    first = False

            def pv_mm(pt, kt, first, lastk, g=g, num=num, den=den):
                kti, mk = kt
                P.mm(num[:], VA[:, kti, :], pt[:], first, lastk, [VA, pt], [num])
                P.mm(den[:], G.ones[:], pt[:], first, lastk, [G.ones, pt], [den])
            attn_core(G, S_mm, kts, pv_mm, 3, PT, ptc)
            ys = yst[(n // 4) % 2]
            co = (n % 4) * 128
            P.op("dve", lambda e: e.tensor_tensor(
                dn[:], den[:].rearrange("p (h q) -> p h q", h=4), bc_last(es[:, 4 * g:4 * g + 4], 128), ALU.add),
                reads=[den, es], writes=[dn])
            P.op("dve", lambda e: e.reciprocal(dn[:], dn[:]), reads=[dn], writes=[dn])
            P.op("dve", lambda e: e.tensor_tensor(
                ys[:, :, co:co + 128], num[:].rearrange("p (h q) -> p h q", h=4), dn[:], ALU.mult),
                reads=[num, dn], wdisj=[ys])
            if n % 4 == 3 or n == nblocks - 1:
                t0 = (n // 4) * 512
                nn = (n % 4 + 1) * 128
                P.dma("sp", yTa[:, 4 * g:4 * g + 4, t0:t0 + nn], ys[g * 64:(g + 1) * 64, :, :nn],
                      reads=[ys], wdisj=[G.yT], owner=ys)
    P.scope_end()


def phase_attn_c(G, l, last):
    P = G.P
    P.scope_begin()
    QC = P.sb("QC", [128, 4, TT], BF16)
    KCz = P.sb("KCz", [128, 4, 2, TT], BF16)
    VC = P.sb("VC", [128, 34, 512], BF16)
    P.op("pool", lambda e: e.memset(KCz[:], 0.0), writes=[KCz])
    for a0 in range(0, 34, 2):
        P.dma("pool", VC[:, a0:a0 + 2, :], G.pT[a0 * 128:(a0 + 2) * 128, 128:640].rearrange("(a p) c -> p a c", p=128),
              reads=[G.pT], wdisj=[VC], owner=VC)
    raw = P.sb("raw", [128, 8, 512], F32)
    for (t0, n, s) in CHUNKS:
        P.dma("sp", raw[:, :, :n], G.pF[24 * 128:32 * 128, t0:t0 + n].rearrange("(a p) t -> p a t", p=128),
              reads=[G.pF], writes=[raw])
        P.op("act", lambda e: e.activation(QC[:, :, t0:t0 + n], raw[:, 0:4, :n], AF.Copy, scale=0.125),
             reads=[raw], wdisj=[QC])
        for hf in range(2):
            P.op("dve" if hf == 0 else "pool", lambda e: e.tensor_copy(
                KCz[hf * 64:(hf + 1) * 64, :, hf, t0:t0 + n], raw[hf * 64:(hf + 1) * 64, 4:8, :n]),
                reads=[raw], wdisj=[KCz])
    PT = [P.sb("PT%d" % i, [128, 512], BF16) for i in range(3)]
    BT = [P.sb("BT%d" % i, [128, 512], BF16) for i in range(3)]
    ptc = [0]
    btc = [0]
    yst = [P.sb("yst%d" % i, [128, 4, 512], BF16) for i in range(2)]
    dn = P.sb("dn", [128, 512], F32)
    yTc = G.yT[1024:1536, :].rearrange("(h d) t -> d h t", d=64)
    nblocks = 32 if last else 34
    for pg in range(2):
        for n in range(nblocks):
            kts = []
            if n < 32:
                R2 = n
                if R2 < 2:
                    ks = range(0, 4)
                elif R2 > 29:
                    ks = range(28, 32)
                else:
                    ks = range(R2 - 2, R2 + 3)
                kts += [(K2, _na_case_id(R2, K2)) for K2 in ks]
            kts += [(32, None), (33, None)]
            num = G.PB[6]
            den = G.PB[7]

            def S_mm(ps, kt, pg=pg, n=n):
                kti, case = kt
                first = True
                if case is not None:
                    bt = BT[btc[0] % 3]
                    btc[0] += 1
                    P.dma("pool", bt[:], G.nbt[l, case, pg], reads=[G.nbt], writes=[bt])
                    P.mm(ps[:], G.identb[:], bt[:], True, False, [G.identb, bt], [ps])
                    first = False
                for hh in range(4):
                    h = 4 * pg + hh
                    j, hf = h // 2, h % 2
                    P.mm(ps[:, hh * 128:(hh + 1) * 128], KCz[:, j, hf, kti * 128:(kti + 1) * 128],
                         QC[:, j, n * 128:(n + 1) * 128], first, hh == 3, [KCz, QC], [ps])
                    first = False

            def pv_mm(pt, kt, first, lastk, pg=pg, num=num, den=den):
                kti, case = kt
                for hh in range(4):
                    h = 4 * pg + hh
                    j = h // 2
                    P.mm(num[:, hh * 128:(hh + 1) * 128], VC[:, kti, j * 128:(j + 1) * 128], pt[:, hh * 128:(hh + 1) * 128],
                         first and hh == 0, lastk and hh == 3, [VC, pt], [num])
                P.mm(den[:], G.ones[:], pt[:], first, lastk, [G.ones, pt], [den])
            attn_core(G, S_mm, kts, pv_mm, 3, PT, ptc)
            ys = yst[(n // 4) % 2]
            co = (n % 4) * 128
            P.op("dve", lambda e: e.reciprocal(dn[:], den[:]), reads=[den], writes=[dn])
            P.op("dve", lambda e: e.tensor_tensor(
                ys[:, :, co:co + 128], num[:].rearrange("p (h q) -> p h q", h=4),
                dn[:].rearrange("p (h q) -> p h q", h=4), ALU.mult),
                reads=[num, dn], wdisj=[ys])
            if n % 4 == 3 or n == nblocks - 1:
                t0 = (n // 4) * 512
                nn = (n % 4 + 1) * 128
                for hh in range(4):
                    hf = hh % 2
                    P.dma("sp" if hh < 2 else "act", yTc[:, 4 * pg + hh, t0:t0 + nn], ys[hf * 64:(hf + 1) * 64, hh, :nn],
                          reads=[ys], wdisj=[G.yT], owner=ys)
    P.scope_end()


def _zero_rows(G, r0, r1):
    P = G.P
    P.scope_begin()
    z = P.sb("zrow", [128, 2176], BF16)
    P.op("pool", lambda e: e.memset(z[:], 0.0), writes=[z])
    for r in range(r0, r1, 128):
        for c in range(0, TT, 2176):
            P.dma("sp", G.yT[r:r + 128, c:c + 2176], z[:], reads=[z], wdisj=[G.yT], owner=z)
    P.scope_end()


def phase_ssd(G, l, last):
    P = G.P
    PB = G.PB
    P.scope_begin()
    cw = P.sb("cw", [128, 8, 3], F32)
    cb = P.sb("cb", [128, 8], F32)
    P.dma("sp", cw[:], G.scw[l], reads=[G.scw], writes=[cw])
    P.dma("sp", cb[:], G.scb[l], reads=[G.scb], writes=[cb])
    XSb = P.sb("XSb", [128, 4, TT], BF16)
    BC = P.sb("BC", [128, 4, TT], BF16)
    raw = P.sb("raw", [128, 8, 514], F32)
    acc = P.sb("acc", [128, 8, 512], F32)
    for (t0, n, s) in CHUNKS:
        seq0, seq1 = (0, T) if s == 0 else (T, TT)
        lo = max(t0 - 1, seq0)
        hi = min(t0 + n + 1, seq1)
        if lo == t0 or hi == t0 + n:
            P.op("pool", lambda e: e.memset(raw[:], 0.0), writes=[raw])
        P.dma("sp", raw[:, :, lo - (t0 - 1):hi - (t0 - 1)],
              G.pF[36 * 128:44 * 128, lo:hi].rearrange("(a p) t -> p a t", p=128), reads=[G.pF], writes=[raw])
        for ti in range(8):
            eng = "dve"
            P.op(eng, lambda e: e.tensor_scalar(acc[:, ti, :n], raw[:, ti, 1:n + 1], cw[:, ti, 1:2], cb[:, ti:ti + 1],
                                                ALU.mult, ALU.add), reads=[raw, cw, cb], wdisj=[acc])
            P.op(eng, lambda e: e.scalar_tensor_tensor(acc[:, ti, :n], raw[:, ti, 0:n], cw[:, ti, 0:1], acc[:, ti, :n],
                                                       ALU.mult, ALU.add), reads=[raw, cw, acc], wdisj=[acc])
            P.op(eng, lambda e: e.scalar_tensor_tensor(acc[:, ti, :n], raw[:, ti, 2:n + 2], cw[:, ti, 2:3], acc[:, ti, :n],
                                                       ALU.mult, ALU.add), reads=[raw, cw, acc], wdisj=[acc])
        P.op("act", lambda e: e.activation(acc[:, :, :n], acc[:, :, :n], AF.Silu), reads=[acc], writes=[acc])
        P.dma("sp", G.sxs[:, t0:t0 + n].rearrange("(a p) t -> p a t", p=128), acc[:, 0:4, :n], reads=[acc], wdisj=[G.sxs], owner=acc)
        P.op("pool", lambda e: e.tensor_copy(XSb[:, :, t0:t0 + n], acc[:, 0:4, :n]), reads=[acc], wdisj=[XSb])
        P.op("dve", lambda e: e.tensor_copy(BC[:, :, t0:t0 + n], acc[:, 4:8, :n]), reads=[acc], wdisj=[BC])
    import os
    sstop = int(os.environ.get("SSTOP", "99"))
    ssub = int(os.environ.get("SSUB", "99"))
    sson = os.environ.get("SSONLY", "abcd")
    if sstop < 2:
        P.scope_end()
        return
    dtr = P.sb("dtr", [128, 34, 16], F32)
    dtv = P.sb("dtv", [128, 34, 16], F32)
    adt = P.sb("adt", [128, 34, 16], F32)
    dtb = P.sb("dtb", [128, 16], F32)
    aa = P.sb("aa", [128, 16], F32)
    P.dma("sp", dtr[:], G.pT[:, 640:656].rearrange("(a p) c -> p a c", p=128), reads=[G.pT], writes=[dtr])
    P.dma("sp", dtb[:], G.sdtb[l:l + 1, :].broadcast_to([128, 16]), reads=[G.sdtb], writes=[dtb])
    P.dma("sp", aa[:], G.salog[l:l + 1, :].broadcast_to([128, 16]), reads=[G.salog], writes=[aa])
    P.op("act", lambda e: e.activation(aa[:], aa[:], AF.Exp), reads=[aa], writes=[aa])
    P.op("dve", lambda e: e.tensor_tensor(dtr[:], dtr[:], bc_mid(dtb[:], 34), ALU.add), reads=[dtr, dtb], writes=[dtr])
    P.op("act", lambda e: e.activation(dtv[:], dtr[:], AF.Exp), reads=[dtr], writes=[dtv])
    P.op("dve", lambda e: e.tensor_scalar(dtv[:], dtv[:], 1.0, None, ALU.add), reads=[dtv], writes=[dtv])
    P.op("act", lambda e: e.activation(dtv[:], dtv[:], AF.Ln), reads=[dtv], writes=[dtv])
    P.op("dve", lambda e: e.tensor_tensor(adt[:], dtv[:], bc_mid(aa[:], 34), ALU.mult), reads=[dtv, aa], writes=[adt])
    P.op("dve", lambda e: e.tensor_scalar(adt[:], adt[:], -1.0, None, ALU.mult), reads=[adt], writes=[adt])
    adth = P.sb("adth", [128, 34, 16], BF16)
    adtl = P.sb("adtl", [128, 34, 16], BF16)
    adthf = P.sb("adthf", [128, 34, 16], F32)
    P.op("dve", lambda e: e.tensor_copy(adth[:], adt[:]), reads=[adt], writes=[adth])
    P.op("dve", lambda e: e.tensor_copy(adthf[:], adth[:]), reads=[adth], writes=[adthf])
    P.op("dve", lambda e: e.tensor_tensor(adthf[:], adt[:], adthf[:], ALU.subtract), reads=[adt, adthf], writes=[adthf])
    P.op("dve", lambda e: e.tensor_copy(adtl[:], adthf[:]), reads=[adthf], writes=[adtl])
    if sstop < 3:
        P.scope_end()
        return
    Xtm = P.sb("Xtm", [128, 34, 512], BF16)
    Btm = P.sb("Btm", [128, 34, 256], BF16)
    for ti in range(34):
        c0 = ti * 128
        ps = PB[7] if ti % 2 == 0 else PB[6]
        for j in range(4):
            P.mm(ps[:, j * 128:(j + 1) * 128], XSb[:, j, c0:c0 + 128], G.identb[:], j == 0, j == 3, [XSb, G.identb], [ps])
        P.op("act", lambda e: e.activation(Xtm[:, ti, :], ps[:], AF.Copy), reads=[ps], wdisj=[Xtm])
        ps2 = PB[5] if ti % 2 == 0 else PB[4]
        for j in range(2):
            P.mm(ps2[:, j * 128:(j + 1) * 128], BC[:, j, c0:c0 + 128], G.identb[:], j == 0, j == 1, [BC, G.identb], [ps2])
        P.op("dve", lambda e: e.tensor_copy(Btm[:, ti, :], ps2[:, 0:256]), reads=[ps2], wdisj=[Btm])
    if sstop < 5:
        P.scope_end()
        return
    tri = P.sb("tri", [128, 2, 128], BF16)
    tmk = P.sb("tmk", [128, 2, 128], F32)
    P.dma("pool", tri[:], G.tri[:].rearrange("a p n -> p a n"), reads=[G.tri], writes=[tri])
    P.dma("sp", tmk[:], G.trimask[:].rearrange("a p n -> p a n"), reads=[G.trimask], writes=[tmk])
    ST = P.sb("ST", [128, 8, 64], F32)
    STb = P.sb("STb", [128, 8, 64], BF16)
    AdtBh = P.sb("AdtBh", [128, 8, 128], BF16)
    AdtBl = P.sb("AdtBl", [128, 8, 128], BF16)
    css = P.sb("css", [128, 8], F32)
    bl = P.sb("bl", [128, 8], F32)
    dsv = P.sb("dsv", [128, 8], F32)
    ed = P.sb("ed", [128, 8], F32)
    EB = P.sb("EB", [128, 8, 128], F32)
    bcs = P.sb("bcs", [128, 8, 128], F32)
    Gs = P.sb("Gs", [128, 2, 128], F32)
    arg = P.sb("arg", [128, 8, 128], F32)
    Ce = P.sb("Ce", [128, 8, 128], BF16)
    Mt = P.sb("Mt", [128, 8, 128], BF16)
    Xd = P.sb("Xd", [128, 8, 64], BF16)
    Xs = P.sb("Xs", [128, 8, 64], BF16)
    yst = [P.sb("ysd%d" % i, [128, 8, 128], F32) for i in range(2)]
    it = 0
    for d in range(2):
        P.op("dve", lambda e: e.memset(ST[:], 0.0), writes=[ST])
        P.op("pool", lambda e: e.memset(STb[:], 0.0), writes=[STb])
        order = ([32, 33] + list(range(32))) if d == 0 else ([33, 32] + list(range(31, -1, -1)))
        lastc = 127 if d == 0 else 0
        if sstop < 6:
            order = order[:3]
        for ti in order:
            c0 = ti * 128
            need_y = (ti < 32) or (not last)
            if "a" in sson:
                P.op("dve", lambda e: e.tensor_copy(AdtBh[:], bc_last(adth[:, ti, d * 8:(d + 1) * 8], 128)), reads=[adth], writes=[AdtBh])
                P.op("dve", lambda e: e.tensor_copy(AdtBl[:], bc_last(adtl[:, ti, d * 8:(d + 1) * 8], 128)), reads=[adtl], writes=[AdtBl])
            if "b" in sson:
                P.mm(PB[0][:, 0:8], tri[:, d, :], adth[:, ti, d * 8:(d + 1) * 8], True, False, [tri, adth], [PB[0]])
                P.mm(PB[0][:, 0:8], tri[:, d, :], adtl[:, ti, d * 8:(d + 1) * 8], False, True, [tri, adtl], [PB[0]])
            for h in range(8):
                if "c" not in sson:
                    break
                bk = PB[1 + h // 4]
                P.mm(bk[:, (h % 4) * 128:(h % 4 + 1) * 128], AdtBh[:, h, :], tri[:, d, :], h % 4 == 0, False, [AdtBh, tri], [bk])
                P.mm(bk[:, (h % 4) * 128:(h % 4 + 1) * 128], AdtBl[:, h, :], tri[:, d, :], False, h % 4 == 3, [AdtBl, tri], [bk])
            for g in range(2):
                if "d" not in sson:
                    break
                P.mm(PB[3][:, g * 128:(g + 1) * 128], BC[:, g, c0:c0 + 128], BC[:, 2 + g, c0:c0 + 128], g == 0, g == 1, [BC], [PB[3]])
            if ssub < 1:
                continue
            P.op("act", lambda e: e.activation(css[:], PB[0][:, 0:8], AF.Copy), reads=[PB[0]], writes=[css])
            for b in range(2):
                bk = PB[1 + b]
                if b == 0:
                    P.op("act", lambda e: e.activation(bcs[:, 0:4, :].rearrange("p h l -> p (h l)"), bk[:], AF.Copy),
                         reads=[bk], wdisj=[bcs])
                else:
                    P.op("dve", lambda e: e.tensor_copy(bcs[:, 4:8, :].rearrange("p h l -> p (h l)"), bk[:]),
                         reads=[bk], wdisj=[bcs])
            P.op("dve", lambda e: e.tensor_copy(bl[:], bcs[:, :, lastc]), reads=[bcs], writes=[bl])
            P.op("act", lambda e: e.activation(EB[:], bcs[:], AF.Exp), reads=[bcs], writes=[EB])
            P.op("dve", lambda e: e.tensor_tensor(arg[:], bcs[:], bc_last(css[:], 128), ALU.subtract),
                 reads=[bcs, css], writes=[arg])
            if ssub < 2:
                continue
            P.op("dve", lambda e: e.tensor_tensor(arg[:], arg[:], bc_mid(tmk[:, d, :], 8), ALU.add), reads=[arg, tmk], writes=[arg])
            P.op("act", lambda e: e.activation(arg[:], arg[:], AF.Exp), reads=[arg], writes=[arg])
            for g in range(2):
                P.op("dve", lambda e: e.tensor_tensor(Ce[:, 4 * g:4 * g + 4, :], EB[:, 4 * g:4 * g + 4, :],
                                                       bc_mid(BC[:, 2 + g, c0:c0 + 128], 4), ALU.mult),
                     reads=[EB, BC], wdisj=[Ce])
                if g == 0:
                    P.op("act", lambda e: e.activation(Gs[:].rearrange("p g l -> p (g l)"), PB[3][:, 0:256], AF.Copy),
                         reads=[PB[3]], writes=[Gs])
                P.op("dve", lambda e: e.tensor_tensor(Mt[:, 4 * g:4 * g + 4, :], arg[:, 4 * g:4 * g + 4, :],
                                                      bc_mid(Gs[:, g, :], 4), ALU.mult),
                     reads=[arg, Gs], wdisj=[Mt])
            if ssub < 3:
                continue
            P.op("dve", lambda e: e.tensor_tensor(Xd[:], Xtm[:, ti, :].rearrange("p (h q) -> p h q", h=8),
                                                   bc_last(dtv[:, ti, d * 8:(d + 1) * 8], 64), ALU.mult),
                 reads=[Xtm, dtv], writes=[Xd])
            P.op("dve", lambda e: e.tensor_tensor(dsv[:], bl[:], css[:], ALU.subtract), reads=[bl, css], writes=[dsv])
            P.op("act", lambda e: e.activation(dsv[:], dsv[:], AF.Exp), reads=[dsv], writes=[dsv])
            P.op("act", lambda e: e.activation(ed[:], bl[:], AF.Exp), reads=[bl], writes=[ed])
            P.op("dve", lambda e: e.tensor_tensor(Xs[:], Xd[:], bc_last(dsv[:], 64), ALU.mult), reads=[Xd, dsv], writes=[Xs])
            if ssub < 4:
                continue
            if need_y:
                for h in range(8):
                    bk = PB[4 + h // 4]
                    cs_ = slice((h % 4) * 128, (h % 4 + 1) * 128)
                    pr = h // 2
                    P.mm(bk[:, cs_], Xd[:, 2 * pr:2 * pr + 2, :].rearrange("p a q -> p (a q)"), Mt[:, h, :],
                         h % 4 == 0, False, [Xd, Mt], [bk])
                    P.mm(bk[:, cs_], STb[:, 2 * pr:2 * pr + 2, :].rearrange("p a q -> p (a q)"), Ce[:, h, :],
                         False, True, [STb, Ce], [bk])
                ys = yst[it % 2]
                it += 1
                for b in range(2):
                    bk = PB[4 + b]
                    P.op("act", lambda e: e.activation(ys[:, 4 * b:4 * b + 4, :], bk[:].rearrange("p (h l) -> p h l", h=4), AF.Copy),
                         reads=[bk], wdisj=[ys])
                ysv = ys[:].rearrange("p (q two) l -> p q two l", two=2)
                dst = G.ysd[d].rearrange("(q two c) t -> two c q t", two=2, c=64)
                for par in range(2):
                    P.dma("sp" if par == 0 else "act", dst[par, :, :, c0:c0 + 128], ysv[par * 64:(par + 1) * 64, :, par, :],
                          reads=[ys], wdisj=[G.ysd], owner=ys)
            if ssub < 5:
                continue
            for g in range(2):
                P.mm(PB[6][:, g * 256:(g + 1) * 256], Btm[:, ti, g * 128:(g + 1) * 128],
                     Xs[:, 4 * g:4 * g + 4, :].rearrange("p a q -> p (a q)"), g == 0, g == 1, [Btm, Xs], [PB[6]])
            P.op("dve", lambda e: e.tensor_tensor(ST[:], ST[:], bc_last(ed[:], 64), ALU.mult), reads=[ST, ed], writes=[ST])
            P.op("dve", lambda e: e.tensor_tensor(ST[:].rearrange("p h q -> p (h q)"), PB[6][:], ST[:].rearrange("p h q -> p (h q)"), ALU.add),
                 reads=[ST, PB[6]], writes=[ST])
            P.op("act", lambda e: e.activation(STb[:], ST[:], AF.Copy), reads=[ST], writes=[STb])
    P.scope_end()
    if sstop < 7:
        return
    P.scope_begin()
    sdv = P.sb("sdv", [128, 4], F32)
    snw = P.sb("snw", [128, 4], F32)
    P.dma("sp", sdv[:], G.sd[l], reads=[G.sd], writes=[sdv])
    P.dma("sp", snw[:], G.snw[l], reads=[G.snw], writes=[snw])
    yf = P.sb("yf", [128, 4, 512], F32)
    yb = P.sb("yb", [128, 4, 512], F32)
    xsl = P.sb("xsl", [128, 4, 512], F32)
    zz = P.sb("zz", [128, 4, 512], F32)
    sqv = P.sb("sqv", [128, 4, 512], BF16)
    rs = P.sb("rs", [128, 2, 512], F32)
    yo = P.sb("yo", [128, 4, 512], BF16)
    for (t0, n, s) in CHUNKS:
        if last and s == 1:
            continue
        P.dma("sp", yf[:, :, :n], G.ysd[0, :, t0:t0 + n].rearrange("(a p) t -> p a t", p=128), reads=[G.ysd], writes=[yf])
        P.dma("act", yb[:, :, :n], G.ysd[1, :, t0:t0 + n].rearrange("(a p) t -> p a t", p=128), reads=[G.ysd], writes=[yb])
        P.dma("sp", xsl[:, :, :n], G.sxs[:, t0:t0 + n].rearrange("(a p) t -> p a t", p=128), reads=[G.sxs], writes=[xsl])
        P.dma("act", zz[:, :, :n], G.pF[32 * 128:36 * 128, t0:t0 + n].rearrange("(a p) t -> p a t", p=128), reads=[G.pF], writes=[zz])
        P.op("dve", lambda e: e.tensor_tensor(yf[:, :, :n], yf[:, :, :n], yb[:, :, :n], ALU.add), reads=[yf, yb], writes=[yf])
        P.op("pool", lambda e: e.tensor_tensor(xsl[:, :, :n], xsl[:, :, :n], bc_last(sdv[:], n), ALU.mult), reads=[xsl, sdv], writes=[xsl])
        P.op("dve", lambda e: e.tensor_tensor(yf[:, :, :n], yf[:, :, :n], xsl[:, :, :n], ALU.add), reads=[yf, xsl], writes=[yf])
        P.op("act", lambda e: e.activation(zz[:, :, :n], zz[:, :, :n], AF.Silu), reads=[zz], writes=[zz])
        P.op("dve", lambda e: e.tensor_tensor(yf[:, :, :n], yf[:, :, :n], zz[:, :, :n], ALU.mult), reads=[yf, zz], writes=[yf])
        P.op("act", lambda e: e.activation(sqv[:, :, :n], yf[:, :, :n], AF.Square), reads=[yf], writes=[sqv])
        for g in range(2):
            ps = G.pb()
            for j in range(2):
                P.mm(ps[:, :n], G.ones[:], sqv[:, 2 * g + j, :n], j == 0, j == 1, [G.ones, sqv], [ps])
            P.op("dve", lambda e: e.tensor_scalar(rs[:, g, :n], ps[:, :n], 1.0 / 256, 1e-6, ALU.mult, ALU.add), reads=[ps], wdisj=[rs])
        P.op("act", lambda e: e.activation(rs[:, :, :n], rs[:, :, :n], AF.Sqrt), reads=[rs], writes=[rs])
        P.op("dve", lambda e: e.reciprocal(rs[:, :, :n], rs[:, :, :n]), reads=[rs], writes=[rs])
        for j in range(4):
            P.op("dve", lambda e: e.scalar_tensor_tensor(
                yo[:, j, :n], yf[:, j, :n], snw[:, j:j + 1], rs[:, j // 2, :n], ALU.mult, ALU.mult),
                reads=[yf, snw, rs], wdisj=[yo])
        P.dma("sp", G.yT[1536:2048, t0:t0 + n].rearrange("(a p) t -> p a t", p=128), yo[:, :, :n], reads=[yo], wdisj=[G.yT], owner=yo)
    P.scope_end()


def hy_mlp(G, l, hA, NP):
    P = G.P
    P.scope_begin()
    zt = P.sb("zt", [128, NP], F32)
    P.dma("sp", zt[:], G.hz[:], reads=[G.hz], writes=[zt])
    hbv = P.sb("hbv", [128, 4], F32)
    P.dma("sp", hbv[:], G.hb[l], reads=[G.hb], writes=[hbv])
    wm = P.sb("wm", [128, 3, 128], F32)
    P.dma("sp", wm[:], G.hw123[l].rearrange("a p n -> p a n"), reads=[G.hw123], writes=[wm])
    hB = P.sb("hB", [128, NP], F32)
    ttm = P.sb("ttm", [128, 512], F32)
    src = zt
    for li in range(3):
        dst = hA if li % 2 == 0 else hB
        for t0 in range(0, NP, 512):
            n = min(512, NP - t0)
            ps = G.pb()
            P.mm(ps[:, :n], wm[:, li, :], src[:, t0:t0 + n], True, True, [wm, src], [ps])
            P.op("dve", lambda e: e.tensor_scalar(dst[:, t0:t0 + n], ps[:, :n], hbv[:, li:li + 1], hbv[:, 3:4], ALU.add, ALU.mult),
                 reads=[ps, hbv], wdisj=[dst])
            P.op("act", lambda e: e.activation(dst[:, t0:t0 + n], dst[:, t0:t0 + n], AF.Sin, scale=1.0 / 9.0), reads=[dst], wdisj=[dst])
            for _ in range(2):
                P.op("dve", lambda e: e.tensor_tensor(ttm[:, :n], dst[:, t0:t0 + n], dst[:, t0:t0 + n], ALU.mult), reads=[dst], writes=[ttm])
                P.op("dve", lambda e: e.tensor_scalar(ttm[:, :n], ttm[:, :n], -4.0, 3.0, ALU.mult, ALU.add), reads=[ttm], writes=[ttm])
                P.op("dve", lambda e: e.tensor_tensor(dst[:, t0:t0 + n], dst[:, t0:t0 + n], ttm[:, :n], ALU.mult), reads=[dst, ttm], wdisj=[dst])
        src = dst
    P.scope_end()


def phase_hyena(G, l, last):
    P = G.P
    PB = G.PB
    P.scope_begin()
    NP = TT + T
    KV = 8320
    hA = P.sb("hA", [128, NP], F32)
    hy_mlp(G, l, hA, NP)
    h3 = hA
    w4 = P.sb("w4", [128, 2048], F32)
    P.dma("sp", w4[:], G.hw4p[l], reads=[G.hw4p], writes=[w4])
    sw = P.sb("sw", [128, 12, 3], F32)
    sbv = P.sb("sbv", [128, 12], F32)
    skp = P.sb("skp", [128, 2, 4], F32)
    P.dma("sp", sw[:], G.hsw[l], reads=[G.hsw], writes=[sw])
    P.dma("sp", sbv[:], G.hsb[l], reads=[G.hsb], writes=[sbv])
    P.dma("sp", skp[:], G.hskip[l], reads=[G.hskip], writes=[skp])
    revb = P.sb("revb", [128, 128], BF16)
    P.dma("pool", revb[:], G.rev[:], reads=[G.rev], writes=[revb])
    pin = P.sb("pin", [128, T + 2], F32)
    ux = [P.sb("ux%d" % i, [128, T], F32) for i in range(3)]
    kern = P.sb("kern", [128, 2, TC], F32)
    dch = [P.sb("dch%d" % i, [128, 512], F32) for i in range(2)]
    acc = P.sb("acc", [128, TC], F32)
    k0 = P.sb("k0", [128, 1], F32)
    yo = P.sb("yo", [128, T], BF16)
    kvs = P.sb("kvs", [128, KV], BF16)
    curb = P.sb("curb", [128, T], BF16)
    Utm = P.sb("Utm", [128, 32, 128], BF16)
    Urev = P.sb("Urev", [128, 32, 128], BF16)
    band = [P.sb("band%d" % i, [128, 8192], BF16) for i in range(2)]
    Yt2 = P.sb("Yt2", [128, 32, 128], BF16)
    P.op("pool", lambda e: e.memset(kvs[:], 0.0), writes=[kvs])
    import os
    nct = int(os.environ.get("HY_CT", "4"))
    seqs = [(0, T)] if last else [(0, T), (T, TC)]
    it = 0
    dci = 0
    for ct in range(nct):
        for (q0, L) in seqs:
            for i in range(3):
                tile = 12 + 4 * i + ct
                wi = 4 * i + ct
                P.op("pool", lambda e: e.memset(pin[:, 0:L + 2], 0.0), writes=[pin])
                P.dma("sp", pin[:, 1:L + 1], G.pF[tile * 128:(tile + 1) * 128, q0:q0 + L], reads=[G.pF], writes=[pin])
                u = ux[i]
                P.op("dve", lambda e: e.tensor_scalar(u[:, :L], pin[:, 1:L + 1], sw[:, wi, 1:2], sbv[:, wi:wi + 1], ALU.mult, ALU.add),
                     reads=[pin, sw, sbv], writes=[u])
                P.op("dve", lambda e: e.scalar_tensor_tensor(u[:, :L], pin[:, 0:L], sw[:, wi, 0:1], u[:, :L], ALU.mult, ALU.add),
                     reads=[pin, sw, u], writes=[u])
                P.op("dve", lambda e: e.scalar_tensor_tensor(u[:, :L], pin[:, 2:L + 2], sw[:, wi, 2:3], u[:, :L], ALU.mult, ALU.add),
                     reads=[pin, sw, u], writes=[u])
            cur = ux[2]
            for o in range(2):
                cf = o * 1024 + ct * 128
                cb_ = o * 1024 + 512 + ct * 128
                if L == TC:
                    P.dma("sp", dch[0][:, :L], G.hdec[ct * 128:(ct + 1) * 128, q0:q0 + L], reads=[G.hdec], writes=[dch[0]])
                    for dr, c0 in ((0, cf), (1, cb_)):
                        ps = G.pb()
                        P.mm(ps[:, :L], w4[:, c0:c0 + 128], h3[:, q0:q0 + L], True, True, [w4, h3], [ps])
                        P.op("dve", lambda e: e.tensor_tensor(kern[:, dr, :L], ps[:, :L], dch[0][:, :L], ALU.mult),
                             reads=[ps, dch[0]], wdisj=[kern])
                    P.op("dve", lambda e: e.tensor_tensor(k0[:], kern[:, 0, 0:1], skp[:, o, ct:ct + 1], ALU.add), reads=[kern, skp], writes=[k0])
                    P.op("dve", lambda e: e.tensor_scalar(acc[:, :L], cur[:, :L], k0[:, 0:1], None, ALU.mult), reads=[cur, k0], writes=[acc])
                    for tau in range(1, L):
                        P.op("dve", lambda e: e.scalar_tensor_tensor(acc[:, tau:L], cur[:, 0:L - tau], kern[:, 0, tau:tau + 1], acc[:, tau:L],
                                                                    ALU.mult, ALU.add), reads=[cur, kern, acc], writes=[acc])
                        P.op("dve", lambda e: e.scalar_tensor_tensor(acc[:, 0:L - tau], cur[:, tau:L], kern[:, 1, tau:tau + 1], acc[:, 0:L - tau],
                                                                    ALU.mult, ALU.add), reads=[cur, kern, acc], writes=[acc])
                    if o == 0:
                        P.op("dve", lambda e: e.tensor_tensor(ux[2][:, :L], ux[0][:, :L], acc[:, :L], ALU.mult), reads=[ux[0], acc], writes=[ux[2]])
                    else:
                        P.op("dve", lambda e: e.tensor_tensor(yo[:, :L], ux[1][:, :L], acc[:, :L], ALU.mult), reads=[ux[1], acc], writes=[yo])
                    continue
                for a in range(0, T, 512):
                    dc = dch[dci % 2]
                    dci += 1
                    P.dma("act", dc[:], G.hdecr[ct * 128:(ct + 1) * 128, a:a + 512], reads=[G.hdecr], writes=[dc])
                    ps = G.pb()
                    P.mm(ps[:], w4[:, cb_:cb_ + 128], h3[:, TT + a:TT + a + 512], True, True, [w4, h3], [ps])
                    P.op("dve", lambda e: e.tensor_tensor(kvs[:, a:a + 512], ps[:], dc[:], ALU.mult), reads=[ps, dc], wdisj=[kvs])
                for a in range(0, T, 512):
                    dc = dch[dci % 2]
                    dci += 1
                    P.dma("act", dc[:], G.hdec[ct * 128:(ct + 1) * 128, a:a + 512], reads=[G.hdec], writes=[dc])
                    ps = G.pb()
                    P.mm(ps[:], w4[:, cf:cf + 128], h3[:, a:a + 512], True, True, [w4, h3], [ps])
                    if a == 0:
                        P.op("dve", lambda e: e.tensor_tensor(dc[:, 0:1], dc[:, 0:1], ps[:, 0:1], ALU.mult), reads=[ps, dc], writes=[dc])
                        P.op("dve", lambda e: e.tensor_tensor(k0[:], dc[:, 0:1], skp[:, o, ct:ct + 1], ALU.add), reads=[dc, skp], writes=[k0])
                        P.op("dve", lambda e: e.tensor_tensor(kvs[:, 4096:4096 + 511], ps[:, 1:512], dc[:, 1:512], ALU.mult), reads=[ps, dc], wdisj=[kvs])
                        P.op("act", lambda e: e.activation(kvs[:, 4095:4096], k0[:], AF.Copy), reads=[k0], wdisj=[kvs])
                    else:
                        P.op("dve", lambda e: e.tensor_tensor(kvs[:, 4095 + a:4095 + a + 512], ps[:], dc[:], ALU.mult), reads=[ps, dc], wdisj=[kvs])
                kvd = G.kvd[it % 2]
                it += 1
                P.dma("sp", kvd[:], kvs[:], reads=[kvs], writes=[kvd], owner=kvs)
                P.op("act", lambda e: e.activation(curb[:], cur[:], AF.Copy), reads=[cur], writes=[curb])
                for jb in range(8):
                    ps = G.pb()
                    for q in range(4):
                        J = jb * 4 + q
                        P.mm(ps[:, q * 128:(q + 1) * 128], curb[:, J * 128:(J + 1) * 128], G.identb[:], q == 0, q == 3, [curb, G.identb], [ps])
                    evac(G, jb, Utm[:, jb * 4:jb * 4 + 4, :].rearrange("p j c -> p (j c)"), ps[:], [ps], [], wdisj=[Utm])
                for jb in range(8):
                    ps = G.pb()
                    P.mm(ps[:], revb[:], Utm[:, jb * 4:jb * 4 + 4, :].rearrange("p j c -> p (j c)"), True, True, [revb, Utm], [ps])
                    evac(G, jb + 1, Urev[:, jb * 4:jb * 4 + 4, :].rearrange("p j c -> p (j c)"), ps[:], [ps], [], wdisj=[Urev])
                kt = kvd.t.tensor
                for c in range(128):
                    bd = band[c % 2]
                    P.dma("sp" if c % 2 == 0 else "act", bd[:], bass.AP(kt, kvd.t.offset + c * KV, [[1, 128], [1, 8192]]),
                          reads=[kvd], writes=[bd])
                    cc = c % 16
                    bk = PB[(c // 16) % 2]
                    for dq in [31] + [x for x in range(63) if x != 31]:
                        d = dq - 31
                        J0 = max(0, -d)
                        N = 32 - abs(d)
                        I0 = J0 + d
                        P.mm(bk[:, cc * 32 + I0:cc * 32 + I0 + N], bd[:, 128 * dq:128 * dq + 128], Urev[:, J0:J0 + N, c],
                             cc == 0 and d == 0, False, [bd, Urev], [bk])
                    if cc == 15:
                        c0 = c - 15
                        evac(G, c // 16, Yt2[:, :, c0:c0 + 16], bk[:].rearrange("p (c i) -> p i c", c=16), [bk], [], wdisj=[Yt2])
                for ib in range(8):
                    ps = G.pb()
                    for q in range(4):
                        I = ib * 4 + q
                        P.mm(ps[:, q * 128:(q + 1) * 128], Yt2[:, I, :], G.identb[:], q == 0, q == 3, [Yt2, G.identb], [ps])
                    cs_ = slice(ib * 512, (ib + 1) * 512)
                    if o == 0:
                        P.op("dve", lambda e: e.tensor_tensor(ux[2][:, cs_], ps[:], ux[0][:, cs_], ALU.mult), reads=[ps, ux[0]], wdisj=[ux[2]])
                    else:
                        P.op("dve", lambda e: e.tensor_tensor(yo[:, cs_], ps[:], ux[1][:, cs_], ALU.mult), reads=[ps, ux[1]], wdisj=[yo])
            P.dma("sp", G.yT[512 + ct * 128:512 + (ct + 1) * 128, q0:q0 + L], yo[:, :L], reads=[yo], wdisj=[G.yT], owner=yo)
    P.scope_end()


_CACHE = {}


def kernel(**inputs):
    inp = {k: np.asarray(v) for k, v in inputs.items()}
    if "nc" not in _CACHE:
        _CACHE["nc"] = build_program()
    nc = _CACHE["nc"]
    sh = _prep_shared(inp)
    in_maps = []
    for c in range(8):
        m = dict(sh)
        m.update(_prep_core(inp, c % 4))
        in_maps.append(m)
    res = run_bass_kernel_spmd(nc, in_maps, core_ids=list(range(8)))
    outs = []
    for b in range(4):
        o = np.asarray(res.results[b]["out"])
        outs.append(o.transpose(2, 0, 1).reshape(T, D))
    return np.stack(outs).astype(np.float32)
```

```python
import math
import numpy as np
import concourse.bass as bass
import concourse.mybir as mybir
from concourse.bass_utils import run_bass_kernel_spmd

F32 = mybir.dt.float32
BF16 = mybir.dt.bfloat16
AF = mybir.ActivationFunctionType
ALU = mybir.AluOpType
AX = mybir.AxisListType


class Buf:
    __slots__ = ("t", "name", "w", "r", "dsem", "dcnt")

    def __init__(self, t, name):
        self.t = t
        self.name = name
        self.w = {}
        self.r = {}
        self.dsem = None
        self.dcnt = 0

    def __getitem__(self, idx):
        return self.t[idx]


class _Rec:
    def __init__(self):
        self.call = None

    def __getattr__(self, name):
        def f(*a, **k):
            self.call = (name, a, k)
            return self
        return f


class Prog:
    ENG = ("pe", "act", "dve", "pool", "sp")

    def __init__(self, nc):
        self.nc = nc
        self.q = {e: [] for e in self.ENG}
        self.cnt = {e: 0 for e in self.ENG}
        self.sem = {}
        self.known = {e: {} for e in self.ENG}
        self.ctx = []
        self.perm = []
        self.nsem = 0
        self.out_tokens = {}
        self.scopes = []
        self.free_dsems = []
        self.all_dsems = []
        self.semcount = {}
        self.scope_bufs = []
        for e in self.ENG:
            self.sem[e] = self._newsem("s_" + e)

    def _newsem(self, name):
        g = self.nc.semaphore(name)
        s = g.__enter__()
        self.perm.append(g)
        self.nsem += 1
        return s

    def sb(self, name, shape, dt):
        self.uid = getattr(self, "uid", 0) + 1
        g = self.nc.sbuf_tensor(name + "_%d" % self.uid, list(shape), dt)
        t = g.__enter__()
        self.ctx.append(g)
        b = Buf(t, name)
        self.scope_bufs.append(b)
        return b

    def _get_dsem(self, owner):
        if owner.dsem is None:
            if self.free_dsems:
                owner.dsem = self.free_dsems.pop()
            else:
                owner.dsem = self._newsem_perm("d%d" % len(self.all_dsems))
                self.all_dsems.append(owner.dsem)
                self.semcount[owner.dsem] = 0
        return owner.dsem

    def _newsem_perm(self, name):
        g = self.nc.semaphore(name)
        s = g.__enter__()
        self.perm.append(g)
        return s

    def scope_begin(self):
        self.scopes.append((len(self.ctx), len(self.scope_bufs)))

    def barrier(self):
        for e in self.ENG:
            waits = []
            kn = self.known[e]
            for e2 in self.ENG:
                if e2 != e and self.cnt[e2] > kn.get(self.sem[e2], 0):
                    kn[self.sem[e2]] = self.cnt[e2]
                    waits.append((self.sem[e2], self.cnt[e2]))
            for s_ in self.all_dsems:
                v = self.semcount[s_]
                if v > kn.get(s_, 0):
                    kn[s_] = v
                    waits.append((s_, v))
            if waits:
                self.q[e].append((waits, None, None, 0))

    def scope_end(self):
        self.barrier()
        n, nb = self.scopes.pop()
        for b in self.scope_bufs[nb:]:
            if b.dsem is not None:
                self.free_dsems.append(b.dsem)
                b.dsem = None
        del self.scope_bufs[nb:]
        while len(self.ctx) > n:
            g = self.ctx.pop()
            g.__exit__(None, None, None)

    def ps(self, name, shape, dt=F32):
        g = self.nc.psum_tensor(name, list(shape), dt)
        t = g.__enter__()
        self.ctx.append(g)
        return Buf(t, name)

    def dram(self, name, shape, dt, kind="Internal"):
        t = self.nc.dram_tensor(name, list(shape), dt, kind=kind)
        return Buf(t.ap(), name)

    def mm(self, out, lhsT, rhs, start, stop, reads, writes):
        return self.op("pe", lambda e: e.matmul(out, lhsT, rhs, start=start, stop=stop, skip_group_check=True),
                       reads=reads, writes=writes)

    def _waits(self, eng, reads, writes, wdisj=()):
        need = {}
        for b in wdisj:
            for s, v in b.r.items():
                if need.get(s, 0) < v:
                    need[s] = v
        for b in reads:
            for s, v in b.w.items():
                if need.get(s, 0) < v:
                    need[s] = v
        for b in writes:
            for s, v in b.w.items():
                if need.get(s, 0) < v:
                    need[s] = v
            for s, v in b.r.items():
                if need.get(s, 0) < v:
                    need[s] = v
        out = []
        kn = self.known[eng]
        for s, v in need.items():
            if eng == "pe" and s is self.sem["pe"]:
                continue
            if kn.get(s, 0) >= v:
                continue
            kn[s] = v
            out.append((s, v))
        return out

    def _record(self, tok, reads, writes, wdisj=()):
        s, v = tok
        for b in wdisj:
            if b.r:
                b.w = {s: v}
                b.r = {}
            elif b.w.get(s, 0) < v:
                b.w[s] = v
        for b in reads:
            if b.r.get(s, 0) < v:
                b.r[s] = v
        for b in writes:
            b.w = {s: v}
            b.r = {}

    def op(self, eng, fn, reads=(), writes=(), wdisj=()):
        rec = _Rec()
        fn(rec)
        name_, a_, k_ = rec.call

        def fn(e, name_=name_, a_=a_, k_=k_):
            return getattr(e, name_)(*a_, **k_)
        waits = self._waits(eng, reads, writes, wdisj)
        self.cnt[eng] += 1
        tok = (self.sem[eng], self.cnt[eng])
        self.q[eng].append((waits, fn, tok[0], 1))
        self._record(tok, reads, writes, wdisj)
        return tok

    def dma(self, eng, out_ap, in_ap, reads=(), writes=(), wdisj=(), owner=None, final=False):
        if owner is None:
            owner = writes[0] if writes else reads[0]
        ds = self._get_dsem(owner)
        waits = self._waits(eng, reads, writes, wdisj)
        kn = self.known[eng]
        cur = self.semcount[ds]
        if cur and kn.get(ds, 0) < cur:
            kn[ds] = cur
            waits.append((ds, cur))
        self.semcount[ds] = cur + 16
        tok = (ds, cur + 16)

        def fn(e, o=out_ap, i=in_ap):
            return e.dma_start(out=o, in_=i)
        self.q[eng].append((waits, fn, tok[0], 16))
        self._record(tok, reads, writes, wdisj)
        if final:
            self.out_tokens[tok[0]] = tok[1]
        return tok

    def simulate_sync(self):
        pos = {e: 0 for e in self.ENG}
        val = {}
        progress = True
        while progress:
            progress = False
            for e in self.ENG:
                q = self.q[e]
                while pos[e] < len(q):
                    waits, fn, s_, inc = q[pos[e]]
                    if any(val.get(id(ws), 0) < wv for ws, wv in waits):
                        break
                    if fn is not None:
                        val[id(s_)] = val.get(id(s_), 0) + inc
                    pos[e] += 1
                    progress = True
        stuck = {e: (pos[e], len(self.q[e])) for e in self.ENG if pos[e] < len(self.q[e])}
        if not stuck:
            return None
        rep = {}
        for e, (p, n) in stuck.items():
            waits = self.q[e][p][0]
            rep[e] = (p, n, [(getattr(ws, "name", str(ws)), wv, val.get(id(ws), 0)) for ws, wv in waits])
        return rep

    def emit(self):
        nc = self.nc
        fin = list(self.out_tokens.items())
        q = self.q
        with nc.Block() as block:
            def run(e, lst, extra=()):
                for waits, fn, s, inc in lst:
                    for ws, wv in waits:
                        e.wait_ge(ws, wv)
                    if fn is not None:
                        fn(e).then_inc(s, inc)
                for ws, wv in extra:
                    e.wait_ge(ws, wv)

            @block.sync
            def _(e):
                run(e, q["sp"], fin)

            @block.tensor
            def _(e):
                run(e, q["pe"])

            @block.scalar
            def _(e):
                run(e, q["act"])

            @block.vector
            def _(e):
                run(e, q["dve"])

            @block.gpsimd
            def _(e):
                run(e, q["pool"])
        for g in reversed(self.ctx):
            g.__exit__(None, None, None)
        for g in reversed(self.perm):
            g.__exit__(None, None, None)


D = 2048
KC = 16
T = 4096
TC = 256
TT = T + TC
NL = 4
NFM = 44
NTM = 656
CHUNKS = [(i * 512, 512, 0) for i in range(8)] + [(T, TC, 1)]
NEGM = -30000.0


def _fm_cols():
    cols = []
    rp = np.concatenate([np.arange(16, 32), np.arange(0, 16), np.arange(48, 64), np.arange(32, 48)])
    qa = np.arange(512)
    qap = (np.arange(8)[:, None] * 64 + rp[None]).reshape(-1)
    cols += [qa, qap]
    for perm in (False, True):
        for g in range(2):
            base = 512 + g * 64 + (rp if perm else np.arange(64))
            cols.append(np.concatenate([base, base]))
    cols.append(768 + np.arange(1536))
    cols.append(2304 + np.arange(1024))
    cols.append(3840 + np.arange(512))
    cols.append(4352 + np.arange(1024))
    c = np.concatenate(cols)
    assert c.shape[0] == NFM * 128
    return c


def _tm_cols():
    return np.concatenate([640 + np.arange(128), 3328 + np.arange(512), 5376 + np.arange(16)])


def _rope_tables():
    t = np.arange(T)
    row, col = t // 64, t % 64
    inv = 10000.0 ** (-np.arange(16, dtype=np.float64) / 16)
    d = np.arange(64)
    pos = np.where(d[:, None] < 32, row[None], col[None]).astype(np.float64)
    ang = pos * inv[d % 16][:, None]
    cos = np.cos(ang)
    sin = np.sin(ang) * np.where((d % 32) < 16, -1.0, 1.0)[:, None]
    cos2 = np.concatenate([cos, cos], 0)
    sin2 = np.concatenate([sin, sin], 0)
    return np.stack([cos2 * 0.125, sin2 * 0.125, cos2, sin2]).astype(np.float32)


def _na_cases():
    cases = [(10, 10 + dk) for dk in range(-2, 3)]
    for R2 in (0, 1):
        cases += [(R2, K2) for K2 in range(4)]
    for R2 in (30, 31):
        cases += [(R2, K2) for K2 in range(28, 32)]
    return cases


def _na_case_id(R2, K2):
    if 2 <= R2 <= 29:
        return K2 - R2 + 2
    if R2 < 2:
        return 5 + R2 * 4 + K2
    return 13 + (R2 - 30) * 4 + (K2 - 28)


def _na_tables(rpb):
    cases = _na_cases()
    kk = np.arange(128)
    qq = np.arange(128)
    out = np.empty((len(cases), 2, 128, 512), np.float32)
    for ci, (R2, K2) in enumerate(cases):
        kr = 2 * K2 + kk // 64
        ck = kk % 64
        r = 2 * R2 + qq // 64
        cq = qq % 64
        rstart = np.clip(r - 4, 0, 56)
        cstart = np.clip(cq - 8, 0, 48)
        vr = (kr[:, None] >= rstart[None]) & (kr[:, None] < rstart[None] + 8)
        vc = (ck[:, None] >= cstart[None]) & (ck[:, None] < cstart[None] + 16)
        roff = np.clip(kr[:, None] - r[None] + 7, 0, 14)
        coff = np.clip(ck[:, None] - cq[None], -15, 15) + 15
        valid = vr & vc
        for h in range(8):
            bias = rpb[h][roff, coff]
            out[ci, h // 4, :, (h % 4) * 128:(h % 4 + 1) * 128] = np.where(valid, bias, np.float32(NEGM))
    return out


def _hy_tables(L):
    t = np.linspace(0.0, 1.0, L, dtype=np.float32)[:, None]
    f = np.linspace(1e-4, 15, 16, dtype=np.float32)[None]
    wpos = (2.0 * math.pi * np.arange(L, dtype=np.float32)[:, None] / L).astype(np.float32)
    z = np.concatenate([t, np.cos(f * wpos), -np.sin(f * wpos)], axis=-1).astype(np.float32)
    max_decay = math.log(1e-2) / 0.3
    min_decay = math.log(1e-2) / 1.5
    deltas = np.linspace(min_decay, max_decay, 512, dtype=np.float32)
    decay = np.exp(-t * np.abs(deltas)[None]).astype(np.float32)
    return np.ascontiguousarray(z.T), np.ascontiguousarray(decay.T)


def _prep_shared(inp):
    sh = {}
    f32 = np.float32
    sh["adaw"] = np.ascontiguousarray(inp["ada_w"].reshape(NL, 128, 16, 6 * D).transpose(0, 2, 1, 3))
    sh["adab"] = np.ascontiguousarray(inp["ada_b"])
    sh["nrm"] = np.ascontiguousarray(np.stack([inp["norm_mix"], inp["norm_mlp"]], 1).reshape(NL, 2, 128, 16))
    sh["fnorm"] = np.ascontiguousarray(inp["final_norm"].reshape(128, 16))
    w_in = inp["w_in"].reshape(NL, 128, 16, -1)
    fm = w_in[..., _fm_cols()].reshape(NL, 128, 16, NFM, 128)
    sh["win_fm"] = np.ascontiguousarray(fm.transpose(0, 3, 1, 2, 4))
    sh["win_tm"] = np.ascontiguousarray(w_in[..., _tm_cols()])
    wo = inp["w_out"].reshape(NL, 16, 128, 128, 16)
    sh["wout"] = np.ascontiguousarray(wo.transpose(0, 4, 2, 1, 3))
    w1 = inp["mlp_w1"].reshape(NL, 128, 16, 64, 128)
    sh["w1"] = np.ascontiguousarray(w1.transpose(0, 3, 1, 2, 4))
    w2 = inp["mlp_w2"].reshape(NL, 64, 128, 128, 16)
    sh["w2"] = np.ascontiguousarray(w2.transpose(0, 4, 2, 1, 3))
    sh["rope"] = _rope_tables()
    kk = np.arange(128)[:, None]
    qq = np.arange(128)[None]
    mp = np.where(qq <= kk, 0.0, NEGM).astype(f32)
    mn = np.where(kk <= qq, 0.0, NEGM).astype(f32)
    sh["cmask"] = np.stack([np.tile(mp, (1, 4)), np.tile(mn, (1, 4))]).astype(f32)
    sh["sink"] = np.ascontiguousarray(inp["attn_sink"])
    sh["nbt"] = np.stack([_na_tables(inp["na_rpb"][l]) for l in range(NL)])
    sh["scw"] = np.ascontiguousarray(inp["ssm_conv_w"].reshape(NL, 3, 8, 128).transpose(0, 3, 2, 1))
    sh["scb"] = np.ascontiguousarray(inp["ssm_conv_b"].reshape(NL, 8, 128).transpose(0, 2, 1))
    sh["sdtb"] = np.ascontiguousarray(inp["ssm_dt_bias"].reshape(NL, 16))
    sh["salog"] = np.ascontiguousarray(inp["ssm_a_log"].reshape(NL, 16))
    sh["sd"] = np.ascontiguousarray(np.repeat(inp["ssm_d"], 64, axis=1).reshape(NL, 4, 128).transpose(0, 2, 1))
    sh["snw"] = np.ascontiguousarray(inp["ssm_norm"].reshape(NL, 4, 128).transpose(0, 2, 1))
    tri = np.triu(np.ones((128, 128), f32))
    sh["tri"] = np.stack([tri, tri.T]).astype(f32)
    mk = np.where(tri > 0, 0.0, NEGM).astype(f32)
    sh["trimask"] = np.stack([mk, mk.T]).astype(f32)
    sh["ident"] = np.eye(128, dtype=f32)
    sh["hsw"] = np.ascontiguousarray(inp["hy_short_w"].reshape(NL, 3, 12, 128).transpose(0, 3, 2, 1))
    sh["hsb"] = np.ascontiguousarray(inp["hy_short_b"].reshape(NL, 12, 128).transpose(0, 2, 1))
    w123 = np.zeros((NL, 3, 128, 128), f32)
    w123[:, 0, :33, :64] = inp["hy_w1"]
    w123[:, 1, :64, :64] = inp["hy_w2"]
    w123[:, 2, :64, :64] = inp["hy_w3"]
    sh["hw123"] = w123
    w4p = np.zeros((NL, 128, 2048), f32)
    w4p[:, :64, :] = inp["hy_w4"]
    sh["hw4p"] = w4p
    hb = np.zeros((NL, 128, 4), f32)
    hb[:, :64, :] = np.stack([inp["hy_b1"], inp["hy_b2"], inp["hy_b3"], inp["hy_freq"]], 2)
    sh["hb"] = hb
    sh["hskip"] = np.ascontiguousarray(inp["hy_skip"].reshape(NL, 2, 4, 128).transpose(0, 3, 1, 2))
    zl, dl = _hy_tables(T)
    zc, dc = _hy_tables(TC)
    hz = np.zeros((128, TT + T), f32)
    hz[:33] = np.concatenate([zl, zc, zl[:, ::-1]], 1)
    sh["hz"] = hz
    sh["hdecr"] = np.ascontiguousarray(dl[:, ::-1])
    sh["rev"] = np.ascontiguousarray(np.eye(128, dtype=f32)[::-1])
    sh["hdec"] = np.ascontiguousarray(np.concatenate([dl, dc], 1))
    return sh


def _prep_core(inp, b):
    xcat = np.concatenate([inp["x"][b], inp["ctx"][b]], 0)
    xT = np.ascontiguousarray(xcat.reshape(TT, 128, 16).transpose(1, 2, 0))
    cv = np.ascontiguousarray(np.stack([inp["c"][b].reshape(128, 16), inp["c_ctx"].reshape(128, 16)], 2))
    return {"xT": xT, "cv": cv}


def bcast(ap, shape):
    return ap.broadcast_to(list(shape))


class Ctx:
    pass


def build_program(nlayers=NL, dbg=False, stop=99, attn=True):
    nc = bass.Bass("TRN2", target_bir_lowering=False)
    P = Prog(nc)
    G = Ctx()
    G.P = P
    G.dbg = dbg
    di = lambda n, s, dt=F32: P.dram(n, s, dt, kind="ExternalInput")
    G.xT = di("xT", [128, KC, TT])
    G.cv = di("cv", [128, KC, 2])
    G.adaw = di("adaw", [NL, KC, 128, 6 * D])
    G.adab = di("adab", [NL, 6 * D])
    G.nrm = di("nrm", [NL, 2, 128, KC])
    G.fnorm = di("fnorm", [128, KC])
    G.win_fm = di("win_fm", [NL, NFM, 128, KC, 128])
    G.win_tm = di("win_tm", [NL, 128, KC, NTM])
    G.wout = di("wout", [NL, 16, 128, 16, 128])
    G.w1 = di("w1", [NL, 64, 128, KC, 128])
    G.w2 = di("w2", [NL, 16, 128, 64, 128])
    G.rope = di("rope", [4, 128, T])
    G.cmask = di("cmask", [2, 128, 512])
    G.sink = di("sink", [NL, 8])
    G.nbt = di("nbt", [NL, 21, 2, 128, 512])
    G.scw = di("scw", [NL, 128, 8, 3])
    G.scb = di("scb", [NL, 128, 8])
    G.sdtb = di("sdtb", [NL, 16])
    G.salog = di("salog", [NL, 16])
    G.sd = di("sd", [NL, 128, 4])
    G.snw = di("snw", [NL, 128, 4])
    G.tri = di("tri", [2, 128, 128])
    G.trimask = di("trimask", [2, 128, 128])
    G.ident = di("ident", [128, 128])
    G.hsw = di("hsw", [NL, 128, 12, 3])
    G.hsb = di("hsb", [NL, 128, 12])
    G.hw123 = di("hw123", [NL, 3, 128, 128])
    G.hw4p = di("hw4p", [NL, 128, 2048])
    G.hb = di("hb", [NL, 128, 4])
    G.hskip = di("hskip", [NL, 128, 2, 4])
    G.hz = di("hz", [128, TT + T])
    G.hdecr = di("hdecr", [512, T])
    G.rev = di("rev", [128, 128])
    G.kvd = [P.dram("kvd%d" % i, [128, 8320], BF16) for i in range(2)]
    G.hdec = di("hdec", [512, TT])
    G.out = P.dram("out", [128, KC, T], F32, kind="ExternalOutput")
    G.xs = P.dram("xs", [128, KC, TT], F32)
    G.modv = P.dram("modv", [NL, 2, 6 * D], F32)
    G.pF = P.dram("pF", [NFM * 128, TT], F32)
    G.pT = P.dram("pT", [TT, NTM], F32)
    G.yT = P.dram("yT", [2048, TT], BF16)
    G.sxs = P.dram("sxs", [512, TT], F32)
    G.ysd = P.dram("ysd", [2, 512, TT], F32)
    G.wob = P.dram("wob", [16, 128, 16, 128], BF16)
    G.w1c = P.dram("w1c", [64, 128, KC, 128], BF16)
    G.w2c = P.dram("w2c", [16, 128, 64, 128], BF16)
    if dbg:
        G.d_pF = P.dram("d_pF", [NFM * 128, TT], F32, kind="ExternalOutput")
        G.d_pT = P.dram("d_pT", [TT, NTM], F32, kind="ExternalOutput")
        G.d_yT = P.dram("d_yT", [2048, TT], BF16, kind="ExternalOutput")
        G.d_xs = P.dram("d_xs", [128, KC, TT], F32, kind="ExternalOutput")
        G.d_mod = P.dram("d_mod", [NL, 2, 6 * D], F32, kind="ExternalOutput")
    G.PB = [P.ps("pb%d" % i, [128, 512]) for i in range(8)]
    G.pbi = 0

    def pb():
        G.pbi = (G.pbi + 1) % 8
        return G.PB[G.pbi]
    G.pb = pb
    G.pb6i = 0

    def pb6():
        G.pb6i = (G.pb6i + 1) % 6
        return G.PB[G.pb6i]
    G.pb6 = pb6
    G.ones = P.sb("ones", [128, 128], BF16)
    P.op("dve", lambda e: e.memset(G.ones[:], 1.0), writes=[G.ones])
    G.identb = P.sb("identb", [128, 128], BF16)
    P.dma("pool", G.identb[:], G.ident[:], reads=[G.ident], writes=[G.identb])
    G.modsb = P.sb("modsb", [128, 2, 6, KC], F32)
    G.amod = P.sb("amod", [128, 2, 2, KC], F32)
    G.nrmsb = P.sb("nrmsb", [128, 2, KC], F32)

    phase_mod(G)
    P.dma("sp", G.xs[:], G.xT[:], reads=[G.xT], writes=[G.xs], owner=G.ones)
    for l in range(nlayers):
        last = (l == NL - 1)
        if stop < 1:
            break
        load_mod(G, l)
        for half in (CHUNKS[0:4], CHUNKS[4:9]):
            phase_inproj(G, l, half)
        if stop < 2:
            break
        if dbg and l == 0:
            P.dma("sp", G.d_pF[:], G.pF[:], reads=[G.pF], writes=[G.d_pF], owner=G.ones, final=True)
            P.dma("sp", G.d_pT[:], G.pT[:], reads=[G.pT], writes=[G.d_pT], owner=G.identb, final=True)
        if attn:
            phase_attn_a(G, l, last)
            if stop < 3:
                break
            phase_attn_c(G, l, last)
            if stop < 4:
                break
        else:
            _zero_rows(G, 0, 512)
            _zero_rows(G, 1024, 1536)
        phase_ssd(G, l, last)
        phase_hyena(G, l, last)
        phase_out_mlp(G, l, last)
    if dbg:
        P.dma("sp", G.d_yT[:], G.yT[:], reads=[G.yT], writes=[G.d_yT], owner=G.ones, final=True)
        P.dma("sp", G.d_xs[:], G.xs[:], reads=[G.xs], writes=[G.d_xs], owner=G.ones, final=True)
        P.dma("sp", G.d_mod[:], G.modv[:], reads=[G.modv], writes=[G.d_mod], owner=G.identb, final=True)
    phase_final(G)
    P.emit()
    return nc


def evac(G, i, out_ap, in_ap, reads, writes, wdisj=()):
    P = G.P
    if i % 2 == 0:
        return P.op("act", lambda e: e.activation(out_ap, in_ap, AF.Copy), reads=reads, writes=writes, wdisj=wdisj)
    return P.op("dve", lambda e: e.tensor_copy(out_ap, in_ap), reads=reads, writes=writes, wdisj=wdisj)


def phase_mod(G):
    P = G.P
    P.scope_begin()
    cvs = P.sb("cvs", [128, KC, 2], F32)
    scb = P.sb("scb", [128, KC, 2], BF16)
    P.dma("sp", cvs[:], G.cv[:], reads=[G.cv], writes=[cvs])
    P.op("act", lambda e: e.activation(scb[:], cvs[:], AF.Silu), reads=[cvs], writes=[scb])
    wb = [P.sb("adw%d" % i, [128, KC, 512], BF16) for i in range(2)]
    bt = [P.sb("adb%d" % i, [2, 512], F32) for i in range(2)]
    mr = [P.sb("mr%d" % i, [2, 512], F32) for i in range(2)]
    it = 0
    for l in range(NL):
        for nch in range(24):
            w = wb[it % 2]
            b_ = bt[it % 2]
            m_ = mr[it % 2]
            P.dma("pool", w[:], G.adaw[l, :, :, nch * 512:(nch + 1) * 512].rearrange("k p n -> p k n"),
                  reads=[G.adaw], writes=[w])
            P.dma("sp", b_[:], G.adab[l:l + 1, nch * 512:(nch + 1) * 512].broadcast_to([2, 512]),
                  reads=[G.adab], writes=[b_])
            ps = G.pb()
            for kc in range(KC):
                P.mm(ps[0:2, :], scb[:, kc, :], w[:, kc, :], kc == 0, kc == KC - 1, [scb, w], [ps])
            P.op("dve", lambda e, m_=m_, ps=ps, b_=b_: e.tensor_tensor(m_[:], ps[0:2, :], b_[:], ALU.add),
                 reads=[ps, b_], writes=[m_])
            P.dma("sp", G.modv[l, :, nch * 512:(nch + 1) * 512], m_[:], reads=[m_], wdisj=[G.modv], owner=m_)
            it += 1
    P.scope_end()


def load_mod(G, l):
    P = G.P
    P.dma("sp", G.modsb[:], G.modv[l].rearrange("s (x p k) -> p s x k", x=6, p=128, k=KC),
          reads=[G.modv], writes=[G.modsb])
    P.dma("sp", G.nrmsb[:], G.nrm[l].rearrange("a p k -> p a k"), reads=[G.nrm], writes=[G.nrmsb])
    for s in range(2):
        for j, x in ((0, 1), (1, 4)):
            P.op("dve", lambda e, s=s, j=j, x=x: e.scalar_tensor_tensor(
                G.amod[:, s, j, :], G.modsb[:, s, x, :], 1.0, G.nrmsb[:, j, :], ALU.add, ALU.mult),
                reads=[G.modsb, G.nrmsb], writes=[G.amod])


def norm_mod(G, xc, n, s, which, hT, hoff, sqb, rstd):
    P = G.P
    P.op("act", lambda e: e.activation(sqb[:, :, :n], xc[:, :, :n], AF.Square), reads=[xc], writes=[sqb])
    ps = G.pb()
    for kc in range(KC):
        P.mm(ps[:, :n], G.ones[:], sqb[:, kc, :n], kc == 0, kc == KC - 1, [G.ones, sqb], [ps])
    P.op("dve", lambda e: e.tensor_scalar(rstd[:, :n], ps[:, :n], 1.0 / D, 1e-6, ALU.mult, ALU.add),
         reads=[ps], writes=[rstd])
    P.op("act", lambda e: e.activation(rstd[:, :n], rstd[:, :n], AF.Sqrt), reads=[rstd], writes=[rstd])
    P.op("dve", lambda e: e.reciprocal(rstd[:, :n], rstd[:, :n]), reads=[rstd], writes=[rstd])
    P.op("dve", lambda e: e.tensor_tensor(xc[:, :, :n], xc[:, :, :n], bcast(rstd[:, None, :n], [128, KC, n]), ALU.mult),
         reads=[xc, rstd], writes=[xc])
    shi = 0 if which == 0 else 3
    for kc in range(KC):
        P.op("act", lambda e, kc=kc: e.activation(hT[:, kc, hoff:hoff + n], xc[:, kc, :n], AF.Identity,
                                                  bias=G.modsb[:, s, shi, kc:kc + 1],
                                                  scale=G.amod[:, s, which, kc:kc + 1]),
             reads=[xc, G.modsb, G.amod], writes=[hT])


def phase_inproj(G, l, chunks):
    P = G.P
    P.scope_begin()
    ntok = sum(c[1] for c in chunks)
    tbase = chunks[0][0]
    hT = P.sb("hT", [128, KC, ntok], BF16)
    xc = P.sb("xc", [128, KC, 512], F32)
    sqb = P.sb("sqb", [128, KC, 512], BF16)
    rstd = P.sb("rstd", [128, 512], F32)
    for (t0, n, s) in chunks:
        P.dma("sp", xc[:, :, :n], G.xs[:, :, t0:t0 + n], reads=[G.xs], writes=[xc])
        norm_mod(G, xc, n, s, 0, hT, t0 - tbase, sqb, rstd)
    wb = [P.sb("wfm%d" % i, [128, KC, 128], BF16) for i in range(2)]
    stg = [P.sb("stg%d" % i, [128, 512], F32) for i in range(4)]
    it = 0
    for mt in range(NFM):
        w = wb[mt % 2]
        P.dma("pool", w[:], G.win_fm[l, mt], reads=[G.win_fm], writes=[w])
        for (t0, n, s) in chunks:
            ps = G.pb()
            for kc in range(KC):
                P.mm(ps[:, :n], w[:, kc, :], hT[:, kc, t0 - tbase:t0 - tbase + n], kc == 0, kc == KC - 1, [w, hT], [ps])
            sg = stg[it % 4]
            evac(G, it, sg[:, :n], ps[:, :n], [ps], [sg])
            P.dma("sp" if it % 2 == 0 else "act", G.pF[mt * 128:(mt + 1) * 128, t0:t0 + n], sg[:, :n],
                  reads=[sg], wdisj=[G.pF], owner=sg)
            it += 1
    wtm = P.sb("wtm", [128, KC, NTM], BF16)
    P.dma("pool", wtm[:], G.win_tm[l], reads=[G.win_tm], writes=[wtm])
    stT = [P.sb("stT%d" % i, [128, NTM], F32) for i in range(2)]
    for ti in range(ntok // 128):
        psa = G.pb()
        psb = G.pb()
        for kc in range(KC):
            P.mm(psa[:, :], hT[:, kc, ti * 128:(ti + 1) * 128], wtm[:, kc, 0:512], kc == 0, kc == KC - 1, [wtm, hT], [psa])
        for kc in range(KC):
            P.mm(psb[:, :NTM - 512], hT[:, kc, ti * 128:(ti + 1) * 128], wtm[:, kc, 512:NTM], kc == 0, kc == KC - 1, [wtm, hT], [psb])
        sg = stT[ti % 2]
        evac(G, 0, sg[:, 0:512], psa[:, :], [psa], [], wdisj=[sg])
        evac(G, 1, sg[:, 512:NTM], psb[:, :NTM - 512], [psb], [], wdisj=[sg])
        P.dma("sp", G.pT[tbase + ti * 128: tbase + (ti + 1) * 128, :], sg[:], reads=[sg], wdisj=[G.pT], owner=sg)
    P.scope_end()


def phase_out_mlp(G, l, last):
    P = G.P
    P.scope_begin()
    xc = P.sb("xc", [128, KC, 512], F32)
    yc = P.sb("yc", [128, KC, 512], BF16)
    hm = yc
    sqb = P.sb("sqb", [128, KC, 512], BF16)
    rstd = P.sb("rstd", [128, 512], F32)
    hid = P.sb("hid", [128, 64, 512], BF16)
    wo = [P.sb("wo%d" % i, [128, KC, 128], BF16) for i in range(2)]
    w1b = [P.sb("w1b%d" % i, [128, KC, 128], BF16) for i in range(2)]
    w2b = [P.sb("w2b%d" % i, [128, 64, 128], BF16) for i in range(2)]
    it = 0
    for (t0, n, s) in CHUNKS:
        if last and s == 1:
            continue
        P.dma("sp", xc[:, :, :n], G.xs[:, :, t0:t0 + n], reads=[G.xs], writes=[xc])
        P.dma("act", yc[:, :, :n], G.yT[:, t0:t0 + n].rearrange("(k p) t -> p k t", p=128), reads=[G.yT], writes=[yc])
        first = (t0 == 0)
        for mt in range(16):
            w = wo[mt % 2]
            if first:
                P.dma("pool", w[:], G.wout[l, mt], reads=[G.wout], writes=[w])
                P.dma("sp", G.wob[mt], w[:], reads=[w], wdisj=[G.wob], owner=w)
            else:
                P.dma("sp", w[:], G.wob[mt], reads=[G.wob], writes=[w])
            ps = G.pb()
            for kc in range(KC):
                P.mm(ps[:, :n], w[:, kc, :], yc[:, kc, :n], kc == 0, kc == KC - 1, [w, yc], [ps])
            P.op("dve", lambda e, ps=ps, mt=mt: e.scalar_tensor_tensor(
                xc[:, mt, :n], ps[:, :n], G.modsb[:, s, 2, mt:mt + 1], xc[:, mt, :n], ALU.mult, ALU.add),
                reads=[ps, G.modsb, xc], writes=[xc])
        P.dma("sp", G.xs[:, :, t0:t0 + n], xc[:, :, :n], reads=[xc], wdisj=[G.xs], owner=sqb)
        norm_mod(G, xc, n, s, 1, hm, 0, sqb, rstd)
        for ht in range(64):
            w = w1b[ht % 2]
            if first:
                P.dma("pool", w[:], G.w1[l, ht], reads=[G.w1], writes=[w])
                P.dma("sp", G.w1c[ht], w[:], reads=[w], wdisj=[G.w1c], owner=w)
            else:
                P.dma("sp", w[:], G.w1c[ht], reads=[G.w1c], writes=[w])
            ps = G.pb()
            for kc in range(KC):
                P.mm(ps[:, :n], w[:, kc, :], hm[:, kc, :n], kc == 0, kc == KC - 1, [w, hm], [ps])
            P.op("act", lambda e, ps=ps, ht=ht: e.activation(sqb[:, ht % KC, :n], ps[:, :n], AF.Relu),
                 reads=[ps], wdisj=[sqb])
            P.op("dve", lambda e, ht=ht: e.tensor_tensor(
                hid[:, ht, :n], sqb[:, ht % KC, :n], sqb[:, ht % KC, :n], ALU.mult), reads=[sqb], wdisj=[hid])
        P.dma("sp", xc[:, :, :n], G.xs[:, :, t0:t0 + n], reads=[G.xs], writes=[xc])
        for mt in range(16):
            w = w2b[mt % 2]
            if first:
                P.dma("pool", w[:], G.w2[l, mt], reads=[G.w2], writes=[w])
                P.dma("sp", G.w2c[mt], w[:], reads=[w], wdisj=[G.w2c], owner=w)
            else:
                P.dma("pool", w[:], G.w2c[mt], reads=[G.w2c], writes=[w])
            ps = G.pb()
            for hc in range(64):
                P.mm(ps[:, :n], w[:, hc, :], hid[:, hc, :n], hc == 0, hc == 63, [w, hid], [ps])
            P.op("dve", lambda e, ps=ps, mt=mt: e.scalar_tensor_tensor(
                xc[:, mt, :n], ps[:, :n], G.modsb[:, s, 5, mt:mt + 1], xc[:, mt, :n], ALU.mult, ALU.add),
                reads=[ps, G.modsb, xc], writes=[xc])
        P.dma("sp", G.xs[:, :, t0:t0 + n], xc[:, :, :n], reads=[xc], wdisj=[G.xs], owner=rstd)
        it += 1
    P.scope_end()


def phase_final(G):
    P = G.P
    P.scope_begin()
    xc = P.sb("xc", [128, KC, 512], F32)
    sqb = P.sb("sqb", [128, KC, 512], BF16)
    rstd = P.sb("rstd", [128, 512], F32)
    fw = P.sb("fw", [128, KC], F32)
    P.dma("sp", fw[:], G.fnorm[:], reads=[G.fnorm], writes=[fw])
    for (t0, n, s) in CHUNKS[:8]:
        P.dma("sp", xc[:, :, :n], G.xs[:, :, t0:t0 + n], reads=[G.xs], writes=[xc])
        P.op("act", lambda e: e.activation(sqb[:, :, :n], xc[:, :, :n], AF.Square), reads=[xc], writes=[sqb])
        ps = G.pb()
        for kc in range(KC):
            P.mm(ps[:, :n], G.ones[:], sqb[:, kc, :n], kc == 0, kc == KC - 1, [G.ones, sqb], [ps])
        P.op("dve", lambda e, ps=ps: e.tensor_scalar(rstd[:, :n], ps[:, :n], 1.0 / D, 1e-6, ALU.mult, ALU.add),
             reads=[ps], writes=[rstd])
        P.op("act", lambda e: e.activation(rstd[:, :n], rstd[:, :n], AF.Sqrt), reads=[rstd], writes=[rstd])
        P.op("dve", lambda e: e.reciprocal(rstd[:, :n], rstd[:, :n]), reads=[rstd], writes=[rstd])
        P.op("dve", lambda e: e.tensor_tensor(xc[:, :, :n], xc[:, :, :n], bcast(rstd[:, None, :n], [128, KC, n]), ALU.mult),
             reads=[xc, rstd], writes=[xc])
        P.op("dve", lambda e: e.tensor_tensor(xc[:, :, :n], xc[:, :, :n], bcast(fw[:, :, None], [128, KC, n]), ALU.mult),
             reads=[xc, fw], writes=[xc])
        P.dma("sp", G.out[:, :, t0:t0 + n], xc[:, :, :n], reads=[xc], wdisj=[G.out], owner=xc, final=True)
    P.scope_end()


def bc_mid(ap2, k):
    p, n = ap2.shape
    return ap2.unsqueeze(1).broadcast_to([p, k, n])


def bc_last(ap2, n):
    p, k = ap2.shape
    return ap2.unsqueeze(2).broadcast_to([p, k, n])


def attn_core(G, S_mm, key_tiles, pv_mm, n_pt, PT, ptc):
    P = G.P
    nk = len(key_tiles)
    for i, kt in enumerate(key_tiles):
        ps = G.pb6()
        S_mm(ps, kt)
        pt = PT[ptc[0] % n_pt]
        ptc[0] += 1
        P.op("act", lambda e, pt=pt, ps=ps: e.activation(pt[:], ps[:], AF.Exp), reads=[ps], writes=[pt])
        pv_mm(pt, kt, i == 0, i == nk - 1)


def phase_attn_a(G, l, last):
    P = G.P
    P.scope_begin()
    QA = P.sb("QA", [128, 4, TT], BF16)
    KAz = P.sb("KAz", [128, 2, 2, TT], BF16)
    VA = P.sb("VA", [128, 34, 128], BF16)
    cm = P.sb("cm", [128, 2, 512], BF16)
    es = P.sb("es", [128, 8], F32)
    P.op("pool", lambda e: e.memset(KAz[:], 0.0), writes=[KAz])
    P.dma("pool", cm[:], G.cmask[:].rearrange("a p n -> p a n"), reads=[G.cmask], writes=[cm])
    P.dma("sp", es[:], G.sink[l:l + 1, :].broadcast_to([128, 8]), reads=[G.sink], writes=[es])
    P.op("act", lambda e: e.activation(es[:], es[:], AF.Exp), reads=[es], writes=[es])
    for a0 in range(0, 34, 2):
        P.dma("pool", VA[:, a0:a0 + 2, :], G.pT[a0 * 128:(a0 + 2) * 128, 0:128].rearrange("(a p) c -> p a c", p=128),
              reads=[G.pT], wdisj=[VA], owner=VA)
    raw = P.sb("raw", [128, 12, 512], F32)
    rp = P.sb("rp", [128, 4, 512], F32)
    tmp = P.sb("tmp", [128, 4, 512], F32)
    for (t0, n, s) in CHUNKS:
        P.dma("sp", raw[:, :, :n], G.pF[0:12 * 128, t0:t0 + n].rearrange("(a p) t -> p a t", p=128),
              reads=[G.pF], writes=[raw])
        if s == 0:
            P.dma("act", rp[:, :, :n], G.rope[:, :, t0:t0 + n].rearrange("a p t -> p a t"), reads=[G.rope], writes=[rp])
            P.op("dve", lambda e: e.tensor_tensor(tmp[:, :, :n], raw[:, 0:4, :n], bc_mid(rp[:, 0, :n], 4), ALU.mult),
                 reads=[raw, rp], writes=[tmp])
            P.op("pool", lambda e: e.tensor_tensor(raw[:, 4:8, :n], raw[:, 4:8, :n], bc_mid(rp[:, 1, :n], 4), ALU.mult),
                 reads=[raw, rp], writes=[raw])
            P.op("dve", lambda e: e.tensor_tensor(QA[:, :, t0:t0 + n], tmp[:, :, :n], raw[:, 4:8, :n], ALU.add),
                 reads=[tmp, raw], wdisj=[QA])
            P.op("dve", lambda e: e.tensor_tensor(tmp[:, 0:2, :n], raw[:, 8:10, :n], bc_mid(rp[:, 2, :n], 2), ALU.mult),
                 reads=[raw, rp], writes=[tmp])
            P.op("pool", lambda e: e.tensor_tensor(raw[:, 10:12, :n], raw[:, 10:12, :n], bc_mid(rp[:, 3, :n], 2), ALU.mult),
                 reads=[raw, rp], writes=[raw])
            for hf in range(2):
                P.op("dve", lambda e: e.tensor_tensor(KAz[hf * 64:(hf + 1) * 64, :, hf, t0:t0 + n],
                                                      tmp[hf * 64:(hf + 1) * 64, 0:2, :n],
                                                      raw[hf * 64:(hf + 1) * 64, 10:12, :n], ALU.add),
                     reads=[tmp, raw], wdisj=[KAz])
        else:
            P.op("act", lambda e: e.activation(QA[:, :, t0:t0 + n], raw[:, 0:4, :n], AF.Copy, scale=0.125),
                 reads=[raw], wdisj=[QA])
            for hf in range(2):
                P.op("dve", lambda e: e.tensor_copy(KAz[hf * 64:(hf + 1) * 64, :, hf, t0:t0 + n],
                                                    raw[hf * 64:(hf + 1) * 64, 8:10, :n]), reads=[raw], wdisj=[KAz])
    PT = [P.sb("PT%d" % i, [128, 512], BF16) for i in range(3)]
    ptc = [0]
    yst = [P.sb("yst%d" % i, [128, 4, 512], BF16) for i in range(2)]
    dn = P.sb("dn", [128, 4, 128], F32)
    yTa = G.yT[0:512, :].rearrange("(h d) t -> d h t", d=64)
    nblocks = 32 if last else 34
    for g in range(2):
        for n in range(nblocks):
            kts = []
            if n < 32:
                if n > 0:
                    kts.append((n - 1, 0))
                kts.append((n, None))
                if n < 31:
                    kts.append((n + 1, 1))
            kts += [(32, None), (33, None)]
            num = G.PB[6]
            den = G.PB[7]

            def S_mm(ps, kt, g=g, n=n):
                kti, mk = kt
                first = True
                if mk is not None:
                    P.mm(ps[:], G.identb[:], cm[:, mk, :], True, False, [G.identb, cm], [ps])
                    first = False
                for hh in range(4):
                    h = 4 * g + hh
                    j, hf = h // 2, h % 2
                    P.mm(ps[:, hh * 128:(hh + 1) * 128], KAz[:, g, hf, kti * 128:(kti + 1) * 128],
                         QA[:, j, n * 128:(n + 1) * 128], first, hh == 3, [KAz, QA], [ps])
                    first = False

            def pv_mm(pt, kt, first, lastk, g=g, num=num, den=den):
                kti, mk = kt
                P.mm(num[:], VA[:, kti, :], pt[:], first, lastk, [VA, pt], [num])
                P.mm(den[:], G.ones[:], pt[:], first, lastk, [G.ones, pt], [den])
            attn_core(G, S_mm, kts, pv_mm, 3, PT, ptc)
            ys = yst[(n // 4) % 2]
            co = (n % 4) * 128
            P.op("dve", lambda e: e.tensor_tensor(
                dn[:], den[:].rearrange("p (h q) -> p h q", h=4), bc_last(es[:, 4 * g:4 * g + 4], 128), ALU.add),
                reads=[den, es], writes=[dn])
            P.op("dve", lambda e: e.reciprocal(dn[:], dn[:]), reads=[dn], writes=[dn])
            P.op("dve", lambda e: e.tensor_tensor(
                ys[:, :, co:co + 128], num[:].rearrange("p (h q) -> p h q", h=4), dn[:], ALU.mult),
                reads=[num, dn], wdisj=[ys])
            if n % 4 == 3 or n == nblocks - 1:
                t0 = (n // 4) * 512
                nn = (n % 4 + 1) * 128
                P.dma("sp", yTa[:, 4 * g:4 * g + 4, t0:t0 + nn], ys[g * 64:(g + 1) * 64, :, :nn],
                      reads=[ys], wdisj=[G.yT], owner=ys)
    P.scope_end()


def phase_attn_c(G, l, last):
    P = G.P
    P.scope_begin()
    QC = P.sb("QC", [128, 4, TT], BF16)
    KCz = P.sb("KCz", [128, 4, 2, TT], BF16)
    VC = P.sb("VC", [128, 34, 512], BF16)
    P.op("pool", lambda e: e.memset(KCz[:], 0.0), writes=[KCz])
    for a0 in range(0, 34, 2):
        P.dma("pool", VC[:, a0:a0 + 2, :], G.pT[a0 * 128:(a0 + 2) * 128, 128:640].rearrange("(a p) c -> p a c", p=128),
              reads=[G.pT], wdisj=[VC], owner=VC)
    raw = P.sb("raw", [128, 8, 512], F32)
    for (t0, n, s) in CHUNKS:
        P.dma("sp", raw[:, :, :n], G.pF[24 * 128:32 * 128, t0:t0 + n].rearrange("(a p) t -> p a t", p=128),
              reads=[G.pF], writes=[raw])
        P.op("act", lambda e: e.activation(QC[:, :, t0:t0 + n], raw[:, 0:4, :n], AF.Copy, scale=0.125),
             reads=[raw], wdisj=[QC])
        for hf in range(2):
            P.op("dve" if hf == 0 else "pool", lambda e: e.tensor_copy(
                KCz[hf * 64:(hf + 1) * 64, :, hf, t0:t0 + n], raw[hf * 64:(hf + 1) * 64, 4:8, :n]),
                reads=[raw], wdisj=[KCz])
    PT = [P.sb("PT%d" % i, [128, 512], BF16) for i in range(3)]
    BT = [P.sb("BT%d" % i, [128, 512], BF16) for i in range(3)]
    ptc = [0]
    btc = [0]
    yst = [P.sb("yst%d" % i, [128, 4, 512], BF16) for i in range(2)]
    dn = P.sb("dn", [128, 512], F32)
    yTc = G.yT[1024:1536, :].rearrange("(h d) t -> d h t", d=64)
    nblocks = 32 if last else 34
    for pg in range(2):
        for n in range(nblocks):
            kts = []
            if n < 32:
                R2 = n
                if R2 < 2:
                    ks = range(0, 4)
                elif R2 > 29:
                    ks = range(28, 32)
                else:
                    ks = range(R2 - 2, R2 + 3)
                kts += [(K2, _na_case_id(R2, K2)) for K2 in ks]
            kts += [(32, None), (33, None)]
            num = G.PB[6]
            den = G.PB[7]

            def S_mm(ps, kt, pg=pg, n=n):
                kti, case = kt
                first = True
                if case is not None:
                    bt = BT[btc[0] % 3]
                    btc[0] += 1
                    P.dma("pool", bt[:], G.nbt[l, case, pg], reads=[G.nbt], writes=[bt])
                    P.mm(ps[:], G.identb[:], bt[:], True, False, [G.identb, bt], [ps])
                    first = False
                for hh in range(4):
                    h = 4 * pg + hh
                    j, hf = h // 2, h % 2
                    P.mm(ps[:, hh * 128:(hh + 1) * 128], KCz[:, j, hf, kti * 128:(kti + 1) * 128],
                         QC[:, j, n * 128:(n + 1) * 128], first, hh == 3, [KCz, QC], [ps])
                    first = False

            def pv_mm(pt, kt, first, lastk, pg=pg, num=num, den=den):
                kti, case = kt
                for hh in range(4):
                    h = 4 * pg + hh
                    j = h // 2
                    P.mm(num[:, hh * 128:(hh + 1) * 128], VC[:, kti, j * 128:(j + 1) * 128], pt[:, hh * 128:(hh + 1) * 128],
                         first and hh == 0, lastk and hh == 3, [VC, pt], [num])
                P.mm(den[:], G.ones[:], pt[:], first, lastk, [G.ones, pt], [den])
            attn_core(G, S_mm, kts, pv_mm, 3, PT, ptc)
            ys = yst[(n // 4) % 2]
            co = (n % 4) * 128
            P.op("dve", lambda e: e.reciprocal(dn[:], den[:]), reads=[den], writes=[dn])
            P.op("dve", lambda e: e.tensor_tensor(
                ys[:, :, co:co + 128], num[:].rearrange("p (h q) -> p h q", h=4),
                dn[:].rearrange("p (h q) -> p h q", h=4), ALU.mult),
                reads=[num, dn], wdisj=[ys])
            if n % 4 == 3 or n == nblocks - 1:
                t0 = (n // 4) * 512
                nn = (n % 4 + 1) * 128
                for hh in range(4):
                    hf = hh % 2
                    P.dma("sp" if hh < 2 else "act", yTc[:, 4 * pg + hh, t0:t0 + nn], ys[hf * 64:(hf + 1) * 64, hh, :nn],
                          reads=[ys], wdisj=[G.yT], owner=ys)
    P.scope_end()


def _zero_rows(G, r0, r1):
    P = G.P
    P.scope_begin()
    z = P.sb("zrow", [128, 2176], BF16)
    P.op("pool", lambda e: e.memset(z[:], 0.0), writes=[z])
    for r in range(r0, r1, 128):
        for c in range(0, TT, 2176):
            P.dma("sp", G.yT[r:r + 128, c:c + 2176], z[:], reads=[z], wdisj=[G.yT], owner=z)
    P.scope_end()


def phase_ssd(G, l, last):
    P = G.P
    PB = G.PB
    P.scope_begin()
    cw = P.sb("cw", [128, 8, 3], F32)
    cb = P.sb("cb", [128, 8], F32)
    P.dma("sp", cw[:], G.scw[l], reads=[G.scw], writes=[cw])
    P.dma("sp", cb[:], G.scb[l], reads=[G.scb], writes=[cb])
    XSb = P.sb("XSb", [128, 4, TT], BF16)
    BC = P.sb("BC", [128, 4, TT], BF16)
    raw = P.sb("raw", [128, 8, 514], F32)
    acc = P.sb("acc", [128, 8, 512], F32)
    for (t0, n, s) in CHUNKS:
        seq0, seq1 = (0, T) if s == 0 else (T, TT)
        lo = max(t0 - 1, seq0)
        hi = min(t0 + n + 1, seq1)
        if lo == t0 or hi == t0 + n:
            P.op("pool", lambda e: e.memset(raw[:], 0.0), writes=[raw])
        P.dma("sp", raw[:, :, lo - (t0 - 1):hi - (t0 - 1)],
              G.pF[36 * 128:44 * 128, lo:hi].rearrange("(a p) t -> p a t", p=128), reads=[G.pF], writes=[raw])
        for ti in range(8):
            eng = "dve"
            P.op(eng, lambda e: e.tensor_scalar(acc[:, ti, :n], raw[:, ti, 1:n + 1], cw[:, ti, 1:2], cb[:, ti:ti + 1],
                                                ALU.mult, ALU.add), reads=[raw, cw, cb], wdisj=[acc])
            P.op(eng, lambda e: e.scalar_tensor_tensor(acc[:, ti, :n], raw[:, ti, 0:n], cw[:, ti, 0:1], acc[:, ti, :n],
                                                       ALU.mult, ALU.add), reads=[raw, cw, acc], wdisj=[acc])
            P.op(eng, lambda e: e.scalar_tensor_tensor(acc[:, ti, :n], raw[:, ti, 2:n + 2], cw[:, ti, 2:3], acc[:, ti, :n],
                                                       ALU.mult, ALU.add), reads=[raw, cw, acc], wdisj=[acc])
        P.op("act", lambda e: e.activation(acc[:, :, :n], acc[:, :, :n], AF.Silu), reads=[acc], writes=[acc])
        P.dma("sp", G.sxs[:, t0:t0 + n].rearrange("(a p) t -> p a t", p=128), acc[:, 0:4, :n], reads=[acc], wdisj=[G.sxs], owner=acc)
        P.op("pool", lambda e: e.tensor_copy(XSb[:, :, t0:t0 + n], acc[:, 0:4, :n]), reads=[acc], wdisj=[XSb])
        P.op("dve", lambda e: e.tensor_copy(BC[:, :, t0:t0 + n], acc[:, 4:8, :n]), reads=[acc], wdisj=[BC])
    import os
    sstop = int(os.environ.get("SSTOP", "99"))
    ssub = int(os.environ.get("SSUB", "99"))
    sson = os.environ.get("SSONLY", "abcd")
    if sstop < 2:
        P.scope_end()
        return
    dtr = P.sb("dtr", [128, 34, 16], F32)
    dtv = P.sb("dtv", [128, 34, 16], F32)
    adt = P.sb("adt", [128, 34, 16], F32)
    dtb = P.sb("dtb", [128, 16], F32)
    aa = P.sb("aa", [128, 16], F32)
    P.dma("sp", dtr[:], G.pT[:, 640:656].rearrange("(a p) c -> p a c", p=128), reads=[G.pT], writes=[dtr])
    P.dma("sp", dtb[:], G.sdtb[l:l + 1, :].broadcast_to([128, 16]), reads=[G.sdtb], writes=[dtb])
    P.dma("sp", aa[:], G.salog[l:l + 1, :].broadcast_to([128, 16]), reads=[G.salog], writes=[aa])
    P.op("act", lambda e: e.activation(aa[:], aa[:], AF.Exp), reads=[aa], writes=[aa])
    P.op("dve", lambda e: e.tensor_tensor(dtr[:], dtr[:], bc_mid(dtb[:], 34), ALU.add), reads=[dtr, dtb], writes=[dtr])
    P.op("act", lambda e: e.activation(dtv[:], dtr[:], AF.Exp), reads=[dtr], writes=[dtv])
    P.op("dve", lambda e: e.tensor_scalar(dtv[:], dtv[:], 1.0, None, ALU.add), reads=[dtv], writes=[dtv])
    P.op("act", lambda e: e.activation(dtv[:], dtv[:], AF.Ln), reads=[dtv], writes=[dtv])
    P.op("dve", lambda e: e.tensor_tensor(adt[:], dtv[:], bc_mid(aa[:], 34), ALU.mult), reads=[dtv, aa], writes=[adt])
    P.op("dve", lambda e: e.tensor_scalar(adt[:], adt[:], -1.0, None, ALU.mult), reads=[adt], writes=[adt])
    adth = P.sb("adth", [128, 34, 16], BF16)
    adtl = P.sb("adtl", [128, 34, 16], BF16)
    adthf = P.sb("adthf", [128, 34, 16], F32)
    P.op("dve", lambda e: e.tensor_copy(adth[:], adt[:]), reads=[adt], writes=[adth])
    P.op("dve", lambda e: e.tensor_copy(adthf[:], adth[:]), reads=[adth], writes=[adthf])
    P.op("dve", lambda e: e.tensor_tensor(adthf[:], adt[:], adthf[:], ALU.subtract), reads=[adt, adthf], writes=[adthf])
    P.op("dve", lambda e: e.tensor_copy(adtl[:], adthf[:]), reads=[adthf], writes=[adtl])
    if sstop < 3:
        P.scope_end()
        return
    Xtm = P.sb("Xtm", [128, 34, 512], BF16)
    Btm = P.sb("Btm", [128, 34, 256], BF16)
    for ti in range(34):
        c0 = ti * 128
        ps = PB[7] if ti % 2 == 0 else PB[6]
        for j in range(4):
            P.mm(ps[:, j * 128:(j + 1) * 128], XSb[:, j, c0:c0 + 128], G.identb[:], j == 0, j == 3, [XSb, G.identb], [ps])
        P.op("act", lambda e: e.activation(Xtm[:, ti, :], ps[:], AF.Copy), reads=[ps], wdisj=[Xtm])
        ps2 = PB[5] if ti % 2 == 0 else PB[4]
        for j in range(2):
            P.mm(ps2[:, j * 128:(j + 1) * 128], BC[:, j, c0:c0 + 128], G.identb[:], j == 0, j == 1, [BC, G.identb], [ps2])
        P.op("dve", lambda e: e.tensor_copy(Btm[:, ti, :], ps2[:, 0:256]), reads=[ps2], wdisj=[Btm])
    if sstop < 5:
        P.scope_end()
        return
    tri = P.sb("tri", [128, 2, 128], BF16)
    tmk = P.sb("tmk", [128, 2, 128], F32)
    P.dma("pool", tri[:], G.tri[:].rearrange("a p n -> p a n"), reads=[G.tri], writes=[tri])
    P.dma("sp", tmk[:], G.trimask[:].rearrange("a p n -> p a n"), reads=[G.trimask], writes=[tmk])
    ST = P.sb("ST", [128, 8, 64], F32)
    STb = P.sb("STb", [128, 8, 64], BF16)
    AdtBh = P.sb("AdtBh", [128, 8, 128], BF16)
    AdtBl = P.sb("AdtBl", [128, 8, 128], BF16)
    css = P.sb("css", [128, 8], F32)
    bl = P.sb("bl", [128, 8], F32)
    dsv = P.sb("dsv", [128, 8], F32)
    ed = P.sb("ed", [128, 8], F32)
    EB = P.sb("EB", [128, 8, 128], F32)
    bcs = P.sb("bcs", [128, 8, 128], F32)
    Gs = P.sb("Gs", [128, 2, 128], F32)
    arg = P.sb("arg", [128, 8, 128], F32)
    Ce = P.sb("Ce", [128, 8, 128], BF16)
    Mt = P.sb("Mt", [128, 8, 128], BF16)
    Xd = P.sb("Xd", [128, 8, 64], BF16)
    Xs = P.sb("Xs", [128, 8, 64], BF16)
    yst = [P.sb("ysd%d" % i, [128, 8, 128], F32) for i in range(2)]
    it = 0
    for d in range(2):
        P.op("dve", lambda e: e.memset(ST[:], 0.0), writes=[ST])
        P.op("pool", lambda e: e.memset(STb[:], 0.0), writes=[STb])
        order = ([32, 33] + list(range(32))) if d == 0 else ([33, 32] + list(range(31, -1, -1)))
        lastc = 127 if d == 0 else 0
        if sstop < 6:
            order = order[:3]
        for ti in order:
            c0 = ti * 128
            need_y = (ti < 32) or (not last)
            if "a" in sson:
                P.op("dve", lambda e: e.tensor_copy(AdtBh[:], bc_last(adth[:, ti, d * 8:(d + 1) * 8], 128)), reads=[adth], writes=[AdtBh])
                P.op("dve", lambda e: e.tensor_copy(AdtBl[:], bc_last(adtl[:, ti, d * 8:(d + 1) * 8], 128)), reads=[adtl], writes=[AdtBl])
            if "b" in sson:
                P.mm(PB[0][:, 0:8], tri[:, d, :], adth[:, ti, d * 8:(d + 1) * 8], True, False, [tri, adth], [PB[0]])
                P.mm(PB[0][:, 0:8], tri[:, d, :], adtl[:, ti, d * 8:(d + 1) * 8], False, True, [tri, adtl], [PB[0]])
            for h in range(8):
                if "c" not in sson:
                    break
                bk = PB[1 + h // 4]
                P.mm(bk[:, (h % 4) * 128:(h % 4 + 1) * 128], AdtBh[:, h, :], tri[:, d, :], h % 4 == 0, False, [AdtBh, tri], [bk])
                P.mm(bk[:, (h % 4) * 128:(h % 4 + 1) * 128], AdtBl[:, h, :], tri[:, d, :], False, h % 4 == 3, [AdtBl, tri], [bk])
            for g in range(2):
                if "d" not in sson:
                    break
                P.mm(PB[3][:, g * 128:(g + 1) * 128], BC[:, g, c0:c0 + 128], BC[:, 2 + g, c0:c0 + 128], g == 0, g == 1, [BC], [PB[3]])
            if ssub < 1:
                continue
            P.op("act", lambda e: e.activation(css[:], PB[0][:, 0:8], AF.Copy), reads=[PB[0]], writes=[css])
            for b in range(2):
                bk = PB[1 + b]
                if b == 0:
                    P.op("act", lambda e: e.activation(bcs[:, 0:4, :].rearrange("p h l -> p (h l)"), bk[:], AF.Copy),
                         reads=[bk], wdisj=[bcs])
                else:
                    P.op("dve", lambda e: e.tensor_copy(bcs[:, 4:8, :].rearrange("p h l -> p (h l)"), bk[:]),
                         reads=[bk], wdisj=[bcs])
            P.op("dve", lambda e: e.tensor_copy(bl[:], bcs[:, :, lastc]), reads=[bcs], writes=[bl])
            P.op("act", lambda e: e.activation(EB[:], bcs[:], AF.Exp), reads=[bcs], writes=[EB])
            P.op("dve", lambda e: e.tensor_tensor(arg[:], bcs[:], bc_last(css[:], 128), ALU.subtract),
                 reads=[bcs, css], writes=[arg])
            if ssub < 2:
                continue
            P.op("dve", lambda e: e.tensor_tensor(arg[:], arg[:], bc_mid(tmk[:, d, :], 8), ALU.add), reads=[arg, tmk], writes=[arg])
            P.op("act", lambda e: e.activation(arg[:], arg[:], AF.Exp), reads=[arg], writes=[arg])
            for g in range(2):
                P.op("dve", lambda e: e.tensor_tensor(Ce[:, 4 * g:4 * g + 4, :], EB[:, 4 * g:4 * g + 4, :],
                                                       bc_mid(BC[:, 2 + g, c0:c0 + 128], 4), ALU.mult),
                     reads=[EB, BC], wdisj=[Ce])
                if g == 0:
                    P.op("act", lambda e: e.activation(Gs[:].rearrange("p g l -> p (g l)"), PB[3][:, 0:256], AF.Copy),
                         reads=[PB[3]], writes=[Gs])
                P.op("dve", lambda e: e.tensor_tensor(Mt[:, 4 * g:4 * g + 4, :], arg[:, 4 * g:4 * g + 4, :],
                                                      bc_mid(Gs[:, g, :], 4), ALU.mult),
                     reads=[arg, Gs], wdisj=[Mt])
            if ssub < 3:
                continue
            P.op("dve", lambda e: e.tensor_tensor(Xd[:], Xtm[:, ti, :].rearrange("p (h q) -> p h q", h=8),
                                                   bc_last(dtv[:, ti, d * 8:(d + 1) * 8], 64), ALU.mult),
                 reads=[Xtm, dtv], writes=[Xd])
            P.op("dve", lambda e: e.tensor_tensor(dsv[:], bl[:], css[:], ALU.subtract), reads=[bl, css], writes=[dsv])
            P.op("act", lambda e: e.activation(dsv[:], dsv[:], AF.Exp), reads=[dsv], writes=[dsv])
            P.op("act", lambda e: e.activation(ed[:], bl[:], AF.Exp), reads=[bl], writes=[ed])
            P.op("dve", lambda e: e.tensor_tensor(Xs[:], Xd[:], bc_last(dsv[:], 64), ALU.mult), reads=[Xd, dsv], writes=[Xs])
            if ssub < 4:
                continue
            if need_y:
                for h in range(8):
                    bk = PB[4 + h // 4]
                    cs_ = slice((h % 4) * 128, (h % 4 + 1) * 128)
                    pr = h // 2
                    P.mm(bk[:, cs_], Xd[:, 2 * pr:2 * pr + 2, :].rearrange("p a q -> p (a q)"), Mt[:, h, :],
                         h % 4 == 0, False, [Xd, Mt], [bk])
                    P.mm(bk[:, cs_], STb[:, 2 * pr:2 * pr + 2, :].rearrange("p a q -> p (a q)"), Ce[:, h, :],
                         False, True, [STb, Ce], [bk])
                ys = yst[it % 2]
                it += 1
                for b in range(2):
                    bk = PB[4 + b]
                    P.op("act", lambda e: e.activation(ys[:, 4 * b:4 * b + 4, :], bk[:].rearrange("p (h l) -> p h l", h=4), AF.Copy),
                         reads=[bk], wdisj=[ys])
                ysv = ys[:].rearrange("p (q two) l -> p q two l", two=2)
                dst = G.ysd[d].rearrange("(q two c) t -> two c q t", two=2, c=64)
                for par in range(2):
                    P.dma("sp" if par == 0 else "act", dst[par, :, :, c0:c0 + 128], ysv[par * 64:(par + 1) * 64, :, par, :],
                          reads=[ys], wdisj=[G.ysd], owner=ys)
            if ssub < 5:
                continue
            for g in range(2):
                P.mm(PB[6][:, g * 256:(g + 1) * 256], Btm[:, ti, g * 128:(g + 1) * 128],
                     Xs[:, 4 * g:4 * g + 4, :].rearrange("p a q -> p (a q)"), g == 0, g == 1, [Btm, Xs], [PB[6]])
            P.op("dve", lambda e: e.tensor_tensor(ST[:], ST[:], bc_last(ed[:], 64), ALU.mult), reads=[ST, ed], writes=[ST])
            P.op("dve", lambda e: e.tensor_tensor(ST[:].rearrange("p h q -> p (h q)"), PB[6][:], ST[:].rearrange("p h q -> p (h q)"), ALU.add),
                 reads=[ST, PB[6]], writes=[ST])
            P.op("act", lambda e: e.activation(STb[:], ST[:], AF.Copy), reads=[ST], writes=[STb])
    P.scope_end()
    if sstop < 7:
        return
    P.scope_begin()
    sdv = P.sb("sdv", [128, 4], F32)
    snw = P.sb("snw", [128, 4], F32)
    P.dma("sp", sdv[:], G.sd[l], reads=[G.sd], writes=[sdv])
    P.dma("sp", snw[:], G.snw[l], reads=[G.snw], writes=[snw])
    yf = P.sb("yf", [128, 4, 512], F32)
    yb = P.sb("yb", [128, 4, 512], F32)
    xsl = P.sb("xsl", [128, 4, 512], F32)
    zz = P.sb("zz", [128, 4, 512], F32)
    sqv = P.sb("sqv", [128, 4, 512], BF16)
    rs = P.sb("rs", [128, 2, 512], F32)
    yo = P.sb("yo", [128, 4, 512], BF16)
    for (t0, n, s) in CHUNKS:
        if last and s == 1:
            continue
        P.dma("sp", yf[:, :, :n], G.ysd[0, :, t0:t0 + n].rearrange("(a p) t -> p a t", p=128), reads=[G.ysd], writes=[yf])
        P.dma("act", yb[:, :, :n], G.ysd[1, :, t0:t0 + n].rearrange("(a p) t -> p a t", p=128), reads=[G.ysd], writes=[yb])
        P.dma("sp", xsl[:, :, :n], G.sxs[:, t0:t0 + n].rearrange("(a p) t -> p a t", p=128), reads=[G.sxs], writes=[xsl])
        P.dma("act", zz[:, :, :n], G.pF[32 * 128:36 * 128, t0:t0 + n].rearrange("(a p) t -> p a t", p=128), reads=[G.pF], writes=[zz])
        P.op("dve", lambda e: e.tensor_tensor(yf[:, :, :n], yf[:, :, :n], yb[:, :, :n], ALU.add), reads=[yf, yb], writes=[yf])
        P.op("pool", lambda e: e.tensor_tensor(xsl[:, :, :n], xsl[:, :, :n], bc_last(sdv[:], n), ALU.mult), reads=[xsl, sdv], writes=[xsl])
        P.op("dve", lambda e: e.tensor_tensor(yf[:, :, :n], yf[:, :, :n], xsl[:, :, :n], ALU.add), reads=[yf, xsl], writes=[yf])
        P.op("act", lambda e: e.activation(zz[:, :, :n], zz[:, :, :n], AF.Silu), reads=[zz], writes=[zz])
        P.op("dve", lambda e: e.tensor_tensor(yf[:, :, :n], yf[:, :, :n], zz[:, :, :n], ALU.mult), reads=[yf, zz], writes=[yf])
        P.op("act", lambda e: e.activation(sqv[:, :, :n], yf[:, :, :n], AF.Square), reads=[yf], writes=[sqv])
        for g in range(2):
            ps = G.pb()
            for j in range(2):
                P.mm(ps[:, :n], G.ones[:], sqv[:, 2 * g + j, :n], j == 0, j == 1, [G.ones, sqv], [ps])
            P.op("dve", lambda e: e.tensor_scalar(rs[:, g, :n], ps[:, :n], 1.0 / 256, 1e-6, ALU.mult, ALU.add), reads=[ps], wdisj=[rs])
        P.op("act", lambda e: e.activation(rs[:, :, :n], rs[:, :, :n], AF.Sqrt), reads=[rs], writes=[rs])
        P.op("dve", lambda e: e.reciprocal(rs[:, :, :n], rs[:, :, :n]), reads=[rs], writes=[rs])
        for j in range(4):
            P.op("dve", lambda e: e.scalar_tensor_tensor(
                yo[:, j, :n], yf[:, j, :n], snw[:, j:j + 1], rs[:, j // 2, :n], ALU.mult, ALU.mult),
                reads=[yf, snw, rs], wdisj=[yo])
        P.dma("sp", G.yT[1536:2048, t0:t0 + n].rearrange("(a p) t -> p a t", p=128), yo[:, :, :n], reads=[yo], wdisj=[G.yT], owner=yo)
    P.scope_end()


def hy_mlp(G, l, hA, NP):
    P = G.P
    P.scope_begin()
    zt = P.sb("zt", [128, NP], F32)
    P.dma("sp", zt[:], G.hz[:], reads=[G.hz], writes=[zt])
    hbv = P.sb("hbv", [128, 4], F32)
    P.dma("sp", hbv[:], G.hb[l], reads=[G.hb], writes=[hbv])
    wm = P.sb("wm", [128, 3, 128], F32)
    P.dma("sp", wm[:], G.hw123[l].rearrange("a p n -> p a n"), reads=[G.hw123], writes=[wm])
    hB = P.sb("hB", [128, NP], F32)
    ttm = P.sb("ttm", [128, 512], F32)
    src = zt
    for li in range(3):
        dst = hA if li % 2 == 0 else hB
        for t0 in range(0, NP, 512):
            n = min(512, NP - t0)
            ps = G.pb()
            P.mm(ps[:, :n], wm[:, li, :], src[:, t0:t0 + n], True, True, [wm, src], [ps])
            P.op("dve", lambda e: e.tensor_scalar(dst[:, t0:t0 + n], ps[:, :n], hbv[:, li:li + 1], hbv[:, 3:4], ALU.add, ALU.mult),
                 reads=[ps, hbv], wdisj=[dst])
            P.op("act", lambda e: e.activation(dst[:, t0:t0 + n], dst[:, t0:t0 + n], AF.Sin, scale=1.0 / 9.0), reads=[dst], wdisj=[dst])
            for _ in range(2):
                P.op("dve", lambda e: e.tensor_tensor(ttm[:, :n], dst[:, t0:t0 + n], dst[:, t0:t0 + n], ALU.mult), reads=[dst], writes=[ttm])
                P.op("dve", lambda e: e.tensor_scalar(ttm[:, :n], ttm[:, :n], -4.0, 3.0, ALU.mult, ALU.add), reads=[ttm], writes=[ttm])
                P.op("dve", lambda e: e.tensor_tensor(dst[:, t0:t0 + n], dst[:, t0:t0 + n], ttm[:, :n], ALU.mult), reads=[dst, ttm], wdisj=[dst])
        src = dst
    P.scope_end()


def phase_hyena(G, l, last):
    P = G.P
    PB = G.PB
    P.scope_begin()
    NP = TT + T
    KV = 8320
    hA = P.sb("hA", [128, NP], F32)
    hy_mlp(G, l, hA, NP)
    h3 = hA
    w4 = P.sb("w4", [128, 2048], F32)
    P.dma("sp", w4[:], G.hw4p[l], reads=[G.hw4p], writes=[w4])
    sw = P.sb("sw", [128, 12, 3], F32)
    sbv = P.sb("sbv", [128, 12], F32)
    skp = P.sb("skp", [128, 2, 4], F32)
    P.dma("sp", sw[:], G.hsw[l], reads=[G.hsw], writes=[sw])
    P.dma("sp", sbv[:], G.hsb[l], reads=[G.hsb], writes=[sbv])
    P.dma("sp", skp[:], G.hskip[l], reads=[G.hskip], writes=[skp])
    revb = P.sb("revb", [128, 128], BF16)
    P.dma("pool", revb[:], G.rev[:], reads=[G.rev], writes=[revb])
    pin = P.sb("pin", [128, T + 2], F32)
    ux = [P.sb("ux%d" % i, [128, T], F32) for i in range(3)]
    kern = P.sb("kern", [128, 2, TC], F32)
    dch = [P.sb("dch%d" % i, [128, 512], F32) for i in range(2)]
    acc = P.sb("acc", [128, TC], F32)
    k0 = P.sb("k0", [128, 1], F32)
    yo = P.sb("yo", [128, T], BF16)
    kvs = P.sb("kvs", [128, KV], BF16)
    curb = P.sb("curb", [128, T], BF16)
    Utm = P.sb("Utm", [128, 32, 128], BF16)
    Urev = P.sb("Urev", [128, 32, 128], BF16)
    band = [P.sb("band%d" % i, [128, 8192], BF16) for i in range(2)]
    Yt2 = P.sb("Yt2", [128, 32, 128], BF16)
    P.op("pool", lambda e: e.memset(kvs[:], 0.0), writes=[kvs])
    import os
    nct = int(os.environ.get("HY_CT", "4"))
    seqs = [(0, T)] if last else [(0, T), (T, TC)]
    it = 0
    dci = 0
    for ct in range(nct):
        for (q0, L) in seqs:
            for i in range(3):
                tile = 12 + 4 * i + ct
                wi = 4 * i + ct
                P.op("pool", lambda e: e.memset(pin[:, 0:L + 2], 0.0), writes=[pin])
                P.dma("sp", pin[:, 1:L + 1], G.pF[tile * 128:(tile + 1) * 128, q0:q0 + L], reads=[G.pF], writes=[pin])
                u = ux[i]
                P.op("dve", lambda e: e.tensor_scalar(u[:, :L], pin[:, 1:L + 1], sw[:, wi, 1:2], sbv[:, wi:wi + 1], ALU.mult, ALU.add),
                     reads=[pin, sw, sbv], writes=[u])
                P.op("dve", lambda e: e.scalar_tensor_tensor(u[:, :L], pin[:, 0:L], sw[:, wi, 0:1], u[:, :L], ALU.mult, ALU.add),
                     reads=[pin, sw, u], writes=[u])
                P.op("dve", lambda e: e.scalar_tensor_tensor(u[:, :L], pin[:, 2:L + 2], sw[:, wi, 2:3], u[:, :L], ALU.mult, ALU.add),
                     reads=[pin, sw, u], writes=[u])
            cur = ux[2]
            for o in range(2):
                cf = o * 1024 + ct * 128
                cb_ = o * 1024 + 512 + ct * 128
                if L == TC:
                    P.dma("sp", dch[0][:, :L], G.hdec[ct * 128:(ct + 1) * 128, q0:q0 + L], reads=[G.hdec], writes=[dch[0]])
                    for dr, c0 in ((0, cf), (1, cb_)):
                        ps = G.pb()
                        P.mm(ps[:, :L], w4[:, c0:c0 + 128], h3[:, q0:q0 + L], True, True, [w4, h3], [ps])
                        P.op("dve", lambda e: e.tensor_tensor(kern[:, dr, :L], ps[:, :L], dch[0][:, :L], ALU.mult),
                             reads=[ps, dch[0]], wdisj=[kern])
                    P.op("dve", lambda e: e.tensor_tensor(k0[:], kern[:, 0, 0:1], skp[:, o, ct:ct + 1], ALU.add), reads=[kern, skp], writes=[k0])
                    P.op("dve", lambda e: e.tensor_scalar(acc[:, :L], cur[:, :L], k0[:, 0:1], None, ALU.mult), reads=[cur, k0], writes=[acc])
                    for tau in range(1, L):
                        P.op("dve", lambda e: e.scalar_tensor_tensor(acc[:, tau:L], cur[:, 0:L - tau], kern[:, 0, tau:tau + 1], acc[:, tau:L],
                                                                    ALU.mult, ALU.add), reads=[cur, kern, acc], writes=[acc])
                        P.op("dve", lambda e: e.scalar_tensor_tensor(acc[:, 0:L - tau], cur[:, tau:L], kern[:, 1, tau:tau + 1], acc[:, 0:L - tau],
                                                                    ALU.mult, ALU.add), reads=[cur, kern, acc], writes=[acc])
                    if o == 0:
                        P.op("dve", lambda e: e.tensor_tensor(ux[2][:, :L], ux[0][:, :L], acc[:, :L], ALU.mult), reads=[ux[0], acc], writes=[ux[2]])
                    else:
                        P.op("dve", lambda e: e.tensor_tensor(yo[:, :L], ux[1][:, :L], acc[:, :L], ALU.mult), reads=[ux[1], acc], writes=[yo])
                    continue
                for a in range(0, T, 512):
                    dc = dch[dci % 2]
                    dci += 1
                    P.dma("act", dc[:], G.hdecr[ct * 128:(ct + 1) * 128, a:a + 512], reads=[G.hdecr], writes=[dc])
                    ps = G.pb()
                    P.mm(ps[:], w4[:, cb_:cb_ + 128], h3[:, TT + a:TT + a + 512], True, True, [w4, h3], [ps])
                    P.op("dve", lambda e: e.tensor_tensor(kvs[:, a:a + 512], ps[:], dc[:], ALU.mult), reads=[ps, dc], wdisj=[kvs])
                for a in range(0, T, 512):
                    dc = dch[dci % 2]
                    dci += 1
                    P.dma("act", dc[:], G.hdec[ct * 128:(ct + 1) * 128, a:a + 512], reads=[G.hdec], writes=[dc])
                    ps = G.pb()
                    P.mm(ps[:], w4[:, cf:cf + 128], h3[:, a:a + 512], True, True, [w4, h3], [ps])
                    if a == 0:
                        P.op("dve", lambda e: e.tensor_tensor(dc[:, 0:1], dc[:, 0:1], ps[:, 0:1], ALU.mult), reads=[ps, dc], writes=[dc])
                        P.op("dve", lambda e: e.tensor_tensor(k0[:], dc[:, 0:1], skp[:, o, ct:ct + 1], ALU.add), reads=[dc, skp], writes=[k0])
                        P.op("dve", lambda e: e.tensor_tensor(kvs[:, 4096:4096 + 511], ps[:, 1:512], dc[:, 1:512], ALU.mult), reads=[ps, dc], wdisj=[kvs])
                        P.op("act", lambda e: e.activation(kvs[:, 4095:4096], k0[:], AF.Copy), reads=[k0], wdisj=[kvs])
                    else:
                        P.op("dve", lambda e: e.tensor_tensor(kvs[:, 4095 + a:4095 + a + 512], ps[:], dc[:], ALU.mult), reads=[ps, dc], wdisj=[kvs])
                kvd = G.kvd[it % 2]
                it += 1
                P.dma("sp", kvd[:], kvs[:], reads=[kvs], writes=[kvd], owner=kvs)
                P.op("act", lambda e: e.activation(curb[:], cur[:], AF.Copy), reads=[cur], writes=[curb])
                for jb in range(8):
                    ps = G.pb()
                    for q in range(4):
                        J = jb * 4 + q
                        P.mm(ps[:, q * 128:(q + 1) * 128], curb[:, J * 128:(J + 1) * 128], G.identb[:], q == 0, q == 3, [curb, G.identb], [ps])
                    evac(G, jb, Utm[:, jb * 4:jb * 4 + 4, :].rearrange("p j c -> p (j c)"), ps[:], [ps], [], wdisj=[Utm])
                for jb in range(8):
                    ps = G.pb()
                    P.mm(ps[:], revb[:], Utm[:, jb * 4:jb * 4 + 4, :].rearrange("p j c -> p (j c)"), True, True, [revb, Utm], [ps])
                    evac(G, jb + 1, Urev[:, jb * 4:jb * 4 + 4, :].rearrange("p j c -> p (j c)"), ps[:], [ps], [], wdisj=[Urev])
                kt = kvd.t.tensor
                for c in range(128):
                    bd = band[c % 2]
                    P.dma("sp" if c % 2 == 0 else "act", bd[:], bass.AP(kt, kvd.t.offset + c * KV, [[1, 128], [1, 8192]]),
                          reads=[kvd], writes=[bd])
                    cc = c % 16
                    bk = PB[(c // 16) % 2]
                    for dq in [31] + [x for x in range(63) if x != 31]:
                        d = dq - 31
                        J0 = max(0, -d)
                        N = 32 - abs(d)
                        I0 = J0 + d
                        P.mm(bk[:, cc * 32 + I0:cc * 32 + I0 + N], bd[:, 128 * dq:128 * dq + 128], Urev[:, J0:J0 + N, c],
                             cc == 0 and d == 0, False, [bd, Urev], [bk])
                    if cc == 15:
                        c0 = c - 15
                        evac(G, c // 16, Yt2[:, :, c0:c0 + 16], bk[:].rearrange("p (c i) -> p i c", c=16), [bk], [], wdisj=[Yt2])
                for ib in range(8):
                    ps = G.pb()
                    for q in range(4):
                        I = ib * 4 + q
                        P.mm(ps[:, q * 128:(q + 1) * 128], Yt2[:, I, :], G.identb[:], q == 0, q == 3, [Yt2, G.identb], [ps])
                    cs_ = slice(ib * 512, (ib + 1) * 512)
                    if o == 0:
                        P.op("dve", lambda e: e.tensor_tensor(ux[2][:, cs_], ps[:], ux[0][:, cs_], ALU.mult), reads=[ps, ux[0]], wdisj=[ux[2]])
                    else:
                        P.op("dve", lambda e: e.tensor_tensor(yo[:, cs_], ps[:], ux[1][:, cs_], ALU.mult), reads=[ps, ux[1]], wdisj=[yo])
            P.dma("sp", G.yT[512 + ct * 128:512 + (ct + 1) * 128, q0:q0 + L], yo[:, :L], reads=[yo], wdisj=[G.yT], owner=yo)
    P.scope_end()


_CACHE = {}


def kernel(**inputs):
    inp = {k: np.asarray(v) for k, v in inputs.items()}
    if "nc" not in _CACHE:
        _CACHE["nc"] = build_program()
    nc = _CACHE["nc"]
    sh = _prep_shared(inp)
    in_maps = []
    for c in range(8):
        m = dict(sh)
        m.update(_prep_core(inp, c % 4))
        in_maps.append(m)
    res = run_bass_kernel_spmd(nc, in_maps, core_ids=list(range(8)))
    outs = []
    for b in range(4):
        o = np.asarray(res.results[b]["out"])
        outs.append(o.transpose(2, 0, 1).reshape(T, D))
    return np.stack(outs).astype(np.float32)
```

```python
import math
import numpy as np
import concourse.bass as bass
import concourse.mybir as mybir
from concourse.bass_utils import run_bass_kernel_spmd

F32 = mybir.dt.float32
BF16 = mybir.dt.bfloat16
AF = mybir.ActivationFunctionType
ALU = mybir.AluOpType
AX = mybir.AxisListType


class Buf:
    __slots__ = ("t", "name", "w", "r", "dsem", "dcnt")

    def __init__(self, t, name):
        self.t = t
        self.name = name
        self.w = {}
        self.r = {}
        self.dsem = None
        self.dcnt = 0

    def __getitem__(self, idx):
        return self.t[idx]


class _Rec:
    def __init__(self):
        self.call = None

    def __getattr__(self, name):
        def f(*a, **k):
            self.call = (name, a, k)
            return self
        return f


class Prog:
    ENG = ("pe", "act", "dve", "pool", "sp")

    def __init__(self, nc):
        self.nc = nc
        self.q = {e: [] for e in self.ENG}
        self.cnt = {e: 0 for e in self.ENG}
        self.sem = {}
        self.known = {e: {} for e in self.ENG}
        self.ctx = []
        self.perm = []
        self.nsem = 0
        self.out_tokens = {}
        self.scopes = []
        self.free_dsems = []
        self.all_dsems = []
        self.semcount = {}
        self.scope_bufs = []
        for e in self.ENG:
            self.sem[e] = self._newsem("s_" + e)

    def _newsem(self, name):
        g = self.nc.semaphore(name)
        s = g.__enter__()
        self.perm.append(g)
        self.nsem += 1
        return s

    def sb(self, name, shape, dt):
        self.uid = getattr(self, "uid", 0) + 1
        g = self.nc.sbuf_tensor(name + "_%d" % self.uid, list(shape), dt)
        t = g.__enter__()
        self.ctx.append(g)
        b = Buf(t, name)
        self.scope_bufs.append(b)
        return b

    def _get_dsem(self, owner):
        if owner.dsem is None:
            if self.free_dsems:
                owner.dsem = self.free_dsems.pop()
            else:
                owner.dsem = self._newsem_perm("d%d" % len(self.all_dsems))
                self.all_dsems.append(owner.dsem)
                self.semcount[owner.dsem] = 0
        return owner.dsem

    def _newsem_perm(self, name):
        g = self.nc.semaphore(name)
        s = g.__enter__()
        self.perm.append(g)
        return s

    def scope_begin(self):
        self.scopes.append((len(self.ctx), len(self.scope_bufs)))

    def barrier(self):
        for e in self.ENG:
            waits = []
            kn = self.known[e]
            for e2 in self.ENG:
                if e2 != e and self.cnt[e2] > kn.get(self.sem[e2], 0):
                    kn[self.sem[e2]] = self.cnt[e2]
                    waits.append((self.sem[e2], self.cnt[e2]))
            for s_ in self.all_dsems:
                v = self.semcount[s_]
                if v > kn.get(s_, 0):
                    kn[s_] = v
                    waits.append((s_, v))
            if waits:
                self.q[e].append((waits, None, None, 0))

    def scope_end(self):
        self.barrier()
        n, nb = self.scopes.pop()
        for b in self.scope_bufs[nb:]:
            if b.dsem is not None:
                self.free_dsems.append(b.dsem)
                b.dsem = None
        del self.scope_bufs[nb:]
        while len(self.ctx) > n:
            g = self.ctx.pop()
            g.__exit__(None, None, None)

    def ps(self, name, shape, dt=F32):
        g = self.nc.psum_tensor(name, list(shape), dt)
        t = g.__enter__()
        self.ctx.append(g)
        return Buf(t, name)

    def dram(self, name, shape, dt, kind="Internal"):
        t = self.nc.dram_tensor(name, list(shape), dt, kind=kind)
        return Buf(t.ap(), name)

    def mm(self, out, lhsT, rhs, start, stop, reads, writes):
        return self.op("pe", lambda e: e.matmul(out, lhsT, rhs, start=start, stop=stop, skip_group_check=True),
                       reads=reads, writes=writes)

    def _waits(self, eng, reads, writes, wdisj=()):
        need = {}
        for b in wdisj:
            for s, v in b.r.items():
                if need.get(s, 0) < v:
                    need[s] = v
        for b in reads:
            for s, v in b.w.items():
                if need.get(s, 0) < v:
                    need[s] = v
        for b in writes:
            for s, v in b.w.items():
                if need.get(s, 0) < v:
                    need[s] = v
            for s, v in b.r.items():
                if need.get(s, 0) < v:
                    need[s] = v
        out = []
        kn = self.known[eng]
        for s, v in need.items():
            if eng == "pe" and s is self.sem["pe"]:
                continue
            if kn.get(s, 0) >= v:
                continue
            kn[s] = v
            out.append((s, v))
        return out

    def _record(self, tok, reads, writes, wdisj=()):
        s, v = tok
        for b in wdisj:
            if b.r:
                b.w = {s: v}
                b.r = {}
            elif b.w.get(s, 0) < v:
                b.w[s] = v
        for b in reads:
            if b.r.get(s, 0) < v:
                b.r[s] = v
        for b in writes:
            b.w = {s: v}
            b.r = {}

    def op(self, eng, fn, reads=(), writes=(), wdisj=()):
        rec = _Rec()
        fn(rec)
        name_, a_, k_ = rec.call

        def fn(e, name_=name_, a_=a_, k_=k_):
            return getattr(e, name_)(*a_, **k_)
        waits = self._waits(eng, reads, writes, wdisj)
        self.cnt[eng] += 1
        tok = (self.sem[eng], self.cnt[eng])
        self.q[eng].append((waits, fn, tok[0], 1))
        self._record(tok, reads, writes, wdisj)
        return tok

    def dma(self, eng, out_ap, in_ap, reads=(), writes=(), wdisj=(), owner=None, final=False):
        if owner is None:
            owner = writes[0] if writes else reads[0]
        ds = self._get_dsem(owner)
        waits = self._waits(eng, reads, writes, wdisj)
        kn = self.known[eng]
        cur = self.semcount[ds]
        if cur and kn.get(ds, 0) < cur:
            kn[ds] = cur
            waits.append((ds, cur))
        self.semcount[ds] = cur + 16
        tok = (ds, cur + 16)

        def fn(e, o=out_ap, i=in_ap):
            return e.dma_start(out=o, in_=i)
        self.q[eng].append((waits, fn, tok[0], 16))
        self._record(tok, reads, writes, wdisj)
        if final:
            self.out_tokens[tok[0]] = tok[1]
        return tok

    def simulate_sync(self):
        pos = {e: 0 for e in self.ENG}
        val = {}
        progress = True
        while progress:
            progress = False
            for e in self.ENG:
                q = self.q[e]
                while pos[e] < len(q):
                    waits, fn, s_, inc = q[pos[e]]
                    if any(val.get(id(ws), 0) < wv for ws, wv in waits):
                        break
                    if fn is not None:
                        val[id(s_)] = val.get(id(s_), 0) + inc
                    pos[e] += 1
                    progress = True
        stuck = {e: (pos[e], len(self.q[e])) for e in self.ENG if pos[e] < len(self.q[e])}
        if not stuck:
            return None
        rep = {}
        for e, (p, n) in stuck.items():
            waits = self.q[e][p][0]
            rep[e] = (p, n, [(getattr(ws, "name", str(ws)), wv, val.get(id(ws), 0)) for ws, wv in waits])
        return rep

    def emit(self):
        nc = self.nc
        fin = list(self.out_tokens.items())
        q = self.q
        with nc.Block() as block:
            def run(e, lst, extra=()):
                for waits, fn, s, inc in lst:
                    for ws, wv in waits:
                        e.wait_ge(ws, wv)
                    if fn is not None:
                        fn(e).then_inc(s, inc)
                for ws, wv in extra:
                    e.wait_ge(ws, wv)

            @block.sync
            def _(e):
                run(e, q["sp"], fin)

            @block.tensor
            def _(e):
                run(e, q["pe"])

            @block.scalar
            def _(e):
                run(e, q["act"])

            @block.vector
            def _(e):
                run(e, q["dve"])

            @block.gpsimd
            def _(e):
                run(e, q["pool"])
        for g in reversed(self.ctx):
            g.__exit__(None, None, None)
        for g in reversed(self.perm):
            g.__exit__(None, None, None)


D = 2048
KC = 16
T = 4096
TC = 256
TT = T + TC
NL = 4
NFM = 44
NTM = 656
CHUNKS = [(i * 512, 512, 0) for i in range(8)] + [(T, TC, 1)]
NEGM = -30000.0


def _fm_cols():
    cols = []
    rp = np.concatenate([np.arange(16, 32), np.arange(0, 16), np.arange(48, 64), np.arange(32, 48)])
    qa = np.arange(512)
    qap = (np.arange(8)[:, None] * 64 + rp[None]).reshape(-1)
    cols += [qa, qap]
    for perm in (False, True):
        for g in range(2):
            base = 512 + g * 64 + (rp if perm else np.arange(64))
            cols.append(np.concatenate([base, base]))
    cols.append(768 + np.arange(1536))
    cols.append(2304 + np.arange(1024))
    cols.append(3840 + np.arange(512))
    cols.append(4352 + np.arange(1024))
    c = np.concatenate(cols)
    assert c.shape[0] == NFM * 128
    return c


def _tm_cols():
    return np.concatenate([640 + np.arange(128), 3328 + np.arange(512), 5376 + np.arange(16)])


def _rope_tables():
    t = np.arange(T)
    row, col = t // 64, t % 64
    inv = 10000.0 ** (-np.arange(16, dtype=np.float64) / 16)
    d = np.arange(64)
    pos = np.where(d[:, None] < 32, row[None], col[None]).astype(np.float64)
    ang = pos * inv[d % 16][:, None]
    cos = np.cos(ang)
    sin = np.sin(ang) * np.where((d % 32) < 16, -1.0, 1.0)[:, None]
    cos2 = np.concatenate([cos, cos], 0)
    sin2 = np.concatenate([sin, sin], 0)
    return np.stack([cos2 * 0.125, sin2 * 0.125, cos2, sin2]).astype(np.float32)


def _na_cases():
    cases = [(10, 10 + dk) for dk in range(-2, 3)]
    for R2 in (0, 1):
        cases += [(R2, K2) for K2 in range(4)]
    for R2 in (30, 31):
        cases += [(R2, K2) for K2 in range(28, 32)]
    return cases


def _na_case_id(R2, K2):
    if 2 <= R2 <= 29:
        return K2 - R2 + 2
    if R2 < 2:
        return 5 + R2 * 4 + K2
    return 13 + (R2 - 30) * 4 + (K2 - 28)


def _na_tables(rpb):
    cases = _na_cases()
    kk = np.arange(128)
    qq = np.arange(128)
    out = np.empty((len(cases), 2, 128, 512), np.float32)
    for ci, (R2, K2) in enumerate(cases):
        kr = 2 * K2 + kk // 64
        ck = kk % 64
        r = 2 * R2 + qq // 64
        cq = qq % 64
        rstart = np.clip(r - 4, 0, 56)
        cstart = np.clip(cq - 8, 0, 48)
        vr = (kr[:, None] >= rstart[None]) & (kr[:, None] < rstart[None] + 8)
        vc = (ck[:, None] >= cstart[None]) & (ck[:, None] < cstart[None] + 16)
        roff = np.clip(kr[:, None] - r[None] + 7, 0, 14)
        coff = np.clip(ck[:, None] - cq[None], -15, 15) + 15
        valid = vr & vc
        for h in range(8):
            bias = rpb[h][roff, coff]
            out[ci, h // 4, :, (h % 4) * 128:(h % 4 + 1) * 128] = np.where(valid, bias, np.float32(NEGM))
    return out


def _hy_tables(L):
    t = np.linspace(0.0, 1.0, L, dtype=np.float32)[:, None]
    f = np.linspace(1e-4, 15, 16, dtype=np.float32)[None]
    wpos = (2.0 * math.pi * np.arange(L, dtype=np.float32)[:, None] / L).astype(np.float32)
    z = np.concatenate([t, np.cos(f * wpos), -np.sin(f * wpos)], axis=-1).astype(np.float32)
    max_decay = math.log(1e-2) / 0.3
    min_decay = math.log(1e-2) / 1.5
    deltas = np.linspace(min_decay, max_decay, 512, dtype=np.float32)
    decay = np.exp(-t * np.abs(deltas)[None]).astype(np.float32)
    return np.ascontiguousarray(z.T), np.ascontiguousarray(decay.T)


def _prep_shared(inp):
    sh = {}
    f32 = np.float32
    sh["adaw"] = np.ascontiguousarray(inp["ada_w"].reshape(NL, 128, 16, 6 * D).transpose(0, 2, 1, 3))
    sh["adab"] = np.ascontiguousarray(inp["ada_b"])
    sh["nrm"] = np.ascontiguousarray(np.stack([inp["norm_mix"], inp["norm_mlp"]], 1).reshape(NL, 2, 128, 16))
    sh["fnorm"] = np.ascontiguousarray(inp["final_norm"].reshape(128, 16))
    w_in = inp["w_in"].reshape(NL, 128, 16, -1)
    fm = w_in[..., _fm_cols()].reshape(NL, 128, 16, NFM, 128)
    sh["win_fm"] = np.ascontiguousarray(fm.transpose(0, 3, 1, 2, 4))
    sh["win_tm"] = np.ascontiguousarray(w_in[..., _tm_cols()])
    wo = inp["w_out"].reshape(NL, 16, 128, 128, 16)
    sh["wout"] = np.ascontiguousarray(wo.transpose(0, 4, 2, 1, 3))
    w1 = inp["mlp_w1"].reshape(NL, 128, 16, 64, 128)
    sh["w1"] = np.ascontiguousarray(w1.transpose(0, 3, 1, 2, 4))
    w2 = inp["mlp_w2"].reshape(NL, 64, 128, 128, 16)
    sh["w2"] = np.ascontiguousarray(w2.transpose(0, 4, 2, 1, 3))
    sh["rope"] = _rope_tables()
    kk = np.arange(128)[:, None]
    qq = np.arange(128)[None]
    mp = np.where(qq <= kk, 0.0, NEGM).astype(f32)
    mn = np.where(kk <= qq, 0.0, NEGM).astype(f32)
    sh["cmask"] = np.stack([np.tile(mp, (1, 4)), np.tile(mn, (1, 4))]).astype(f32)
    sh["sink"] = np.ascontiguousarray(inp["attn_sink"])
    sh["nbt"] = np.stack([_na_tables(inp["na_rpb"][l]) for l in range(NL)])
    sh["scw"] = np.ascontiguousarray(inp["ssm_conv_w"].reshape(NL, 3, 8, 128).transpose(0, 3, 2, 1))
    sh["scb"] = np.ascontiguousarray(inp["ssm_conv_b"].reshape(NL, 8, 128).transpose(0, 2, 1))
    sh["sdtb"] = np.ascontiguousarray(inp["ssm_dt_bias"].reshape(NL, 16))
    sh["salog"] = np.ascontiguousarray(inp["ssm_a_log"].reshape(NL, 16))
    sh["sd"] = np.ascontiguousarray(np.repeat(inp["ssm_d"], 64, axis=1).reshape(NL, 4, 128).transpose(0, 2, 1))
    sh["snw"] = np.ascontiguousarray(inp["ssm_norm"].reshape(NL, 4, 128).transpose(0, 2, 1))
    tri = np.triu(np.ones((128, 128), f32))
    sh["tri"] = np.stack([tri, tri.T]).astype(f32)
    mk = np.where(tri > 0, 0.0, NEGM).astype(f32)
    sh["trimask"] = np.stack([mk, mk.T]).astype(f32)
    sh["ident"] = np.eye(128, dtype=f32)
    sh["hsw"] = np.ascontiguousarray(inp["hy_short_w"].reshape(NL, 3, 12, 128).transpose(0, 3, 2, 1))
    sh["hsb"] = np.ascontiguousarray(inp["hy_short_b"].reshape(NL, 12, 128).transpose(0, 2, 1))
    w123 = np.zeros((NL, 3, 128, 128), f32)
    w123[:, 0, :33, :64] = inp["hy_w1"]
    w123[:, 1, :64, :64] = inp["hy_w2"]
    w123[:, 2, :64, :64] = inp["hy_w3"]
    sh["hw123"] = w123
    w4p = np.zeros((NL, 128, 2048), f32)
    w4p[:, :64, :] = inp["hy_w4"]
    sh["hw4p"] = w4p
    hb = np.zeros((NL, 128, 4), f32)
    hb[:, :64, :] = np.stack([inp["hy_b1"], inp["hy_b2"], inp["hy_b3"], inp["hy_freq"]], 2)
    sh["hb"] = hb
    sh["hskip"] = np.ascontiguousarray(inp["hy_skip"].reshape(NL, 2, 4, 128).transpose(0, 3, 1, 2))
    zl, dl = _hy_tables(T)
    zc, dc = _hy_tables(TC)
    hz = np.zeros((128, TT + T), f32)
    hz[:33] = np.concatenate([zl, zc, zl[:, ::-1]], 1)
    sh["hz"] = hz
    sh["hdecr"] = np.ascontiguousarray(dl[:, ::-1])
    sh["rev"] = np.ascontiguousarray(np.eye(128, dtype=f32)[::-1])
    sh["hdec"] = np.ascontiguousarray(np.concatenate([dl, dc], 1))
    return sh


def _prep_core(inp, b):
    xcat = np.concatenate([inp["x"][b], inp["ctx"][b]], 0)
    xT = np.ascontiguousarray(xcat.reshape(TT, 128, 16).transpose(1, 2, 0))
    cv = np.ascontiguousarray(np.stack([inp["c"][b].reshape(128, 16), inp["c_ctx"].reshape(128, 16)], 2))
    return {"xT": xT, "cv": cv}


def bcast(ap, shape):
    return ap.broadcast_to(list(shape))


class Ctx:
    pass


def build_program(nlayers=NL, dbg=False, stop=99, attn=True):
    nc = bass.Bass("TRN2", target_bir_lowering=False)
    P = Prog(nc)
    G = Ctx()
    G.P = P
    G.dbg = dbg
    di = lambda n, s, dt=F32: P.dram(n, s, dt, kind="ExternalInput")
    G.xT = di("xT", [128, KC, TT])
    G.cv = di("cv", [128, KC, 2])
    G.adaw = di("adaw", [NL, KC, 128, 6 * D])
    G.adab = di("adab", [NL, 6 * D])
    G.nrm = di("nrm", [NL, 2, 128, KC])
    G.fnorm = di("fnorm", [128, KC])
    G.win_fm = di("win_fm", [NL, NFM, 128, KC, 128])
    G.win_tm = di("win_tm", [NL, 128, KC, NTM])
    G.wout = di("wout", [NL, 16, 128, 16, 128])
    G.w1 = di("w1", [NL, 64, 128, KC, 128])
    G.w2 = di("w2", [NL, 16, 128, 64, 128])
    G.rope = di("rope", [4, 128, T])
    G.cmask = di("cmask", [2, 128, 512])
    G.sink = di("sink", [NL, 8])
    G.nbt = di("nbt", [NL, 21, 2, 128, 512])
    G.scw = di("scw", [NL, 128, 8, 3])
    G.scb = di("scb", [NL, 128, 8])
    G.sdtb = di("sdtb", [NL, 16])
    G.salog = di("salog", [NL, 16])
    G.sd = di("sd", [NL, 128, 4])
    G.snw = di("snw", [NL, 128, 4])
    G.tri = di("tri", [2, 128, 128])
    G.trimask = di("trimask", [2, 128, 128])
    G.ident = di("ident", [128, 128])
    G.hsw = di("hsw", [NL, 128, 12, 3])
    G.hsb = di("hsb", [NL, 128, 12])
    G.hw123 = di("hw123", [NL, 3, 128, 128])
    G.hw4p = di("hw4p", [NL, 128, 2048])
    G.hb = di("hb", [NL, 128, 4])
    G.hskip = di("hskip", [NL, 128, 2, 4])
    G.hz = di("hz", [128, TT + T])
    G.hdecr = di("hdecr", [512, T])
    G.rev = di("rev", [128, 128])
    G.kvd = [P.dram("kvd%d" % i, [128, 8320], BF16) for i in range(2)]
    G.hdec = di("hdec", [512, TT])
    G.out = P.dram("out", [128, KC, T], F32, kind="ExternalOutput")
    G.xs = P.dram("xs", [128, KC, TT], F32)
    G.modv = P.dram("modv", [NL, 2, 6 * D], F32)
    G.pF = P.dram("pF", [NFM * 128, TT], F32)
    G.pT = P.dram("pT", [TT, NTM], F32)
    G.yT = P.dram("yT", [2048, TT], BF16)
    G.sxs = P.dram("sxs", [512, TT], F32)
    G.ysd = P.dram("ysd", [2, 512, TT], F32)
    G.wob = P.dram("wob", [16, 128, 16, 128], BF16)
    G.w1c = P.dram("w1c", [64, 128, KC, 128], BF16)
    G.w2c = P.dram("w2c", [16, 128, 64, 128], BF16)
    if dbg:
        G.d_pF = P.dram("d_pF", [NFM * 128, TT], F32, kind="ExternalOutput")
        G.d_pT = P.dram("d_pT", [TT, NTM], F32, kind="ExternalOutput")
        G.d_yT = P.dram("d_yT", [2048, TT], BF16, kind="ExternalOutput")
        G.d_xs = P.dram("d_xs", [128, KC, TT], F32, kind="ExternalOutput")
        G.d_mod = P.dram("d_mod", [NL, 2, 6 * D], F32, kind="ExternalOutput")
    G.PB = [P.ps("pb%d" % i, [128, 512]) for i in range(8)]
    G.pbi = 0

    def pb():
        G.pbi = (G.pbi + 1) % 8
        return G.PB[G.pbi]
    G.pb = pb
    G.pb6i = 0

    def pb6():
        G.pb6i = (G.pb6i + 1) % 6
        return G.PB[G.pb6i]
    G.pb6 = pb6
    G.ones = P.sb("ones", [128, 128], BF16)
    P.op("dve", lambda e: e.memset(G.ones[:], 1.0), writes=[G.ones])
    G.identb = P.sb("identb", [128, 128], BF16)
    P.dma("pool", G.identb[:], G.ident[:], reads=[G.ident], writes=[G.identb])
    G.modsb = P.sb("modsb", [128, 2, 6, KC], F32)
    G.amod = P.sb("amod", [128, 2, 2, KC], F32)
    G.nrmsb = P.sb("nrmsb", [128, 2, KC], F32)

    phase_mod(G)
    P.dma("sp", G.xs[:], G.xT[:], reads=[G.xT], writes=[G.xs], owner=G.ones)
    for l in range(nlayers):
        last = (l == NL - 1)
        if stop < 1:
            break
        load_mod(G, l)
        for half in (CHUNKS[0:4], CHUNKS[4:9]):
            phase_inproj(G, l, half)
        if stop < 2:
            break
        if dbg and l == 0:
            P.dma("sp", G.d_pF[:], G.pF[:], reads=[G.pF], writes=[G.d_pF], owner=G.ones, final=True)
            P.dma("sp", G.d_pT[:], G.pT[:], reads=[G.pT], writes=[G.d_pT], owner=G.identb, final=True)
        if attn:
            phase_attn_a(G, l, last)
            if stop < 3:
                break
            phase_attn_c(G, l, last)
            if stop < 4:
                break
        else:
            _zero_rows(G, 0, 512)
            _zero_rows(G, 1024, 1536)
        phase_ssd(G, l, last)
        phase_hyena(G, l, last)
        phase_out_mlp(G, l, last)
    if dbg:
        P.dma("sp", G.d_yT[:], G.yT[:], reads=[G.yT], writes=[G.d_yT], owner=G.ones, final=True)
        P.dma("sp", G.d_xs[:], G.xs[:], reads=[G.xs], writes=[G.d_xs], owner=G.ones, final=True)
        P.dma("sp", G.d_mod[:], G.modv[:], reads=[G.modv], writes=[G.d_mod], owner=G.identb, final=True)
    phase_final(G)
    P.emit()
    return nc


def evac(G, i, out_ap, in_ap, reads, writes, wdisj=()):
    P = G.P
    if i % 2 == 0:
        return P.op("act", lambda e: e.activation(out_ap, in_ap, AF.Copy), reads=reads, writes=writes, wdisj=wdisj)
    return P.op("dve", lambda e: e.tensor_copy(out_ap, in_ap), reads=reads, writes=writes, wdisj=wdisj)


def phase_mod(G):
    P = G.P
    P.scope_begin()
    cvs = P.sb("cvs", [128, KC, 2], F32)
    scb = P.sb("scb", [128, KC, 2], BF16)
    P.dma("sp", cvs[:], G.cv[:], reads=[G.cv], writes=[cvs])
    P.op("act", lambda e: e.activation(scb[:], cvs[:], AF.Silu), reads=[cvs], writes=[scb])
    wb = [P.sb("adw%d" % i, [128, KC, 512], BF16) for i in range(4)]
    bt = [P.sb("adb%d" % i, [2, 512], F32) for i in range(4)]
    mr = [P.sb("mr%d" % i, [2, 512], F32) for i in range(4)]
    it = 0
    for l in range(NL):
        for nch in range(24):
            w = wb[it % 4]
            b_ = bt[it % 4]
            m_ = mr[it % 4]
            P.dma("pool", w[:], G.adaw[l, :, :, nch * 512:(nch + 1) * 512].rearrange("k p n -> p k n"),
                  reads=[G.adaw], writes=[w])
            P.dma("sp", b_[:], G.adab[l:l + 1, nch * 512:(nch + 1) * 512].broadcast_to([2, 512]),
                  reads=[G.adab], writes=[b_])
            ps = G.pb()
            for kc in range(KC):
                P.mm(ps[0:2, :], scb[:, kc, :], w[:, kc, :], kc == 0, kc == KC - 1, [scb, w], [ps])
            P.op("dve", lambda e, m_=m_, ps=ps, b_=b_: e.tensor_tensor(m_[:], ps[0:2, :], b_[:], ALU.add),
                 reads=[ps, b_], writes=[m_])
            P.dma("sp", G.modv[l, :, nch * 512:(nch + 1) * 512], m_[:], reads=[m_], wdisj=[G.modv], owner=m_)
            it += 1
    P.scope_end()


def load_mod(G, l):
    P = G.P
    P.dma("sp", G.modsb[:], G.modv[l].rearrange("s (x p k) -> p s x k", x=6, p=128, k=KC),
          reads=[G.modv], writes=[G.modsb])
    P.dma("sp", G.nrmsb[:], G.nrm[l].rearrange("a p k -> p a k"), reads=[G.nrm], writes=[G.nrmsb])
    for s in range(2):
        for j, x in ((0, 1), (1, 4)):
            P.op("dve", lambda e, s=s, j=j, x=x: e.scalar_tensor_tensor(
                G.amod[:, s, j, :], G.modsb[:, s, x, :], 1.0, G.nrmsb[:, j, :], ALU.add, ALU.mult),
                reads=[G.modsb, G.nrmsb], writes=[G.amod])


def norm_mod(G, xc, n, s, which, hT, hoff, sqb, rstd):
    P = G.P
    P.op("act", lambda e: e.activation(sqb[:, :, :n], xc[:, :, :n], AF.Square), reads=[xc], writes=[sqb])
    ps = G.pb()
    for kc in range(KC):
        P.mm(ps[:, :n], G.ones[:], sqb[:, kc, :n], kc == 0, kc == KC - 1, [G.ones, sqb], [ps])
    P.op("dve", lambda e: e.tensor_scalar(rstd[:, :n], ps[:, :n], 1.0 / D, 1e-6, ALU.mult, ALU.add),
         reads=[ps], writes=[rstd])
    P.op("act", lambda e: e.activation(rstd[:, :n], rstd[:, :n], AF.Sqrt), reads=[rstd], writes=[rstd])
    P.op("dve", lambda e: e.reciprocal(rstd[:, :n], rstd[:, :n]), reads=[rstd], writes=[rstd])
    P.op("dve", lambda e: e.tensor_tensor(xc[:, :, :n], xc[:, :, :n], bcast(rstd[:, None, :n], [128, KC, n]), ALU.mult),
         reads=[xc, rstd], writes=[xc])
    shi = 0 if which == 0 else 3
    for kc in range(KC):
        P.op("act", lambda e, kc=kc: e.activation(hT[:, kc, hoff:hoff + n], xc[:, kc, :n], AF.Identity,
                                                  bias=G.modsb[:, s, shi, kc:kc + 1],
                                                  scale=G.amod[:, s, which, kc:kc + 1]),
             reads=[xc, G.modsb, G.amod], writes=[hT])


def phase_inproj(G, l, chunks):
    P = G.P
    P.scope_begin()
    ntok = sum(c[1] for c in chunks)
    tbase = chunks[0][0]
    hT = P.sb("hT", [128, KC, ntok], BF16)
    xc = P.sb("xc", [128, KC, 512], F32)
    sqb = P.sb("sqb", [128, KC, 512], BF16)
    rstd = P.sb("rstd", [128, 512], F32)
    for (t0, n, s) in chunks:
        P.dma("sp", xc[:, :, :n], G.xs[:, :, t0:t0 + n], reads=[G.xs], writes=[xc])
        norm_mod(G, xc, n, s, 0, hT, t0 - tbase, sqb, rstd)
    wb = [P.sb("wfm%d" % i, [128, KC, 128], BF16) for i in range(2)]
    stg = [P.sb("stg%d" % i, [128, 512], F32) for i in range(4)]
    it = 0
    for mt in range(NFM):
        w = wb[mt % 2]
        P.dma("pool", w[:], G.win_fm[l, mt], reads=[G.win_fm], writes=[w])
        for (t0, n, s) in chunks:
            ps = G.pb()
            for kc in range(KC):
                P.mm(ps[:, :n], w[:, kc, :], hT[:, kc, t0 - tbase:t0 - tbase + n], kc == 0, kc == KC - 1, [w, hT], [ps])
            sg = stg[it % 4]
            evac(G, it, sg[:, :n], ps[:, :n], [ps], [sg])
            P.dma("sp" if it % 2 == 0 else "act", G.pF[mt * 128:(mt + 1) * 128, t0:t0 + n], sg[:, :n],
                  reads=[sg], wdisj=[G.pF], owner=sg)
            it += 1
    wtm = P.sb("wtm", [128, KC, NTM], BF16)
    P.dma("pool", wtm[:], G.win_tm[l], reads=[G.win_tm], writes=[wtm])
    stT = [P.sb("stT%d" % i, [128, NTM], F32) for i in range(2)]
    for ti in range(ntok // 128):
        psa = G.pb()
        psb = G.pb()
        for kc in range(KC):
            P.mm(psa[:, :], hT[:, kc, ti * 128:(ti + 1) * 128], wtm[:, kc, 0:512], kc == 0, kc == KC - 1, [wtm, hT], [psa])
        for kc in range(KC):
            P.mm(psb[:, :NTM - 512], hT[:, kc, ti * 128:(ti + 1) * 128], wtm[:, kc, 512:NTM], kc == 0, kc == KC - 1, [wtm, hT], [psb])
        sg = stT[ti % 2]
        evac(G, 0, sg[:, 0:512], psa[:, :], [psa], [], wdisj=[sg])
        evac(G, 1, sg[:, 512:NTM], psb[:, :NTM - 512], [psb], [], wdisj=[sg])
        P.dma("sp", G.pT[tbase + ti * 128: tbase + (ti + 1) * 128, :], sg[:], reads=[sg], wdisj=[G.pT], owner=sg)
    P.scope_end()


def phase_out_mlp(G, l, last):
    P = G.P
    P.scope_begin()
    xc = P.sb("xc", [128, KC, 512], F32)
    yc = P.sb("yc", [128, KC, 512], BF16)
    hm = yc
    sqb = P.sb("sqb", [128, KC, 512], BF16)
    rstd = P.sb("rstd", [128, 512], F32)
    hid = P.sb("hid", [128, 64, 512], BF16)
    wo = [P.sb("wo%d" % i, [128, KC, 128], BF16) for i in range(2)]
    w1b = [P.sb("w1b%d" % i, [128, KC, 128], BF16) for i in range(2)]
    w2b = [P.sb("w2b%d" % i, [128, 64, 128], BF16) for i in range(2)]
    it = 0
    for (t0, n, s) in CHUNKS:
        if last and s == 1:
            continue
        P.dma("sp", xc[:, :, :n], G.xs[:, :, t0:t0 + n], reads=[G.xs], writes=[xc])
        P.dma("act", yc[:, :, :n], G.yT[:, t0:t0 + n].rearrange("(k p) t -> p k t", p=128), reads=[G.yT], writes=[yc])
        first = (t0 == 0)
        for mt in range(16):
            w = wo[mt % 2]
            if first:
                P.dma("pool", w[:], G.wout[l, mt], reads=[G.wout], writes=[w])
                P.dma("sp", G.wob[mt], w[:], reads=[w], wdisj=[G.wob], owner=w)
            else:
                P.dma("sp", w[:], G.wob[mt], reads=[G.wob], writes=[w])
            ps = G.pb()
            for kc in range(KC):
                P.mm(ps[:, :n], w[:, kc, :], yc[:, kc, :n], kc == 0, kc == KC - 1, [w, yc], [ps])
            P.op("dve", lambda e, ps=ps, mt=mt: e.scalar_tensor_tensor(
                xc[:, mt, :n], ps[:, :n], G.modsb[:, s, 2, mt:mt + 1], xc[:, mt, :n], ALU.mult, ALU.add),
                reads=[ps, G.modsb, xc], writes=[xc])
        P.dma("sp", G.xs[:, :, t0:t0 + n], xc[:, :, :n], reads=[xc], wdisj=[G.xs], owner=sqb)
        norm_mod(G, xc, n, s, 1, hm, 0, sqb, rstd)
        for ht in range(64):
            w = w1b[ht % 2]
            if first:
                P.dma("pool", w[:], G.w1[l, ht], reads=[G.w1], writes=[w])
                P.dma("sp", G.w1c[ht], w[:], reads=[w], wdisj=[G.w1c], owner=w)
            else:
                P.dma("sp", w[:], G.w1c[ht], reads=[G.w1c], writes=[w])
            ps = G.pb()
            for kc in range(KC):
                P.mm(ps[:, :n], w[:, kc, :], hm[:, kc, :n], kc == 0, kc == KC - 1, [w, hm], [ps])
            P.op("act", lambda e, ps=ps, ht=ht: e.activation(sqb[:, ht % KC, :n], ps[:, :n], AF.Relu),
                 reads=[ps], wdisj=[sqb])
            P.op("dve", lambda e, ht=ht: e.tensor_tensor(
                hid[:, ht, :n], sqb[:, ht % KC, :n], sqb[:, ht % KC, :n], ALU.mult), reads=[sqb], wdisj=[hid])
        P.dma("sp", xc[:, :, :n], G.xs[:, :, t0:t0 + n], reads=[G.xs], writes=[xc])
        for mt in range(16):
            w = w2b[mt % 2]
            if first:
                P.dma("pool", w[:], G.w2[l, mt], reads=[G.w2], writes=[w])
                P.dma("sp", G.w2c[mt], w[:], reads=[w], wdisj=[G.w2c], owner=w)
            else:
                P.dma("pool", w[:], G.w2c[mt], reads=[G.w2c], writes=[w])
            ps = G.pb()
            for hc in range(64):
                P.mm(ps[:, :n], w[:, hc, :], hid[:, hc, :n], hc == 0, hc == 63, [w, hid], [ps])
            P.op("dve", lambda e, ps=ps, mt=mt: e.scalar_tensor_tensor(
                xc[:, mt, :n], ps[:, :n], G.modsb[:, s, 5, mt:mt + 1], xc[:, mt, :n], ALU.mult, ALU.add),
                reads=[ps, G.modsb, xc], writes=[xc])
        P.dma("sp", G.xs[:, :, t0:t0 + n], xc[:, :, :n], reads=[xc], wdisj=[G.xs], owner=rstd)
        it += 1
    P.scope_end()


def phase_final(G):
    P = G.P
    P.scope_begin()
    xc = P.sb("xc", [128, KC, 512], F32)
    sqb = P.sb("sqb", [128, KC, 512], BF16)
    rstd = P.sb("rstd", [128, 512], F32)
    fw = P.sb("fw", [128, KC], F32)
    P.dma("sp", fw[:], G.fnorm[:], reads=[G.fnorm], writes=[fw])
    for (t0, n, s) in CHUNKS[:8]:
        P.dma("sp", xc[:, :, :n], G.xs[:, :, t0:t0 + n], reads=[G.xs], writes=[xc])
        P.op("act", lambda e: e.activation(sqb[:, :, :n], xc[:, :, :n], AF.Square), reads=[xc], writes=[sqb])
        ps = G.pb()
        for kc in range(KC):
            P.mm(ps[:, :n], G.ones[:], sqb[:, kc, :n], kc == 0, kc == KC - 1, [G.ones, sqb], [ps])
        P.op("dve", lambda e, ps=ps: e.tensor_scalar(rstd[:, :n], ps[:, :n], 1.0 / D, 1e-6, ALU.mult, ALU.add),
             reads=[ps], writes=[rstd])
        P.op("act", lambda e: e.activation(rstd[:, :n], rstd[:, :n], AF.Sqrt), reads=[rstd], writes=[rstd])
        P.op("dve", lambda e: e.reciprocal(rstd[:, :n], rstd[:, :n]), reads=[rstd], writes=[rstd])
        P.op("dve", lambda e: e.tensor_tensor(xc[:, :, :n], xc[:, :, :n], bcast(rstd[:, None, :n], [128, KC, n]), ALU.mult),
             reads=[xc, rstd], writes=[xc])
        P.op("dve", lambda e: e.tensor_tensor(xc[:, :, :n], xc[:, :, :n], bcast(fw[:, :, None], [128, KC, n]), ALU.mult),
             reads=[xc, fw], writes=[xc])
        P.dma("sp", G.out[:, :, t0:t0 + n], xc[:, :, :n], reads=[xc], wdisj=[G.out], owner=xc, final=True)
    P.scope_end()


def bc_mid(ap2, k):
    p, n = ap2.shape
    return ap2.unsqueeze(1).broadcast_to([p, k, n])


def bc_last(ap2, n):
    p, k = ap2.shape
    return ap2.unsqueeze(2).broadcast_to([p, k, n])


def attn_core(G, S_mm, key_tiles, pv_mm, n_pt, PT, ptc):
    P = G.P
    nk = len(key_tiles)
    for i, kt in enumerate(key_tiles):
        ps = G.pb6()
        S_mm(ps, kt)
        pt = PT[ptc[0] % n_pt]
        ptc[0] += 1
        P.op("act", lambda e, pt=pt, ps=ps: e.activation(pt[:], ps[:], AF.Exp), reads=[ps], writes=[pt])
        pv_mm(pt, kt, i == 0, i == nk - 1)


def phase_attn_a(G, l, last):
    P = G.P
    P.scope_begin()
    QA = P.sb("QA", [128, 4, TT], BF16)
    KAz = P.sb("KAz", [128, 2, 2, TT], BF16)
    VA = P.sb("VA", [128, 34, 128], BF16)
    cm = P.sb("cm", [128, 2, 512], BF16)
    es = P.sb("es", [128, 8], F32)
    P.op("pool", lambda e: e.memset(KAz[:], 0.0), writes=[KAz])
    P.dma("pool", cm[:], G.cmask[:].rearrange("a p n -> p a n"), reads=[G.cmask], writes=[cm])
    P.dma("sp", es[:], G.sink[l:l + 1, :].broadcast_to([128, 8]), reads=[G.sink], writes=[es])
    P.op("act", lambda e: e.activation(es[:], es[:], AF.Exp), reads=[es], writes=[es])
    for a0 in range(0, 34, 2):
        P.dma("pool", VA[:, a0:a0 + 2, :], G.pT[a0 * 128:(a0 + 2) * 128, 0:128].rearrange("(a p) c -> p a c", p=128),
              reads=[G.pT], wdisj=[VA], owner=VA)
    raw = P.sb("raw", [128, 12, 512], F32)
    rp = P.sb("rp", [128, 4, 512], F32)
    tmp = P.sb("tmp", [128, 4, 512], F32)
    for (t0, n, s) in CHUNKS:
        P.dma("sp", raw[:, :, :n], G.pF[0:12 * 128, t0:t0 + n].rearrange("(a p) t -> p a t", p=128),
              reads=[G.pF], writes=[raw])
        if s == 0:
            P.dma("act", rp[:, :, :n], G.rope[:, :, t0:t0 + n].rearrange("a p t -> p a t"), reads=[G.rope], writes=[rp])
            P.op("dve", lambda e: e.tensor_tensor(tmp[:, :, :n], raw[:, 0:4, :n], bc_mid(rp[:, 0, :n], 4), ALU.mult),
                 reads=[raw, rp], writes=[tmp])
            P.op("pool", lambda e: e.tensor_tensor(raw[:, 4:8, :n], raw[:, 4:8, :n], bc_mid(rp[:, 1, :n], 4), ALU.mult),
                 reads=[raw, rp], writes=[raw])
            P.op("dve", lambda e: e.tensor_tensor(QA[:, :, t0:t0 + n], tmp[:, :, :n], raw[:, 4:8, :n], ALU.add),
                 reads=[tmp, raw], wdisj=[QA])
            P.op("dve", lambda e: e.tensor_tensor(tmp[:, 0:2, :n], raw[:, 8:10, :n], bc_mid(rp[:, 2, :n], 2), ALU.mult),
                 reads=[raw, rp], writes=[tmp])
            P.op("pool", lambda e: e.tensor_tensor(raw[:, 10:12, :n], raw[:, 10:12, :n], bc_mid(rp[:, 3, :n], 2), ALU.mult),
                 reads=[raw, rp], writes=[raw])
            for hf in range(2):
                P.op("dve", lambda e: e.tensor_tensor(KAz[hf * 64:(hf + 1) * 64, :, hf, t0:t0 + n],
                                                      tmp[hf * 64:(hf + 1) * 64, 0:2, :n],
                                                      raw[hf * 64:(hf + 1) * 64, 10:12, :n], ALU.add),
                     reads=[tmp, raw], wdisj=[KAz])
        else:
            P.op("act", lambda e: e.activation(QA[:, :, t0:t0 + n], raw[:, 0:4, :n], AF.Copy, scale=0.125),
                 reads=[raw], wdisj=[QA])
            for hf in range(2):
                P.op("dve", lambda e: e.tensor_copy(KAz[hf * 64:(hf + 1) * 64, :, hf, t0:t0 + n],
                                                    raw[hf * 64:(hf + 1) * 64, 8:10, :n]), reads=[raw], wdisj=[KAz])
    PT = [P.sb("PT%d" % i, [128, 512], BF16) for i in range(3)]
    ptc = [0]
    yst = [P.sb("yst%d" % i, [128, 4, 512], BF16) for i in range(2)]
    dn = P.sb("dn", [128, 4, 128], F32)
    yTa = G.yT[0:512, :].rearrange("(h d) t -> d h t", d=64)
    nblocks = 32 if last else 34
    for g in range(2):
        for n in range(nblocks):
            kts = []
            if n < 32:
                if n > 0:
                    kts.append((n - 1, 0))
                kts.append((n, None))
                if n < 31:
                    kts.append((n + 1, 1))
            kts += [(32, None), (33, None)]
            num = G.PB[6]
            den = G.PB[7]

            def S_mm(ps, kt, g=g, n=n):
                kti, mk = kt
                first = True
                if mk is not None:
                    P.mm(ps[:], G.identb[:], cm[:, mk, :], True, False, [G.identb, cm], [ps])
                    first = False
                for hh in range(4):
                    h = 4 * g + hh
                    j, hf = h // 2, h % 2
                    P.mm(ps[:, hh * 128:(hh + 1) * 128], KAz[:, g, hf, kti * 128:(kti + 1) * 128],
                         QA[:, j, n * 128:(n + 1) * 128], first, hh == 3, [KAz, QA], [ps])
                    first = False

            def pv_mm(pt, kt, first, lastk, g=g, num=num, den=den):
                kti, mk = kt
                P.mm(num[:], VA[:, kti, :], pt[:], first, lastk, [VA, pt], [num])
                P.mm(den[:], G.ones[:], pt[:], first, lastk, [G.ones, pt], [den])
            attn_core(G, S_mm, kts, pv_mm, 3, PT, ptc)
            ys = yst[(n // 4) % 2]
            co = (n % 4) * 128
            P.op("dve", lambda e: e.tensor_tensor(
                dn[:], den[:].rearrange("p (h q) -> p h q", h=4), bc_last(es[:, 4 * g:4 * g + 4], 128), ALU.add),
                reads=[den, es], writes=[dn])
            P.op("dve", lambda e: e.reciprocal(dn[:], dn[:]), reads=[dn], writes=[dn])
            P.op("dve", lambda e: e.tensor_tensor(
                ys[:, :, co:co + 128], num[:].rearrange("p (h q) -> p h q", h=4), dn[:], ALU.mult),
                reads=[num, dn], wdisj=[ys])
            if n % 4 == 3 or n == nblocks - 1:
                t0 = (n // 4) * 512
                nn = (n % 4 + 1) * 128
                P.dma("sp", yTa[:, 4 * g:4 * g + 4, t0:t0 + nn], ys[g * 64:(g + 1) * 64, :, :nn],
                      reads=[ys], wdisj=[G.yT], owner=ys)
    P.scope_end()


def phase_attn_c(G, l, last):
    P = G.P
    P.scope_begin()
    QC = P.sb("QC", [128, 4, TT], BF16)
    KCz = P.sb("KCz", [128, 4, 2, TT], BF16)
    VC = P.sb("VC", [128, 34, 512], BF16)
    P.op("pool", lambda e: e.memset(KCz[:], 0.0), writes=[KCz])
    for a0 in range(0, 34, 2):
        P.dma("pool", VC[:, a0:a0 + 2, :], G.pT[a0 * 128:(a0 + 2) * 128, 128:640].rearrange("(a p) c -> p a c", p=128),
              reads=[G.pT], wdisj=[VC], owner=VC)
    raw = P.sb("raw", [128, 8, 512], F32)
    for (t0, n, s) in CHUNKS:
        P.dma("sp", raw[:, :, :n], G.pF[24 * 128:32 * 128, t0:t0 + n].rearrange("(a p) t -> p a t", p=128),
              reads=[G.pF], writes=[raw])
        P.op("act", lambda e: e.activation(QC[:, :, t0:t0 + n], raw[:, 0:4, :n], AF.Copy, scale=0.125),
             reads=[raw], wdisj=[QC])
        for hf in range(2):
            P.op("dve" if hf == 0 else "pool", lambda e: e.tensor_copy(
                KCz[hf * 64:(hf + 1) * 64, :, hf, t0:t0 + n], raw[hf * 64:(hf + 1) * 64, 4:8, :n]),
                reads=[raw], wdisj=[KCz])
    PT = [P.sb("PT%d" % i, [128, 512], BF16) for i in range(3)]
    BT = [P.sb("BT%d" % i, [128, 512], BF16) for i in range(3)]
    ptc = [0]
    btc = [0]
    yst = [P.sb("yst%d" % i, [128, 4, 512], BF16) for i in range(2)]
    dn = P.sb("dn", [128, 512], F32)
    yTc = G.yT[1024:1536, :].rearrange("(h d) t -> d h t", d=64)
    nblocks = 32 if last else 34
    for pg in range(2):
        for n in range(nblocks):
            kts = []
            if n < 32:
                R2 = n
                if R2 < 2:
                    ks = range(0, 4)
                elif R2 > 29:
                    ks = range(28, 32)
                else:
                    ks = range(R2 - 2, R2 + 3)
                kts += [(K2, _na_case_id(R2, K2)) for K2 in ks]
            kts += [(32, None), (33, None)]
            num = G.PB[6]
            den = G.PB[7]

            def S_mm(ps, kt, pg=pg, n=n):
                kti, case = kt
                first = True
                if case is not None:
                    bt = BT[btc[0] % 3]
                    btc[0] += 1
                    P.dma("pool", bt[:], G.nbt[l, case, pg], reads=[G.nbt], writes=[bt])
                    P.mm(ps[:], G.identb[:], bt[:], True, False, [G.identb, bt], [ps])
                    first = False
                for hh in range(4):
                    h = 4 * pg + hh
                    j, hf = h // 2, h % 2
                    P.mm(ps[:, hh * 128:(hh + 1) * 128], KCz[:, j, hf, kti * 128:(kti + 1) * 128],
                         QC[:, j, n * 128:(n + 1) * 128], first, hh == 3, [KCz, QC], [ps])
                    first = False

            def pv_mm(pt, kt, first, lastk, pg=pg, num=num, den=den):
                kti, case = kt
                for hh in range(4):
                    h = 4 * pg + hh
                    j = h // 2
                    P.mm(num[:, hh * 128:(hh + 1) * 128], VC[:, kti, j * 128:(j + 1) * 128], pt[:, hh * 128:(hh + 1) * 128],
                         first and hh == 0, lastk and hh == 3, [VC, pt], [num])
                P.mm(den[:], G.ones[:], pt[:], first, lastk, [G.ones, pt], [den])
            attn_core(G, S_mm, kts, pv_mm, 3, PT, ptc)
            ys = yst[(n // 4) % 2]
            co = (n % 4) * 128
            P.op("dve", lambda e: e.reciprocal(dn[:], den[:]), reads=[den], writes=[dn])
            P.op("dve", lambda e: e.tensor_tensor(
                ys[:, :, co:co + 128], num[:].rearrange("p (h q) -> p h q", h=4),
                dn[:].rearrange("p (h q) -> p h q", h=4), ALU.mult),
                reads=[num, dn], wdisj=[ys])
            if n % 4 == 3 or n == nblocks - 1:
                t0 = (n // 4) * 512
                nn = (n % 4 + 1) * 128
                for hh in range(4):
                    hf = hh % 2
                    P.dma("sp" if hh < 2 else "act", yTc[:, 4 * pg + hh, t0:t0 + nn], ys[hf * 64:(hf + 1) * 64, hh, :nn],
                          reads=[ys], wdisj=[G.yT], owner=ys)
    P.scope_end()


def _zero_rows(G, r0, r1):
    P = G.P
    P.scope_begin()
    z = P.sb("zrow", [128, 2176], BF16)
    P.op("pool", lambda e: e.memset(z[:], 0.0), writes=[z])
    for r in range(r0, r1, 128):
        for c in range(0, TT, 2176):
            P.dma("sp", G.yT[r:r + 128, c:c + 2176], z[:], reads=[z], wdisj=[G.yT], owner=z)
    P.scope_end()


def phase_ssd(G, l, last):
    P = G.P
    PB = G.PB
    P.scope_begin()
    cw = P.sb("cw", [128, 8, 3], F32)
    cb = P.sb("cb", [128, 8], F32)
    P.dma("sp", cw[:], G.scw[l], reads=[G.scw], writes=[cw])
    P.dma("sp", cb[:], G.scb[l], reads=[G.scb], writes=[cb])
    XSb = P.sb("XSb", [128, 4, TT], BF16)
    BC = P.sb("BC", [128, 4, TT], BF16)
    raw = P.sb("raw", [128, 8, 514], F32)
    acc = P.sb("acc", [128, 8, 512], F32)
    for (t0, n, s) in CHUNKS:
        seq0, seq1 = (0, T) if s == 0 else (T, TT)
        lo = max(t0 - 1, seq0)
        hi = min(t0 + n + 1, seq1)
        if lo == t0 or hi == t0 + n:
            P.op("pool", lambda e: e.memset(raw[:], 0.0), writes=[raw])
        P.dma("sp", raw[:, :, lo - (t0 - 1):hi - (t0 - 1)],
              G.pF[36 * 128:44 * 128, lo:hi].rearrange("(a p) t -> p a t", p=128), reads=[G.pF], writes=[raw])
        for ti in range(8):
            eng = "dve"
            P.op(eng, lambda e: e.tensor_scalar(acc[:, ti, :n], raw[:, ti, 1:n + 1], cw[:, ti, 1:2], cb[:, ti:ti + 1],
                                                ALU.mult, ALU.add), reads=[raw, cw, cb], wdisj=[acc])
            P.op(eng, lambda e: e.scalar_tensor_tensor(acc[:, ti, :n], raw[:, ti, 0:n], cw[:, ti, 0:1], acc[:, ti, :n],
                                                       ALU.mult, ALU.add), reads=[raw, cw, acc], wdisj=[acc])
            P.op(eng, lambda e: e.scalar_tensor_tensor(acc[:, ti, :n], raw[:, ti, 2:n + 2], cw[:, ti, 2:3], acc[:, ti, :n],
                                                       ALU.mult, ALU.add), reads=[raw, cw, acc], wdisj=[acc])
        P.op("act", lambda e: e.activation(acc[:, :, :n], acc[:, :, :n], AF.Silu), reads=[acc], writes=[acc])
        P.dma("sp", G.sxs[:, t0:t0 + n].rearrange("(a p) t -> p a t", p=128), acc[:, 0:4, :n], reads=[acc], wdisj=[G.sxs], owner=acc)
        P.op("pool", lambda e: e.tensor_copy(XSb[:, :, t0:t0 + n], acc[:, 0:4, :n]), reads=[acc], wdisj=[XSb])
        P.op("dve", lambda e: e.tensor_copy(BC[:, :, t0:t0 + n], acc[:, 4:8, :n]), reads=[acc], wdisj=[BC])
    import os
    sstop = int(os.environ.get("SSTOP", "99"))
    ssub = int(os.environ.get("SSUB", "99"))
    sson = os.environ.get("SSONLY", "abcd")
    if sstop < 2:
        P.scope_end()
        return
    dtr = P.sb("dtr", [128, 34, 16], F32)
    dtv = P.sb("dtv", [128, 34, 16], F32)
    adt = P.sb("adt", [128, 34, 16], F32)
    dtb = P.sb("dtb", [128, 16], F32)
    aa = P.sb("aa", [128, 16], F32)
    P.dma("sp", dtr[:], G.pT[:, 640:656].rearrange("(a p) c -> p a c", p=128), reads=[G.pT], writes=[dtr])
    P.dma("sp", dtb[:], G.sdtb[l:l + 1, :].broadcast_to([128, 16]), reads=[G.sdtb], writes=[dtb])
    P.dma("sp", aa[:], G.salog[l:l + 1, :].broadcast_to([128, 16]), reads=[G.salog], writes=[aa])
    P.op("act", lambda e: e.activation(aa[:], aa[:], AF.Exp), reads=[aa], writes=[aa])
    P.op("dve", lambda e: e.tensor_tensor(dtr[:], dtr[:], bc_mid(dtb[:], 34), ALU.add), reads=[dtr, dtb], writes=[dtr])
    P.op("act", lambda e: e.activation(dtv[:], dtr[:], AF.Exp), reads=[dtr], writes=[dtv])
    P.op("dve", lambda e: e.tensor_scalar(dtv[:], dtv[:], 1.0, None, ALU.add), reads=[dtv], writes=[dtv])
    P.op("act", lambda e: e.activation(dtv[:], dtv[:], AF.Ln), reads=[dtv], writes=[dtv])
    P.op("dve", lambda e: e.tensor_tensor(adt[:], dtv[:], bc_mid(aa[:], 34), ALU.mult), reads=[dtv, aa], writes=[adt])
    P.op("dve", lambda e: e.tensor_scalar(adt[:], adt[:], -1.0, None, ALU.mult), reads=[adt], writes=[adt])
    adth = P.sb("adth", [128, 34, 16], BF16)
    adtl = P.sb("adtl", [128, 34, 16], BF16)
    adthf = P.sb("adthf", [128, 34, 16], F32)
    P.op("dve", lambda e: e.tensor_copy(adth[:], adt[:]), reads=[adt], writes=[adth])
    P.op("dve", lambda e: e.tensor_copy(adthf[:], adth[:]), reads=[adth], writes=[adthf])
    P.op("dve", lambda e: e.tensor_tensor(adthf[:], adt[:], adthf[:], ALU.subtract), reads=[adt, adthf], writes=[adthf])
    P.op("dve", lambda e: e.tensor_copy(adtl[:], adthf[:]), reads=[adthf], writes=[adtl])
    if sstop < 3:
        P.scope_end()
        return
    Xtm = P.sb("Xtm", [128, 34, 512], BF16)
    Btm = P.sb("Btm", [128, 34, 256], BF16)
    for ti in range(34):
        c0 = ti * 128
        ps = PB[7] if ti % 2 == 0 else PB[6]
        for j in range(4):
            P.mm(ps[:, j * 128:(j + 1) * 128], XSb[:, j, c0:c0 + 128], G.identb[:], j == 0, j == 3, [XSb, G.identb], [ps])
        P.op("act", lambda e: e.activation(Xtm[:, ti, :], ps[:], AF.Copy), reads=[ps], wdisj=[Xtm])
        ps2 = PB[5] if ti % 2 == 0 else PB[4]
        for j in range(2):
            P.mm(ps2[:, j * 128:(j + 1) * 128], BC[:, j, c0:c0 + 128], G.identb[:], j == 0, j == 1, [BC, G.identb], [ps2])
        P.op("dve", lambda e: e.tensor_copy(Btm[:, ti, :], ps2[:, 0:256]), reads=[ps2], wdisj=[Btm])
    if sstop < 5:
        P.scope_end()
        return
    tri = P.sb("tri", [128, 2, 128], BF16)
    tmk = P.sb("tmk", [128, 2, 128], F32)
    P.dma("pool", tri[:], G.tri[:].rearrange("a p n -> p a n"), reads=[G.tri], writes=[tri])
    P.dma("sp", tmk[:], G.trimask[:].rearrange("a p n -> p a n"), reads=[G.trimask], writes=[tmk])
    ST = P.sb("ST", [128, 8, 64], F32)
    STb = P.sb("STb", [128, 8, 64], BF16)
    AdtBh = P.sb("AdtBh", [128, 8, 128], BF16)
    AdtBl = P.sb("AdtBl", [128, 8, 128], BF16)
    css = P.sb("css", [128, 8], F32)
    bl = P.sb("bl", [128, 8], F32)
    dsv = P.sb("dsv", [128, 8], F32)
    ed = P.sb("ed", [128, 8], F32)
    EB = P.sb("EB", [128, 8, 128], F32)
    bcs = P.sb("bcs", [128, 8, 128], F32)
    Gs = P.sb("Gs", [128, 2, 128], F32)
    arg = P.sb("arg", [128, 8, 128], F32)
    Ce = P.sb("Ce", [128, 8, 128], BF16)
    Mt = P.sb("Mt", [128, 8, 128], BF16)
    Xd = P.sb("Xd", [128, 8, 64], BF16)
    Xs = P.sb("Xs", [128, 8, 64], BF16)
    yst = [P.sb("ysd%d" % i, [128, 8, 128], F32) for i in range(2)]
    it = 0
    for d in range(2):
        P.op("dve", lambda e: e.memset(ST[:], 0.0), writes=[ST])
        P.op("pool", lambda e: e.memset(STb[:], 0.0), writes=[STb])
        order = ([32, 33] + list(range(32))) if d == 0 else ([33, 32] + list(range(31, -1, -1)))
        lastc = 127 if d == 0 else 0
        if sstop < 6:
            order = order[:3]
        for ti in order:
            c0 = ti * 128
            need_y = (ti < 32) or (not last)
            if "a" in sson:
                P.op("dve", lambda e: e.tensor_copy(AdtBh[:], bc_last(adth[:, ti, d * 8:(d + 1) * 8], 128)), reads=[adth], writes=[AdtBh])
                P.op("dve", lambda e: e.tensor_copy(AdtBl[:], bc_last(adtl[:, ti, d * 8:(d + 1) * 8], 128)), reads=[adtl], writes=[AdtBl])
            if "b" in sson:
                P.mm(PB[0][:, 0:8], tri[:, d, :], adth[:, ti, d * 8:(d + 1) * 8], True, False, [tri, adth], [PB[0]])
                P.mm(PB[0][:, 0:8], tri[:, d, :], adtl[:, ti, d * 8:(d + 1) * 8], False, True, [tri, adtl], [PB[0]])
            for h in range(8):
                if "c" not in sson:
                    break
                bk = PB[1 + h // 4]
                P.mm(bk[:, (h % 4) * 128:(h % 4 + 1) * 128], AdtBh[:, h, :], tri[:, d, :], h % 4 == 0, False, [AdtBh, tri], [bk])
                P.mm(bk[:, (h % 4) * 128:(h % 4 + 1) * 128], AdtBl[:, h, :], tri[:, d, :], False, h % 4 == 3, [AdtBl, tri], [bk])
            for g in range(2):
                if "d" not in sson:
                    break
                P.mm(PB[3][:, g * 128:(g + 1) * 128], BC[:, g, c0:c0 + 128], BC[:, 2 + g, c0:c0 + 128], g == 0, g == 1, [BC], [PB[3]])
            if ssub < 1:
                continue
            P.op("act", lambda e: e.activation(css[:], PB[0][:, 0:8], AF.Copy), reads=[PB[0]], writes=[css])
            for b in range(2):
                bk = PB[1 + b]
                if b == 0:
                    P.op("act", lambda e: e.activation(bcs[:, 0:4, :].rearrange("p h l -> p (h l)"), bk[:], AF.Copy),
                         reads=[bk], wdisj=[bcs])
                else:
                    P.op("dve", lambda e: e.tensor_copy(bcs[:, 4:8, :].rearrange("p h l -> p (h l)"), bk[:]),
                         reads=[bk], wdisj=[bcs])
            P.op("dve", lambda e: e.tensor_copy(bl[:], bcs[:, :, lastc]), reads=[bcs], writes=[bl])
            P.op("act", lambda e: e.activation(EB[:], bcs[:], AF.Exp), reads=[bcs], writes=[EB])
            P.op("dve", lambda e: e.tensor_tensor(arg[:], bcs[:], bc_last(css[:], 128), ALU.subtract),
                 reads=[bcs, css], writes=[arg])
            if ssub < 2:
                continue
            P.op("dve", lambda e: e.tensor_tensor(arg[:], arg[:], bc_mid(tmk[:, d, :], 8), ALU.add), reads=[arg, tmk], writes=[arg])
            P.op("act", lambda e: e.activation(arg[:], arg[:], AF.Exp), reads=[arg], writes=[arg])
            for g in range(2):
                P.op("dve", lambda e: e.tensor_tensor(Ce[:, 4 * g:4 * g + 4, :], EB[:, 4 * g:4 * g + 4, :],
                                                       bc_mid(BC[:, 2 + g, c0:c0 + 128], 4), ALU.mult),
                     reads=[EB, BC], wdisj=[Ce])
                if g == 0:
                    P.op("act", lambda e: e.activation(Gs[:].rearrange("p g l -> p (g l)"), PB[3][:, 0:256], AF.Copy),
                         reads=[PB[3]], writes=[Gs])
                P.op("dve", lambda e: e.tensor_tensor(Mt[:, 4 * g:4 * g + 4, :], arg[:, 4 * g:4 * g + 4, :],
                                                      bc_mid(Gs[:, g, :], 4), ALU.mult),
                     reads=[arg, Gs], wdisj=[Mt])
            if ssub < 3:
                continue
            P.op("dve", lambda e: e.tensor_tensor(Xd[:], Xtm[:, ti, :].rearrange("p (h q) -> p h q", h=8),
                                                   bc_last(dtv[:, ti, d * 8:(d + 1) * 8], 64), ALU.mult),
                 reads=[Xtm, dtv], writes=[Xd])
            P.op("dve", lambda e: e.tensor_tensor(dsv[:], bl[:], css[:], ALU.subtract), reads=[bl, css], writes=[dsv])
            P.op("act", lambda e: e.activation(dsv[:], dsv[:], AF.Exp), reads=[dsv], writes=[dsv])
            P.op("act", lambda e: e.activation(ed[:], bl[:], AF.Exp), reads=[bl], writes=[ed])
            P.op("dve", lambda e: e.tensor_tensor(Xs[:], Xd[:], bc_last(dsv[:], 64), ALU.mult), reads=[Xd, dsv], writes=[Xs])
            if ssub < 4:
                continue
            if need_y:
                for h in range(8):
                    bk = PB[4 + h // 4]
                    cs_ = slice((h % 4) * 128, (h % 4 + 1) * 128)
                    pr = h // 2
                    P.mm(bk[:, cs_], Xd[:, 2 * pr:2 * pr + 2, :].rearrange("p a q -> p (a q)"), Mt[:, h, :],
                         h % 4 == 0, False, [Xd, Mt], [bk])
                    P.mm(bk[:, cs_], STb[:, 2 * pr:2 * pr + 2, :].rearrange("p a q -> p (a q)"), Ce[:, h, :],
                         False, True, [STb, Ce], [bk])
                ys = yst[it % 2]
                it += 1
                for b in range(2):
                    bk = PB[4 + b]
                    P.op("act", lambda e: e.activation(ys[:, 4 * b:4 * b + 4, :], bk[:].rearrange("p (h l) -> p h l", h=4), AF.Copy),
                         reads=[bk], wdisj=[ys])
                ysv = ys[:].rearrange("p (q two) l -> p q two l", two=2)
                dst = G.ysd[d].rearrange("(q two c) t -> two c q t", two=2, c=64)
                for par in range(2):
                    P.dma("sp" if par == 0 else "act", dst[par, :, :, c0:c0 + 128], ysv[par * 64:(par + 1) * 64, :, par, :],
                          reads=[ys], wdisj=[G.ysd], owner=ys)
            if ssub < 5:
                continue
            for g in range(2):
                P.mm(PB[6][:, g * 256:(g + 1) * 256], Btm[:, ti, g * 128:(g + 1) * 128],
                     Xs[:, 4 * g:4 * g + 4, :].rearrange("p a q -> p (a q)"), g == 0, g == 1, [Btm, Xs], [PB[6]])
            P.op("dve", lambda e: e.tensor_tensor(ST[:], ST[:], bc_last(ed[:], 64), ALU.mult), reads=[ST, ed], writes=[ST])
            P.op("dve", lambda e: e.tensor_tensor(ST[:].rearrange("p h q -> p (h q)"), PB[6][:], ST[:].rearrange("p h q -> p (h q)"), ALU.add),
                 reads=[ST, PB[6]], writes=[ST])
            P.op("act", lambda e: e.activation(STb[:], ST[:], AF.Copy), reads=[ST], writes=[STb])
    P.scope_end()
    if sstop < 7:
        return
    P.scope_begin()
    sdv = P.sb("sdv", [128, 4], F32)
    snw = P.sb("snw", [128, 4], F32)
    P.dma("sp", sdv[:], G.sd[l], reads=[G.sd], writes=[sdv])
    P.dma("sp", snw[:], G.snw[l], reads=[G.snw], writes=[snw])
    yf = P.sb("yf", [128, 4, 512], F32)
    yb = P.sb("yb", [128, 4, 512], F32)
    xsl = P.sb("xsl", [128, 4, 512], F32)
    zz = P.sb("zz", [128, 4, 512], F32)
    sqv = P.sb("sqv", [128, 4, 512], BF16)
    rs = P.sb("rs", [128, 2, 512], F32)
    yo = P.sb("yo", [128, 4, 512], BF16)
    for (t0, n, s) in CHUNKS:
        if last and s == 1:
            continue
        P.dma("sp", yf[:, :, :n], G.ysd[0, :, t0:t0 + n].rearrange("(a p) t -> p a t", p=128), reads=[G.ysd], writes=[yf])
        P.dma("act", yb[:, :, :n], G.ysd[1, :, t0:t0 + n].rearrange("(a p) t -> p a t", p=128), reads=[G.ysd], writes=[yb])
        P.dma("sp", xsl[:, :, :n], G.sxs[:, t0:t0 + n].rearrange("(a p) t -> p a t", p=128), reads=[G.sxs], writes=[xsl])
        P.dma("act", zz[:, :, :n], G.pF[32 * 128:36 * 128, t0:t0 + n].rearrange("(a p) t -> p a t", p=128), reads=[G.pF], writes=[zz])
        P.op("dve", lambda e: e.tensor_tensor(yf[:, :, :n], yf[:, :, :n], yb[:, :, :n], ALU.add), reads=[yf, yb], writes=[yf])
        P.op("pool", lambda e: e.tensor_tensor(xsl[:, :, :n], xsl[:, :, :n], bc_last(sdv[:], n), ALU.mult), reads=[xsl, sdv], writes=[xsl])
        P.op("dve", lambda e: e.tensor_tensor(yf[:, :, :n], yf[:, :, :n], xsl[:, :, :n], ALU.add), reads=[yf, xsl], writes=[yf])
        P.op("act", lambda e: e.activation(zz[:, :, :n], zz[:, :, :n], AF.Silu), reads=[zz], writes=[zz])
        P.op("dve", lambda e: e.tensor_tensor(yf[:, :, :n], yf[:, :, :n], zz[:, :, :n], ALU.mult), reads=[yf, zz], writes=[yf])
        P.op("act", lambda e: e.activation(sqv[:, :, :n], yf[:, :, :n], AF.Square), reads=[yf], writes=[sqv])
        for g in range(2):
            ps = G.pb()
            for j in range(2):
                P.mm(ps[:, :n], G.ones[:], sqv[:, 2 * g + j, :n], j == 0, j == 1, [G.ones, sqv], [ps])
            P.op("dve", lambda e: e.tensor_scalar(rs[:, g, :n], ps[:, :n], 1.0 / 256, 1e-6, ALU.mult, ALU.add), reads=[ps], wdisj=[rs])
        P.op("act", lambda e: e.activation(rs[:, :, :n], rs[:, :, :n], AF.Sqrt), reads=[rs], writes=[rs])
        P.op("dve", lambda e: e.reciprocal(rs[:, :, :n], rs[:, :, :n]), reads=[rs], writes=[rs])
        for j in range(4):
            P.op("dve", lambda e: e.scalar_tensor_tensor(
                yo[:, j, :n], yf[:, j, :n], snw[:, j:j + 1], rs[:, j // 2, :n], ALU.mult, ALU.mult),
                reads=[yf, snw, rs], wdisj=[yo])
        P.dma("sp", G.yT[1536:2048, t0:t0 + n].rearrange("(a p) t -> p a t", p=128), yo[:, :, :n], reads=[yo], wdisj=[G.yT], owner=yo)
    P.scope_end()


def hy_mlp(G, l, hA, NP):
    P = G.P
    P.scope_begin()
    zt = P.sb("zt", [128, NP], F32)
    P.dma("sp", zt[:], G.hz[:], reads=[G.hz], writes=[zt])
    hbv = P.sb("hbv", [128, 4], F32)
    P.dma("sp", hbv[:], G.hb[l], reads=[G.hb], writes=[hbv])
    wm = P.sb("wm", [128, 3, 128], F32)
    P.dma("sp", wm[:], G.hw123[l].rearrange("a p n -> p a n"), reads=[G.hw123], writes=[wm])
    hB = P.sb("hB", [128, NP], F32)
    ttm = P.sb("ttm", [128, 512], F32)
    src = zt
    for li in range(3):
        dst = hA if li % 2 == 0 else hB
        for t0 in range(0, NP, 512):
            n = min(512, NP - t0)
            ps = G.pb()
            P.mm(ps[:, :n], wm[:, li, :], src[:, t0:t0 + n], True, True, [wm, src], [ps])
            P.op("dve", lambda e: e.tensor_scalar(dst[:, t0:t0 + n], ps[:, :n], hbv[:, li:li + 1], hbv[:, 3:4], ALU.add, ALU.mult),
                 reads=[ps, hbv], wdisj=[dst])
            P.op("act", lambda e: e.activation(dst[:, t0:t0 + n], dst[:, t0:t0 + n], AF.Sin, scale=1.0 / 9.0), reads=[dst], wdisj=[dst])
            for _ in range(2):
                P.op("dve", lambda e: e.tensor_tensor(ttm[:, :n], dst[:, t0:t0 + n], dst[:, t0:t0 + n], ALU.mult), reads=[dst], writes=[ttm])
                P.op("dve", lambda e: e.tensor_scalar(ttm[:, :n], ttm[:, :n], -4.0, 3.0, ALU.mult, ALU.add), reads=[ttm], writes=[ttm])
                P.op("dve", lambda e: e.tensor_tensor(dst[:, t0:t0 + n], dst[:, t0:t0 + n], ttm[:, :n], ALU.mult), reads=[dst, ttm], wdisj=[dst])
        src = dst
    P.scope_end()


def phase_hyena(G, l, last):
    P = G.P
    PB = G.PB
    P.scope_begin()
    NP = TT + T
    KV = 8320
    hA = P.sb("hA", [128, NP], F32)
    hy_mlp(G, l, hA, NP)
    h3 = hA
    w4 = P.sb("w4", [128, 2048], F32)
    P.dma("sp", w4[:], G.hw4p[l], reads=[G.hw4p], writes=[w4])
    sw = P.sb("sw", [128, 12, 3], F32)
    sbv = P.sb("sbv", [128, 12], F32)
    skp = P.sb("skp", [128, 2, 4], F32)
    P.dma("sp", sw[:], G.hsw[l], reads=[G.hsw], writes=[sw])
    P.dma("sp", sbv[:], G.hsb[l], reads=[G.hsb], writes=[sbv])
    P.dma("sp", skp[:], G.hskip[l], reads=[G.hskip], writes=[skp])
    revb = P.sb("revb", [128, 128], BF16)
    P.dma("pool", revb[:], G.rev[:], reads=[G.rev], writes=[revb])
    pin = P.sb("pin", [128, T + 2], F32)
    ux = [P.sb("ux%d" % i, [128, T], F32) for i in range(3)]
    kern = P.sb("kern", [128, 2, TC], F32)
    dch = [P.sb("dch%d" % i, [128, 512], F32) for i in range(2)]
    acc = P.sb("acc", [128, TC], F32)
    k0 = P.sb("k0", [128, 1], F32)
    kvs = P.sb("kvs", [128, KV], BF16)
    curb = P.sb("curb", [128, T], BF16)
    yo = curb
    Utm = P.sb("Utm", [128, 32, 128], BF16)
    Urev = P.sb("Urev", [128, 32, 128], BF16)
    band = [P.sb("band%d" % i, [128, 8192], BF16) for i in range(3)]
    Yt2 = Utm
    P.op("pool", lambda e: e.memset(kvs[:], 0.0), writes=[kvs])
    import os
    nct = int(os.environ.get("HY_CT", "4"))
    seqs = [(0, T)] if last else [(0, T), (T, TC)]
    it = 0
    dci = 0
    for ct in range(nct):
        for (q0, L) in seqs:
            for i in range(3):
                tile = 12 + 4 * i + ct
                wi = 4 * i + ct
                P.op("pool", lambda e: e.memset(pin[:, 0:L + 2], 0.0), writes=[pin])
                P.dma("sp", pin[:, 1:L + 1], G.pF[tile * 128:(tile + 1) * 128, q0:q0 + L], reads=[G.pF], writes=[pin])
                u = ux[i]
                P.op("dve", lambda e: e.tensor_scalar(u[:, :L], pin[:, 1:L + 1], sw[:, wi, 1:2], sbv[:, wi:wi + 1], ALU.mult, ALU.add),
                     reads=[pin, sw, sbv], writes=[u])
                P.op("dve", lambda e: e.scalar_tensor_tensor(u[:, :L], pin[:, 0:L], sw[:, wi, 0:1], u[:, :L], ALU.mult, ALU.add),
                     reads=[pin, sw, u], writes=[u])
                P.op("dve", lambda e: e.scalar_tensor_tensor(u[:, :L], pin[:, 2:L + 2], sw[:, wi, 2:3], u[:, :L], ALU.mult, ALU.add),
                     reads=[pin, sw, u], writes=[u])
            cur = ux[2]
            for o in range(2):
                cf = o * 1024 + ct * 128
                cb_ = o * 1024 + 512 + ct * 128
                if L == TC:
                    P.dma("sp", dch[0][:, :L], G.hdec[ct * 128:(ct + 1) * 128, q0:q0 + L], reads=[G.hdec], writes=[dch[0]])
                    for dr, c0 in ((0, cf), (1, cb_)):
                        ps = G.pb()
                        P.mm(ps[:, :L], w4[:, c0:c0 + 128], h3[:, q0:q0 + L], True, True, [w4, h3], [ps])
                        P.op("dve", lambda e: e.tensor_tensor(kern[:, dr, :L], ps[:, :L], dch[0][:, :L], ALU.mult),
                             reads=[ps, dch[0]], wdisj=[kern])
                    P.op("dve", lambda e: e.tensor_tensor(k0[:], kern[:, 0, 0:1], skp[:, o, ct:ct + 1], ALU.add), reads=[kern, skp], writes=[k0])
                    P.op("dve", lambda e: e.tensor_scalar(acc[:, :L], cur[:, :L], k0[:, 0:1], None, ALU.mult), reads=[cur, k0], writes=[acc])
                    for tau in range(1, L):
                        P.op("dve", lambda e: e.scalar_tensor_tensor(acc[:, tau:L], cur[:, 0:L - tau], kern[:, 0, tau:tau + 1], acc[:, tau:L],
                                                                    ALU.mult, ALU.add), reads=[cur, kern, acc], writes=[acc])
                        P.op("dve", lambda e: e.scalar_tensor_tensor(acc[:, 0:L - tau], cur[:, tau:L], kern[:, 1, tau:tau + 1], acc[:, 0:L - tau],
                                                                    ALU.mult, ALU.add), reads=[cur, kern, acc], writes=[acc])
                    if o == 0:
                        P.op("dve", lambda e: e.tensor_tensor(ux[2][:, :L], ux[0][:, :L], acc[:, :L], ALU.mult), reads=[ux[0], acc], writes=[ux[2]])
                    else:
                        P.op("dve", lambda e: e.tensor_tensor(yo[:, :L], ux[1][:, :L], acc[:, :L], ALU.mult), reads=[ux[1], acc], writes=[yo])
                    continue
                for a in range(0, T, 512):
                    dc = dch[dci % 2]
                    dci += 1
                    P.dma("act", dc[:], G.hdecr[ct * 128:(ct + 1) * 128, a:a + 512], reads=[G.hdecr], writes=[dc])
                    ps = G.pb()
                    P.mm(ps[:], w4[:, cb_:cb_ + 128], h3[:, TT + a:TT + a + 512], True, True, [w4, h3], [ps])
                    P.op("dve", lambda e: e.tensor_tensor(kvs[:, a:a + 512], ps[:], dc[:], ALU.mult), reads=[ps, dc], wdisj=[kvs])
                for a in range(0, T, 512):
                    dc = dch[dci % 2]
                    dci += 1
                    P.dma("act", dc[:], G.hdec[ct * 128:(ct + 1) * 128, a:a + 512], reads=[G.hdec], writes=[dc])
                    ps = G.pb()
                    P.mm(ps[:], w4[:, cf:cf + 128], h3[:, a:a + 512], True, True, [w4, h3], [ps])
                    if a == 0:
                        P.op("dve", lambda e: e.tensor_tensor(dc[:, 0:1], dc[:, 0:1], ps[:, 0:1], ALU.mult), reads=[ps, dc], writes=[dc])
                        P.op("dve", lambda e: e.tensor_tensor(k0[:], dc[:, 0:1], skp[:, o, ct:ct + 1], ALU.add), reads=[dc, skp], writes=[k0])
                        P.op("dve", lambda e: e.tensor_tensor(kvs[:, 4096:4096 + 511], ps[:, 1:512], dc[:, 1:512], ALU.mult), reads=[ps, dc], wdisj=[kvs])
                        P.op("act", lambda e: e.activation(kvs[:, 4095:4096], k0[:], AF.Copy), reads=[k0], wdisj=[kvs])
                    else:
                        P.op("dve", lambda e: e.tensor_tensor(kvs[:, 4095 + a:4095 + a + 512], ps[:], dc[:], ALU.mult), reads=[ps, dc], wdisj=[kvs])
                kvd = G.kvd[it % 2]
                it += 1
                P.dma("sp", kvd[:], kvs[:], reads=[kvs], writes=[kvd], owner=kvs)
                P.op("act", lambda e: e.activation(curb[:], cur[:], AF.Copy), reads=[cur], writes=[curb])
                for jb in range(8):
                    ps = G.pb()
                    for q in range(4):
                        J = jb * 4 + q
                        P.mm(ps[:, q * 128:(q + 1) * 128], curb[:, J * 128:(J + 1) * 128], G.identb[:], q == 0, q == 3, [curb, G.identb], [ps])
                    evac(G, jb, Utm[:, jb * 4:jb * 4 + 4, :].rearrange("p j c -> p (j c)"), ps[:], [ps], [], wdisj=[Utm])
                for jb in range(8):
                    ps = G.pb()
                    P.mm(ps[:], revb[:], Utm[:, jb * 4:jb * 4 + 4, :].rearrange("p j c -> p (j c)"), True, True, [revb, Utm], [ps])
                    evac(G, jb + 1, Urev[:, jb * 4:jb * 4 + 4, :].rearrange("p j c -> p (j c)"), ps[:], [ps], [], wdisj=[Urev])
                kt = kvd.t.tensor
                for c in range(128):
                    bd = band[c % 3]
                    P.dma("sp" if c % 2 == 0 else "act", bd[:], bass.AP(kt, kvd.t.offset + c * KV, [[1, 128], [1, 8192]]),
                          reads=[kvd], writes=[bd])
                    cc = c % 16
                    bk = PB[(c // 16) % 2]
                    for dq in [31] + [x for x in range(63) if x != 31]:
                        d = dq - 31
                        J0 = max(0, -d)
                        N = 32 - abs(d)
                        I0 = J0 + d
                        P.mm(bk[:, cc * 32 + I0:cc * 32 + I0 + N], bd[:, 128 * dq:128 * dq + 128], Urev[:, J0:J0 + N, c],
                             cc == 0 and d == 0, False, [bd, Urev], [bk])
                    if cc == 15:
                        c0 = c - 15
                        evac(G, c // 16, Yt2[:, :, c0:c0 + 16], bk[:].rearrange("p (c i) -> p i c", c=16), [bk], [], wdisj=[Yt2])
                for ib in range(8):
                    ps = G.pb()
                    for q in range(4):
                        I = ib * 4 + q
                        P.mm(ps[:, q * 128:(q + 1) * 128], Yt2[:, I, :], G.identb[:], q == 0, q == 3, [Yt2, G.identb], [ps])
                    cs_ = slice(ib * 512, (ib + 1) * 512)
                    if o == 0:
                        P.op("dve", lambda e: e.tensor_tensor(ux[2][:, cs_], ps[:], ux[0][:, cs_], ALU.mult), reads=[ps, ux[0]], wdisj=[ux[2]])
                    else:
                        P.op("dve", lambda e: e.tensor_tensor(yo[:, cs_], ps[:], ux[1][:, cs_], ALU.mult), reads=[ps, ux[1]], wdisj=[yo])
            P.dma("sp", G.yT[512 + ct * 128:512 + (ct + 1) * 128, q0:q0 + L], yo[:, :L], reads=[yo], wdisj=[G.yT], owner=yo)
    P.scope_end()


_CACHE = {}


def kernel(**inputs):
    inp = {k: np.asarray(v) for k, v in inputs.items()}
    if "nc" not in _CACHE:
        _CACHE["nc"] = build_program()
    nc = _CACHE["nc"]
    sh = _prep_shared(inp)
    in_maps = []
    for c in range(8):
        m = dict(sh)
        m.update(_prep_core(inp, c % 4))
        in_maps.append(m)
    res = run_bass_kernel_spmd(nc, in_maps, core_ids=list(range(8)))
    outs = []
    for b in range(4):
        o = np.asarray(res.results[b]["out"])
        outs.append(o.transpose(2, 0, 1).reshape(T, D))
    return np.stack(outs).astype(np.float32)
```
